# Optimizing a Trainium2 kernel written in Bass

```python
import math
import jax, jax.numpy as jnp
from jax import lax
import numpy as np

D_MODEL = 1024
BATCH = 4
SEQ = 4096
DEPTH = 1

GRID_W = 64
HEAD_DIM = 64
NA_HEADS = 8
NA_KH_MAX = 8
NA_KW = 16
NA_QBLK = 16
NA_BAND = 2 * NA_KW
NA_W = NA_HEADS * HEAD_DIM
GQA_HEADS = 8
GQA_KV_HEADS = 2
GQA_QBLK = 128
ROPE_THETA = 10000.0
GQA_Q_W = GQA_HEADS * HEAD_DIM
GQA_KV_W = GQA_KV_HEADS * HEAD_DIM
MEM_TOKENS = 256
MEM_HEADS = 4
MEM_HEAD_DIM = 128
MEM_W = MEM_HEADS * MEM_HEAD_DIM
N_BRANCH = 3
BRANCH_WIDTH = 512
GATE_W = N_BRANCH * D_MODEL
IN_W = 3 * NA_W + GQA_Q_W + 2 * GQA_KV_W + MEM_W + GATE_W
N_EXPERTS = 16
EC_CAPACITY_FACTOR = 2
D_FF_EXPERT = 1024
LN_EPS = 1e-5
RMS_EPS = 1e-6
DN_ALPHA = (2 * DEPTH) ** 0.25
DN_BETA = (8 * DEPTH) ** -0.25
NEG_INF = -1e30

kernel_name = "hybrid_natten_gqa_mem_ec_moe_deepnorm"


def _split_points():
    sizes = (NA_W, NA_W, NA_W, GQA_Q_W, GQA_KV_W, GQA_KV_W, MEM_W)
    return tuple(int(v) for v in np.cumsum(sizes))


def layer_norm(x, g, b):
    xf = x.astype(jnp.float32)
    mu = jnp.mean(xf, axis=-1, keepdims=True)
    var = jnp.mean(jnp.square(xf - mu), axis=-1, keepdims=True)
    return ((xf - mu) * lax.rsqrt(var + LN_EPS) * g.astype(jnp.float32) + b.astype(jnp.float32)).astype(x.dtype)


def rms_norm(x, g):
    xf = x.astype(jnp.float32)
    ms = jnp.mean(jnp.square(xf), axis=-1, keepdims=True)
    return (xf * lax.rsqrt(ms + RMS_EPS) * g.astype(jnp.float32)).astype(x.dtype)


def axial_rope_tables(seq_len):
    t = jnp.arange(seq_len)
    row = (t // GRID_W).astype(jnp.float32)
    col = (t % GRID_W).astype(jnp.float32)
    half = HEAD_DIM // 2
    inv = ROPE_THETA ** (-jnp.arange(0, half, 2, dtype=jnp.float32) / half)
    ang = jnp.concatenate([row[:, None] * inv, col[:, None] * inv], axis=-1)
    return jnp.cos(ang), jnp.sin(ang)


def apply_rope(x, cos, sin):
    xf = x.astype(jnp.float32)
    x1, x2 = xf[..., 0::2], xf[..., 1::2]
    c, s = cos[None, :, None, :], sin[None, :, None, :]
    out = jnp.stack([x1 * c - x2 * s, x1 * s + x2 * c], axis=-1).reshape(x.shape)
    return out.astype(x.dtype)


def neighbourhood_attention(q, k, v, rpb):
    S, H, dh = q.shape
    rows = S // GRID_W
    kh = min(NA_KH_MAX, rows)
    ncb = GRID_W // NA_QBLK
    qg = q.reshape(rows, ncb, NA_QBLK, H, dh)
    kg = k.reshape(rows, GRID_W, H, dh)
    vg = v.reshape(rows, GRID_W, H, dh)
    r = jnp.arange(rows)
    row_start = jnp.clip(r - kh // 2, 0, rows - kh)
    key_rows = row_start[:, None] + jnp.arange(kh)
    cb = jnp.arange(ncb)
    band_start = jnp.clip(cb * NA_QBLK - NA_KW // 2, 0, GRID_W - NA_BAND)
    key_cols = band_start[:, None] + jnp.arange(NA_BAND)
    ri = key_rows[:, None, :, None]
    ci = key_cols[None, :, None, :]
    k_blk = kg[ri, ci]
    v_blk = vg[ri, ci]
    qcol = cb[:, None] * NA_QBLK + jnp.arange(NA_QBLK)
    col_start = jnp.clip(qcol - NA_KW // 2, 0, GRID_W - NA_KW)
    kc = key_cols[:, None, :]
    in_win = (kc >= col_start[:, :, None]) & (kc < col_start[:, :, None] + NA_KW)
    dr = key_rows - r[:, None]
    dc = jnp.clip(kc - qcol[:, :, None] + NA_KW - 1, 0, 2 * NA_KW - 2)
    bias = rpb[:, (dr + NA_KH_MAX - 1)[:, None, None, :, None], dc[None, :, :, None, :]]
    scores = jnp.einsum('rcqhd,rckwhd->hrcqkw', qg, k_blk,
                        preferred_element_type=jnp.float32) * (1.0 / math.sqrt(dh))
    scores = jnp.where(in_win[None, None, :, :, None, :], scores + bias.astype(jnp.float32), NEG_INF)
    p = jax.nn.softmax(scores, axis=(-2, -1))
    out = jnp.einsum('hrcqkw,rckwhd->rcqhd', p.astype(v.dtype), v_blk)
    return out.reshape(S, H * dh)


def gqa_attention(q, k, v, q_gain, k_gain, cos, sin):
    q = apply_rope(rms_norm(q, q_gain), cos, sin)
    k = apply_rope(rms_norm(k, k_gain), cos, sin)
    B, S, Hq, dh = q.shape
    hkv = k.shape[2]
    grp = Hq // hkv
    nblk = S // GQA_QBLK
    qb = q.reshape(B, nblk, GQA_QBLK, hkv, grp, dh).transpose(1, 0, 3, 4, 2, 5)
    scale = 1.0 / math.sqrt(dh)

    def block(qi):
        s = jnp.einsum('bkgqd,bskd->bkgqs', qi, k, preferred_element_type=jnp.float32) * scale
        p = jax.nn.softmax(s, axis=-1)
        return jnp.einsum('bkgqs,bskd->bqkgd', p.astype(v.dtype), v)

    out = lax.map(block, qb)
    return out.transpose(1, 0, 2, 3, 4, 5).reshape(B, S, Hq * dh)


def memory_attention(q, mk, mv):
    B, S, H, dm = q.shape
    s = jnp.einsum('bshd,bmhd->bhsm', q, mk, preferred_element_type=jnp.float32) * (1.0 / math.sqrt(dm))
    p = jax.nn.softmax(s, axis=-1)
    return jnp.einsum('bhsm,bmhd->bshd', p.astype(mv.dtype), mv).reshape(B, S, H * dm)


def expert_choice_ffn(x, w_router, w_gate_up, w_down):
    B, T, D = x.shape
    cap = EC_CAPACITY_FACTOR * T // N_EXPERTS
    logits = jnp.einsum('btd,de->bte', x, w_router, preferred_element_type=jnp.float32)
    aff = jax.nn.softmax(logits, axis=-1)
    gate, idx = lax.top_k(jnp.swapaxes(aff, 1, 2), cap)
    bidx = jnp.arange(B)[:, None, None]
    xs = x[bidx, idx]
    h = jnp.einsum('becd,edf->becf', xs, w_gate_up)
    hg, hu = jnp.split(h, 2, axis=-1)
    ye = jnp.einsum('becf,efd->becd', jax.nn.silu(hg) * hu, w_down)
    ye = ye * gate[..., None].astype(ye.dtype)
    return jnp.zeros_like(x).at[bidx, idx].add(ye)


def setup_inputs(seed: int = 0) -> dict:
    key = jax.random.key(seed)
    ks = jax.random.split(key, 20)

    def nrm(k, shape, scale):
        return jax.random.normal(k, shape, jnp.float32) * scale

    return {
        "x": nrm(ks[0], (BATCH, SEQ, D_MODEL), 1.0),
        "mem": nrm(ks[1], (BATCH, MEM_TOKENS, D_MODEL), 1.0),
        "w_in": nrm(ks[2], (DEPTH, D_MODEL, IN_W), D_MODEL ** -0.5),
        "b_gate": nrm(ks[3], (DEPTH, GATE_W), 0.1),
        "na_rpb": nrm(ks[4], (DEPTH, NA_HEADS, 2 * NA_KH_MAX - 1, 2 * NA_KW - 1), 0.05),
        "gqa_q_gain": 1.0 + nrm(ks[5], (DEPTH, HEAD_DIM), 0.1),
        "gqa_k_gain": 1.0 + nrm(ks[6], (DEPTH, HEAD_DIM), 0.1),
        "w_mem_kv": nrm(ks[7], (DEPTH, D_MODEL, 2 * MEM_W), D_MODEL ** -0.5),
        "w_branch": nrm(ks[8], (DEPTH, N_BRANCH, BRANCH_WIDTH, D_MODEL), BRANCH_WIDTH ** -0.5),
        "w_out": nrm(ks[9], (DEPTH, D_MODEL, D_MODEL), D_MODEL ** -0.5 * DN_BETA),
        "ln1_g": 1.0 + nrm(ks[10], (DEPTH, D_MODEL), 0.1),
        "ln1_b": nrm(ks[11], (DEPTH, D_MODEL), 0.02),
        "w_router": nrm(ks[12], (DEPTH, D_MODEL, N_EXPERTS), D_MODEL ** -0.5),
        "w_gate_up": nrm(ks[13], (DEPTH, N_EXPERTS, D_MODEL, 2 * D_FF_EXPERT), D_MODEL ** -0.5),
        "w_down": nrm(ks[14], (DEPTH, N_EXPERTS, D_FF_EXPERT, D_MODEL), D_FF_EXPERT ** -0.5 * DN_BETA),
        "ln2_g": 1.0 + nrm(ks[15], (DEPTH, D_MODEL), 0.1),
        "ln2_b": nrm(ks[16], (DEPTH, D_MODEL), 0.02),
    }


def reference(x, mem, w_in, b_gate, na_rpb, gqa_q_gain, gqa_k_gain, w_mem_kv, w_branch, w_out,
              ln1_g, ln1_b, w_router, w_gate_up, w_down, ln2_g, ln2_b):
    B, S, D = x.shape
    M = mem.shape[1]
    cos, sin = axial_rope_tables(S)
    for l in range(DEPTH):
        z = jnp.einsum('bsd,df->bsf', x, w_in[l])
        q_na, k_na, v_na, q_g, k_g, v_g, q_m, gate_logits = jnp.split(z, _split_points(), axis=-1)
        rpb = na_rpb[l]
        y_na = lax.map(lambda qkv: neighbourhood_attention(qkv[0], qkv[1], qkv[2], rpb),
                       (q_na.reshape(B, S, NA_HEADS, HEAD_DIM),
                        k_na.reshape(B, S, NA_HEADS, HEAD_DIM),
                        v_na.reshape(B, S, NA_HEADS, HEAD_DIM)))
        y_gqa = gqa_attention(q_g.reshape(B, S, GQA_HEADS, HEAD_DIM),
                              k_g.reshape(B, S, GQA_KV_HEADS, HEAD_DIM),
                              v_g.reshape(B, S, GQA_KV_HEADS, HEAD_DIM),
                              gqa_q_gain[l], gqa_k_gain[l], cos, sin)
        mkv = jnp.einsum('bmd,df->bmf', mem, w_mem_kv[l])
        mk, mv = jnp.split(mkv, 2, axis=-1)
        y_mem = memory_attention(q_m.reshape(B, S, MEM_HEADS, MEM_HEAD_DIM),
                                 mk.reshape(B, M, MEM_HEADS, MEM_HEAD_DIM),
                                 mv.reshape(B, M, MEM_HEADS, MEM_HEAD_DIM))
        branches = jnp.stack([y_na, y_gqa, y_mem], axis=2)
        proj = jnp.einsum('bsgk,gkd->bsgd', branches, w_branch[l])
        gates = jax.nn.sigmoid(gate_logits + b_gate[l]).reshape(B, S, N_BRANCH, D)
        merged = jnp.sum(gates * proj, axis=2)
        mix = jnp.einsum('bsd,de->bse', merged, w_out[l])
        x = layer_norm(DN_ALPHA * x + mix, ln1_g[l], ln1_b[l])
        ffn = expert_choice_ffn(x, w_router[l], w_gate_up[l], w_down[l])
        x = layer_norm(DN_ALPHA * x + ffn, ln2_g[l], ln2_b[l])
    return x
```

```python
import math
from contextlib import ExitStack

import numpy as np
import concourse.bass as bass
import concourse.mybir as mybir
from concourse.bass_utils import run_bass_kernel_spmd

F32 = mybir.dt.float32
BF16 = mybir.dt.bfloat16
I32 = mybir.dt.int32
AF = mybir.ActivationFunctionType
ALU = mybir.AluOpType
AX = mybir.AxisListType

D = 1024
SEQ = 4096
NT_OWN = 16
NSLOT = 36
NTOK = NSLOT * 128
ALPHA = 2.0 ** 0.25
LN_EPS = 1e-5
RMS_EPS = 1e-6
NEG = -30000.0
NE = 16
CAP = 512


class Buf:
    __slots__ = ("w", "r")

    def __init__(self):
        self.w = None
        self.r = {}


class Sched:
    ENG = ("pe", "act", "dve", "pool", "sp")

    def __init__(self, nc):
        self.nc = nc
        self.q = {e: [] for e in self.ENG}
        self.cnt = {}
        self.waited = {}
        self.dnext = {}

    def _deps(self, reads, writes):
        deps = {}

        def add(tok):
            if tok is not None and deps.get(tok[0], 0) < tok[1]:
                deps[tok[0]] = tok[1]

        for b in reads:
            add(b.w)
        for b in writes:
            add(b.w)
            for s, v in b.r.items():
                add((s, v))
        return deps

    def _emit_waits(self, q, deps, own):
        for s, v in deps.items():
            if s == own and q == "pe":
                continue
            key = (q, s)
            if self.waited.get(key, 0) >= v:
                continue
            self.waited[key] = v
            self.q[q].append(("wait", s, v))

    def _update(self, tok, reads, writes):
        s, v = tok
        for b in reads:
            if b.r.get(s, 0) < v:
                b.r[s] = v
        for b in writes:
            b.w = tok
            b.r = {}

    def op(self, q, fn, reads=(), writes=()):
        own = "c_" + q
        self._emit_waits(q, self._deps(reads, writes), own)
        v = self.cnt.get(own, 0) + 1
        self.cnt[own] = v
        self.q[q].append(("op", fn, own, 1))
        tok = (own, v)
        self._update(tok, reads, writes)
        return tok

    NDSEM = 12

    def dma(self, q, fn, reads=(), writes=(), sem=None):
        cls = sem or q
        i = self.dnext.get(cls, 0)
        self.dnext[cls] = (i + 1) % self.NDSEM
        s = "d_%s_%d" % (cls, i)
        prev = self.cnt.get(s, 0)
        if prev:
            self._emit_waits(q, {s: prev}, None)
        self._emit_waits(q, self._deps(reads, writes), None)
        v = prev + 16
        self.cnt[s] = v
        self.q[q].append(("op", fn, s, 16))
        tok = (s, v)
        self._update(tok, reads, writes)
        return tok

    def coll(self, q, fn, reads=(), writes=()):
        s = "cc"
        self._emit_waits(q, self._deps(reads, writes), None)
        v = self.cnt.get(s, 0) + 1
        self.cnt[s] = v
        self.q[q].append(("op", fn, s, 1))
        tok = (s, v)
        self._update(tok, reads, writes)
        return tok

    def barrier(self):
        for q in self.ENG:
            self._emit_waits(q, dict(self.cnt), None)

    def emit(self, stack):
        nc = self.nc
        sems = {s: stack.enter_context(nc.semaphore(s)) for s in self.cnt}
        block = stack.enter_context(nc.Block())

        def run(q):
            def f(eng):
                for it in self.q[q]:
                    if it[0] == "wait":
                        eng.wait_ge(sems[it[1]], it[2])
                    else:
                        it[1](eng).then_inc(sems[it[2]], it[3])
            return f

        block.tensor(run("pe"))
        block.scalar(run("act"))
        block.vector(run("dve"))
        block.gpsimd(run("pool"))
        block.sync(run("sp"))


def MM(out, lhsT, rhs, start, stop):
    return lambda e: e.matmul(out, lhsT, rhs, start=start, stop=stop)


def TR(out, in_, ident):
    return lambda e: e.transpose(out, in_, ident)


def ACT(out, in_, func, **kw):
    return lambda e: e.activation(out=out, in_=in_, func=func, **kw)


def TT(out, in0, in1, op):
    return lambda e: e.tensor_tensor(out=out, in0=in0, in1=in1, op=op)


def TS(out, in0, s1, s2, op0, op1=None, **kw):
    if op1 is None:
        return lambda e: e.tensor_scalar(out, in0, s1, s2, op0, **kw)
    return lambda e: e.tensor_scalar(out, in0, s1, s2, op0, op1, **kw)


def STT(out, in0, scalar, in1, op0, op1):
    return lambda e: e.scalar_tensor_tensor(out=out, in0=in0, scalar=scalar, in1=in1, op0=op0, op1=op1)


def CP(out, in_):
    return lambda e: e.tensor_copy(out=out, in_=in_)


def RECIP(out, in_):
    return lambda e: e.reciprocal(out=out, in_=in_)


def MSET(ap, v):
    return lambda e: e.memset(ap, v)


def DMA(out, in_, **kw):
    return lambda e: e.dma_start(out=out, in_=in_, **kw)


def win_slot(w):
    if w < 2:
        return 16 + w
    if w < 18:
        return w - 2
    return w


def key_slot(kt):
    return kt if kt < 16 else kt + 4


def na_wlist(jl):
    if jl == 0:
        return list(range(0, 6))
    if jl == 15:
        return list(range(14, 20))
    return list(range(jl, jl + 5))


def na_class(jl):
    return {0: 1, 1: 2, 14: 3, 15: 4}.get(jl, 0)


def build_program(debug=False):
    nc = bass.Bass("TRN2", target_bir_lowering=False)

    def din(name, shape, dt=F32):
        return nc.dram_tensor(name, list(shape), dt, kind="ExternalInput").ap()

    def dout(name, shape, dt=F32):
        return nc.dram_tensor(name, list(shape), dt, kind="ExternalOutput").ap()

    def dint(name, shape, dt=F32):
        return nc.dram_tensor(name, list(shape), dt, kind="Internal").ap()

    xT = din("xT", [128, 8, NTOK])
    x_own = din("x_own", [2048, D])
    memT = din("memT", [128, 8, 256])
    wA = din("wA", [128, 8, 1280])
    wq = din("wq", [12, 128, 8, 128])
    wg = din("wg", [24, 128, 8, 128])
    wbr = din("wbr", [8, 128, 12, 128])
    wo = din("wo", [2, 128, 8, 512])
    wm = din("wm", [128, 8, 1024])
    bm = din("bm", [5, 128, 8 * 6 * 128])
    cosT = din("cosT", [128, NTOK])
    sinT = din("sinT", [128, NTOK])
    cmat = din("cmat", [128, 5 * 128])
    gains = din("gains", [128, 2])
    bgate = din("bgate", [128, 24])
    ln1 = din("ln1", [128, 2 * D])
    x1_f = dout("x1_f", [2048, D]) if debug else dint("x1_f", [2048, D])
    identf = din("identf", [128, 128])
    iota = din("iota", [128, 512])
    wr = din("wr", [128, 8, 16])
    slotn = din("slotn", [128, 4])
    ln2 = din("ln2", [128, 2 * D])
    wgu = din("wgu", [16, 128, 8, 2048])
    wd = din("wd", [16, 128, 8, 1024])
    out_d = dout("out", [2048, D])
    x1g = dint("x1g", [2560, 1056], BF16)
    ffn = dint("ffn", [2560, D])
    affbuf = dint("affbuf", [2048, 16])
    affall = dint("affall", [4096, 16])
    if debug:
        aff_dbg = dout("aff_dbg", [128, 16 * 16])
        thr_dbg = dout("thr_dbg", [128, 16])
        idx_dbg = dout("idx_dbg", [16, 128, 4], I32)
    if debug:
        y_dbg = dout("y_dbg", [2048, 1536], BF16)
        mg_dbg = dout("mg_dbg", [128, 8, 2048], BF16)
        qg_dbg = dout("qg_dbg", [128, 8, 512], BF16)
        qna_dbg = dout("qna_dbg", [128, 8, 512], BF16)
        kna_dbg = dout("kna_dbg", [128, 4, 2560], BF16)
        kg_dbg = dout("kg_dbg", [128, 4096], BF16)
        pt_dbg = dout("pt_dbg", [128, 768], BF16)
        vna_dbg = dout("vna_dbg", [128, 20, 520], BF16)

    S = Sched(nc)
    with ExitStack() as st:
        SBN = 94000
        sb_all = st.enter_context(nc.sbuf_tensor("sb_all", [128, SBN], BF16))
        PS = [st.enter_context(nc.psum_tensor("ps%d" % i, [128, 1024], F32)) for i in range(4)]
        bank_buf = [Buf() for _ in range(8)]

        class Alloc:
            def __init__(self):
                self.off = 0

            def get(self, shape, dt):
                n = int(np.prod(shape[1:]))
                e16 = n * 2 if dt in (F32, I32) else n
                e16 = (e16 + 1) // 2 * 2
                assert self.off + e16 <= SBN, ("SBUF overflow", self.off, e16)
                ap = sb_all[:, self.off:self.off + e16]
                self.off += e16
                if dt != BF16:
                    ap = ap.bitcast(dt)
                if len(shape) == 3:
                    ap = ap.rearrange("p (a b) -> p a b", a=shape[1])
                elif len(shape) == 4:
                    ap = ap.rearrange("p (a b c) -> p a b c", a=shape[1], b=shape[2])
                return ap

        A = Alloc()

        def bank(i):
            return PS[i // 2][:, (i % 2) * 512:(i % 2) * 512 + 512]

        rot_state = {"s": 0, "p": 0, "a": 0}

        def ps_single():
            i = rot_state["s"]
            rot_state["s"] = (i + 1) % 6
            return bank(i), [bank_buf[i]]

        def ps_pair():
            i = rot_state["p"]
            rot_state["p"] = (i + 1) % 3
            return PS[i][:, :], [bank_buf[2 * i], bank_buf[2 * i + 1]]

        def ps_acc():
            i = 6 + rot_state["a"]
            rot_state["a"] = 1 - rot_state["a"]
            return bank(i), [bank_buf[i]]

        cm = A.get([128, 640], BF16)
        ident, bones, rotm = cm[:, 0:128], cm[:, 128:256], cm[:, 256:384]
        ones_bf, tri_bf = cm[:, 384:512], cm[:, 512:640]
        idf = A.get([128, 128], F32)
        aff_own = A.get([128, 16, 16], F32)
        b_aff = Buf()
        gn = A.get([128, 2], F32)
        bg = A.get([128, 24], F32)
        eps_t = A.get([128, 2], F32)
        base_off0 = A.off
        KnaT = A.get([128, 4, 20 * 128], BF16)
        Vna = A.get([128, 20, 8 * 65], BF16)
        KgT = A.get([128, 4096], BF16)
        Vg = A.get([128, 32, 2 * 65], BF16)
        mkT = A.get([128, 4, 256], BF16)
        mv = A.get([128, 2, 4 * 129], BF16)
        yT = A.get([128, 12, 512], BF16)
        xTb = [A.get([128, 8, 512], BF16) for _ in range(2)]
        b_const, b_gn, b_bg = Buf(), Buf(), Buf()
        b_Kna = [Buf() for _ in range(20)]
        b_Vna = [Buf() for _ in range(20)]
        b_Kg = [Buf() for _ in range(32)]
        b_Vg = [Buf() for _ in range(32)]
        b_mk, b_mv, b_yT = Buf(), Buf(), Buf()
        b_xTb = [Buf(), Buf()]
        base_off = A.off

        S.dma("pool", DMA(cm, cmat), writes=[b_const])
        S.dma("sp", DMA(idf, identf), writes=[b_const])
        S.dma("sp", DMA(gn, gains), writes=[b_gn])
        S.dma("sp", DMA(bg, bgate), writes=[b_bg])
        S.op("pool", MSET(eps_t[:, 0:1], RMS_EPS), writes=[b_const])
        S.op("pool", MSET(eps_t[:, 1:2], LN_EPS), writes=[b_const])
        b_ones = Buf()
        S.op("pool", MSET(Vna, 1.0), writes=b_Vna)
        S.op("pool", MSET(Vg, 1.0), writes=b_Vg)
        S.op("pool", MSET(mv, 1.0), writes=[b_mv])

        def load_xT(blk, idx):
            buf = xTb[idx]
            S.dma("pool", DMA(buf, xT[:, :, blk * 512:(blk + 1) * 512]), writes=[b_xTb[idx]], sem="x")

        def rms_rope(zp, zp_b, gcol, tok0, tmp, dests):
            sq, rstd, qn, t1, t2, cs, sn, b_t = tmp
            S.dma("sp", DMA(cs, cosT[:, tok0:tok0 + 512]), writes=[b_t[5]])
            S.dma("sp", DMA(sn, sinT[:, tok0:tok0 + 512]), writes=[b_t[6]])
            S.op("act", ACT(sq, zp, AF.Square), reads=zp_b, writes=[b_t[0]])
            ssp, ssb = ps_single()
            S.op("pe", MM(ssp, bones, sq, True, True), reads=[b_t[0], b_const], writes=ssb)
            S.op("act", ACT(rstd, ssp, AF.Sqrt, scale=1.0 / 64.0, bias=eps_t[:, 0:1]), reads=ssb + [b_const], writes=[b_t[1]])
            S.op("dve", RECIP(rstd, rstd), reads=[b_t[1]], writes=[b_t[1]])
            S.op("dve", STT(qn, zp, gn[:, gcol:gcol + 1], rstd, ALU.mult, ALU.mult),
                 reads=zp_b + [b_t[1], b_gn], writes=[b_t[2]])
            rqp, rqb = ps_single()
            S.op("pe", MM(rqp, rotm, qn, True, True), reads=[b_t[2], b_const], writes=rqb)
            S.op("pool", TT(t1, qn, cs, ALU.mult), reads=[b_t[2], b_t[5]], writes=[b_t[3]])
            S.op("dve", TT(t2, rqp, sn, ALU.mult), reads=rqb + [b_t[6]], writes=[b_t[4]])
            for (dst, lo, hi, dbufs) in dests:
                S.op("pool", TT(dst, t1[lo:hi, :], t2[lo:hi, :], ALU.add), reads=[b_t[3], b_t[4]], writes=dbufs)

        A.off = base_off
        wA_sb = A.get([128, 8, 1280], BF16)
        wm_sb = A.get([128, 8, 1024], BF16)
        memT_sb = A.get([128, 8, 256], BF16)
        tmpA = (A.get([128, 512], BF16), A.get([128, 512], F32), A.get([128, 512], BF16),
                A.get([128, 512], F32), A.get([128, 512], F32), A.get([128, 512], F32),
                A.get([128, 512], F32), [Buf() for _ in range(7)])
        b_wA, b_wm, b_memT = Buf(), Buf(), Buf()
        S.dma("pool", DMA(wA_sb, wA), writes=[b_wA], sem="w")
        S.dma("pool", DMA(memT_sb, memT), writes=[b_memT], sem="w")
        S.dma("pool", DMA(wm_sb, wm), writes=[b_wm], sem="w")
        load_xT(0, 0)
        for blk in range(9):
            xi = blk % 2
            if blk + 1 < 9:
                load_xT(blk + 1, 1 - xi)
            xb, xbb = xTb[xi], [b_xTb[xi]]
            slots = [blk * 4 + i for i in range(4)]
            in_win = blk < 5
            in_key = blk != 4
            if in_win:
                for p in range(4):
                    pp, pb = ps_single()
                    for k in range(8):
                        S.op("pe", MM(pp, wA_sb[:, k, p * 128:(p + 1) * 128], xb[:, k, :], k == 0, k == 7),
                             reads=xbb + [b_wA], writes=pb)
                    S.op("act", ACT(KnaT[:, p, blk * 512:(blk + 1) * 512], pp, AF.Copy), reads=pb,
                         writes=[b_Kna[s] for s in slots])
                for i, s in enumerate(slots):
                    pp, pb = ps_single()
                    for k in range(8):
                        S.op("pe", MM(pp, xb[:, k, i * 128:(i + 1) * 128], wA_sb[:, k, 512:1024], k == 0, k == 7),
                             reads=xbb + [b_wA], writes=pb)
                    dst = Vna[:, s, :].rearrange("p (h e) -> p h e", h=8)[:, :, 0:64]
                    S.op("dve", CP(dst, pp.rearrange("p (h e) -> p h e", h=8)), reads=pb, writes=[b_Vna[s]])
            if in_key:
                kt0 = slots[0] if blk < 4 else slots[0] - 4
                pp, pb = ps_single()
                for k in range(8):
                    S.op("pe", MM(pp, wA_sb[:, k, 1024:1152], xb[:, k, :], k == 0, k == 7),
                         reads=xbb + [b_wA], writes=pb)
                rms_rope(pp, pb, 1, blk * 512, tmpA,
                         [(KgT[:, kt0 * 128:kt0 * 128 + 512], 0, 128, [b_Kg[kt0 + i] for i in range(4)])])
                for i in range(4):
                    pp, pb = ps_single()
                    for k in range(8):
                        S.op("pe", MM(pp[:, 0:128], xb[:, k, i * 128:(i + 1) * 128], wA_sb[:, k, 1152:1280],
                                      k == 0, k == 7), reads=xbb + [b_wA], writes=pb)
                    dst = Vg[:, kt0 + i, :].rearrange("p (h e) -> p h e", h=2)[:, :, 0:64]
                    S.op("dve", CP(dst, pp[:, 0:128].rearrange("p (h e) -> p h e", h=2)), reads=pb,
                         writes=[b_Vg[kt0 + i]])
        for h in range(4):
            pp, pb = ps_single()
            for k in range(8):
                S.op("pe", MM(pp[:, 0:256], wm_sb[:, k, h * 128:(h + 1) * 128], memT_sb[:, k, :], k == 0, k == 7),
                     reads=[b_wm, b_memT], writes=pb)
            S.op("act", ACT(mkT[:, h, :], pp[:, 0:256], AF.Copy), reads=pb, writes=[b_mk])
        for mt in range(2):
            pp, pb = ps_single()
            for k in range(8):
                S.op("pe", MM(pp, memT_sb[:, k, mt * 128:(mt + 1) * 128], wm_sb[:, k, 512:1024], k == 0, k == 7),
                     reads=[b_wm, b_memT], writes=pb)
            dst = mv[:, mt, :].rearrange("p (h e) -> p h e", h=4)[:, :, 0:128]
            S.op("dve", CP(dst, pp.rearrange("p (h e) -> p h e", h=4)), reads=pb, writes=[b_mv])
        S.barrier()

        A.off = base_off
        offB = A.off
        Qna = A.get([128, 8, 512], BF16)
        Qg = A.get([128, 8, 512], BF16)
        Qm = A.get([128, 4, 512], BF16)
        tmpB = (A.get([128, 512], BF16), A.get([128, 512], F32), A.get([128, 512], BF16),
                A.get([128, 512], F32), A.get([128, 512], F32), A.get([128, 512], F32),
                A.get([128, 512], F32), [Buf() for _ in range(7)])
        qtmp = A.get([128, 512], BF16)
        wq_sb = [A.get([128, 8, 128], BF16) for _ in range(3)]
        bm_int = A.get([128, 8, 6, 128], BF16)
        bm_brd = A.get([128, 8, 6, 128], BF16)
        PTn = [A.get([128, 768], BF16) for _ in range(2)]
        PTg = [A.get([128, 512], BF16) for _ in range(3)]
        y_sb = [A.get([128, 1536], BF16) for _ in range(2)]
        rc = [A.get([128, 8], F32) for _ in range(2)]
        endB1 = A.off
        A.off = offB
        wg_sb = [A.get([128, 8, 128], BF16) for _ in range(4)]
        wbr_sb = [A.get([128, 12, 128], BF16) for _ in range(2)]
        wo_sb = [A.get([128, 8, 512], BF16) for _ in range(2)]
        gate_sb = [A.get([128, 512], BF16) for _ in range(3)]
        mtmp = [A.get([128, 512], F32) for _ in range(3)]
        mergedT = A.get([128, 8, 512], BF16)
        tbuf = [A.get([128, D], F32) for _ in range(4)]
        xh = [A.get([128, 512], F32) for _ in range(2)]
        ln_sb = A.get([128, 2 * D], F32)
        stat = [A.get([128, 8], F32) for _ in range(2)]
        junk = A.get([128, D], BF16)
        x1T_sb = A.get([128, D], F32)
        rowt = [A.get([128, 1056], BF16) for _ in range(2)]
        wr_sb = A.get([128, 8, 16], F32)
        rt = [A.get([128, 40], F32) for _ in range(2)]
        endB2 = A.off
        A.off = max(endB1, endB2)

        b_Qna, b_Qg, b_Qm, b_qtmp = Buf(), Buf(), Buf(), Buf()
        b_wq = [Buf() for _ in range(3)]
        b_bmi, b_bmb = Buf(), Buf()
        b_PTn = [Buf(), Buf()]
        b_PTg = [Buf() for _ in range(3)]
        b_y = [Buf(), Buf()]
        b_rc = [Buf(), Buf()]
        b_wg = [Buf() for _ in range(4)]
        b_wbr = [Buf(), Buf()]
        b_wo = [Buf(), Buf()]
        b_gate = [Buf() for _ in range(3)]
        b_mtmp = [Buf() for _ in range(3)]
        b_merged = Buf()
        b_t = [Buf() for _ in range(4)]
        b_xh = [Buf(), Buf()]
        b_ln = Buf()
        b_stat = [Buf(), Buf()]
        b_junk = Buf()
        b_x1T, b_wr = Buf(), Buf()
        b_rowt = [Buf(), Buf()]
        b_rt = [Buf(), Buf()]
        b_x1g, b_affbuf, b_ffn = Buf(), Buf(), Buf()
        b_x1f = Buf()
        b_dbg = Buf()

        load_xT(0, 0)
        cnt = {"wq": 0, "wg": 0, "wbr": 0, "wo": 0, "ptn": 0, "ptg": 0, "y": 0, "xh": 0}
        for blk in range(4):
            xi = blk % 2
            xb, xbb = xTb[xi], [b_xTb[xi]]
            S.op("pool", MSET(Qna, 0.0), writes=[b_Qna])
            S.op("pool", MSET(Qg, 0.0), writes=[b_Qg])
            S.dma("pool", DMA(bm_int.rearrange("p a b c -> p (a b c)"), bm[0]), writes=[b_bmi], sem="w")
            for c in range(12):
                wi = cnt["wq"] % 3
                cnt["wq"] += 1
                S.dma("pool", DMA(wq_sb[wi], wq[c]), writes=[b_wq[wi]], sem="w")
                pp, pb = ps_single()
                for k in range(8):
                    S.op("pe", MM(pp, wq_sb[wi][:, k, :], xb[:, k, :], k == 0, k == 7),
                         reads=xbb + [b_wq[wi]], writes=pb)
                if c < 4:
                    S.op("act", ACT(Qna[0:64, 2 * c, :], pp[0:64, :], AF.Copy, scale=0.125), reads=pb, writes=[b_Qna])
                    S.op("act", ACT(Qna[64:128, 2 * c + 1, :], pp[64:128, :], AF.Copy, scale=0.125), reads=pb,
                         writes=[b_Qna])
                elif c < 8:
                    i = c - 4
                    rms_rope(pp, pb, 0, blk * 512, tmpB,
                             [(Qg[0:64, i, :], 0, 64, [b_Qg]), (Qg[64:128, 4 + i, :], 64, 128, [b_Qg])])
                else:
                    S.op("act", ACT(Qm[:, c - 8, :], pp, AF.Copy), reads=pb, writes=[b_Qm])
            if debug and blk == 0:
                S.dma("sp", DMA(qg_dbg, Qg), reads=[b_Qg], writes=[b_dbg])
                S.dma("sp", DMA(qna_dbg, Qna), reads=[b_Qna], writes=[b_dbg])
                S.dma("sp", DMA(kna_dbg, KnaT), reads=b_Kna, writes=[b_dbg])
                S.dma("sp", DMA(kg_dbg, KgT), reads=b_Kg, writes=[b_dbg])
                S.dma("sp", DMA(vna_dbg, Vna), reads=b_Vna, writes=[b_dbg])
            for qt in range(4):
                jl = blk * 4 + qt
                yi = cnt["y"] % 2
                cnt["y"] += 1
                ysb, ybb = y_sb[yi], [b_y[yi]]
                qs = slice(qt * 128, (qt + 1) * 128)
                cls = na_class(jl)
                if cls == 0:
                    bmt, bmb = bm_int, [b_bmi]
                else:
                    S.dma("pool", DMA(bm_brd.rearrange("p a b c -> p (a b c)"), bm[cls]), writes=[b_bmb], sem="w")
                    bmt, bmb = bm_brd, [b_bmb]
                wl = na_wlist(jl)
                nk = len(wl)
                accs = [ps_acc(), ps_acc()]
                for h in range(8):
                    sp, spb = ps_pair()
                    for ki, w in enumerate(wl):
                        s = win_slot(w)
                        S.op("pe", MM(sp[:, ki * 128:(ki + 1) * 128], KnaT[:, h // 2, s * 128:(s + 1) * 128],
                                      Qna[:, h, qs], ki % 4 == 0, False), reads=[b_Kna[s], b_Qna], writes=spb)
                    S.op("pe", MM(sp[:, 0:512], ident, bmt[:, h, 0:4, :], False, True), reads=bmb + [b_const],
                         writes=spb)
                    S.op("pe", MM(sp[:, 512:nk * 128], ident, bmt[:, h, 4:nk, :], False, True),
                         reads=bmb + [b_const], writes=spb)
                    pi = cnt["ptn"] % 2
                    cnt["ptn"] += 1
                    S.op("act", ACT(PTn[pi][:, 0:nk * 128], sp[:, 0:nk * 128], AF.Exp), reads=spb,
                         writes=[b_PTn[pi]])
                    if debug and jl == 5 and h == 0:
                        S.dma("sp", DMA(pt_dbg, PTn[pi]), reads=[b_PTn[pi]], writes=[b_dbg])
                    ap_, ab_ = accs[h // 4]
                    hh = h % 4
                    for ki, w in enumerate(wl):
                        s = win_slot(w)
                        S.op("pe", MM(ap_[:, hh * 65:hh * 65 + 65], PTn[pi][:, ki * 128:(ki + 1) * 128],
                                      Vna[:, s, h * 65:h * 65 + 65], ki == 0 and hh == 0, ki == nk - 1),
                             reads=[b_PTn[pi], b_Vna[s]], writes=ab_)
                for half in range(2):
                    ap_, ab_ = accs[half]
                    a3 = ap_[:, 0:260].rearrange("p (h e) -> p h e", h=4)
                    ri = half
                    S.op("dve", RECIP(rc[ri][:, 0:4], a3[:, :, 64]), reads=ab_, writes=[b_rc[ri]])
                    S.op("dve", TT(ysb[:, half * 256:(half + 1) * 256].rearrange("p (h e) -> p h e", h=4),
                                   a3[:, :, 0:64], rc[ri][:, 0:4].unsqueeze(2).to_broadcast([128, 4, 64]), ALU.mult),
                         reads=ab_ + [b_rc[ri]], writes=ybb)
                for g in range(2):
                    ap_, ab_ = ps_acc()
                    for kt in range(32):
                        sp, spb = ps_single()
                        S.op("pe", MM(sp, KgT[:, kt * 128:(kt + 1) * 128], Qg[:, 4 * g:4 * g + 4, qs], True, True),
                             reads=[b_Kg[kt], b_Qg], writes=spb)
                        pi = cnt["ptg"] % 3
                        cnt["ptg"] += 1
                        S.op("act", ACT(PTg[pi], sp, AF.Exp, scale=0.125), reads=spb, writes=[b_PTg[pi]])
                        for hh in range(4):
                            S.op("pe", MM(ap_[:, hh * 65:hh * 65 + 65], PTg[pi][:, hh * 128:(hh + 1) * 128],
                                          Vg[:, kt, g * 65:g * 65 + 65], kt == 0 and hh == 0, kt == 31),
                                 reads=[b_PTg[pi], b_Vg[kt]], writes=ab_)
                    a3 = ap_[:, 0:260].rearrange("p (h e) -> p h e", h=4)
                    S.op("dve", RECIP(rc[g][:, 4:8], a3[:, :, 64]), reads=ab_, writes=[b_rc[g]])
                    S.op("dve", TT(ysb[:, 512 + g * 256:512 + (g + 1) * 256].rearrange("p (h e) -> p h e", h=4),
                                   a3[:, :, 0:64], rc[g][:, 4:8].unsqueeze(2).to_broadcast([128, 4, 64]), ALU.mult),
                         reads=ab_ + [b_rc[g]], writes=ybb)
                for hp in range(2):
                    ap_, ab_ = ps_acc()
                    for hq in range(2):
                        h = hp * 2 + hq
                        for mt in range(2):
                            sp, spb = ps_single()
                            S.op("pe", MM(sp[:, 0:128], mkT[:, h, mt * 128:(mt + 1) * 128], Qm[:, h, qs], True, True),
                                 reads=[b_mk, b_Qm], writes=spb)
                            pi = cnt["ptg"] % 3
                            cnt["ptg"] += 1
                            S.op("act", ACT(PTg[pi][:, 0:128], sp[:, 0:128], AF.Exp, scale=1.0 / math.sqrt(128.0)),
                                 reads=spb, writes=[b_PTg[pi]])
                            S.op("pe", MM(ap_[:, hq * 129:hq * 129 + 129], PTg[pi][:, 0:128],
                                          mv[:, mt, h * 129:h * 129 + 129], mt == 0 and hq == 0, mt == 1),
                                 reads=[b_PTg[pi], b_mv], writes=ab_)
                    a3 = ap_[:, 0:258].rearrange("p (h e) -> p h e", h=2)
                    S.op("dve", RECIP(rc[hp][:, 0:2], a3[:, :, 128]), reads=ab_, writes=[b_rc[hp]])
                    S.op("dve", TT(ysb[:, 1024 + hp * 256:1024 + (hp + 1) * 256].rearrange("p (h e) -> p h e", h=2),
                                   a3[:, :, 0:128], rc[hp][:, 0:2].unsqueeze(2).to_broadcast([128, 2, 128]),
                                   ALU.mult), reads=ab_ + [b_rc[hp]], writes=ybb)
                if debug:
                    S.dma("sp", DMA(y_dbg[jl * 128:(jl + 1) * 128, :], ysb), reads=ybb, writes=[b_dbg])
                for grp in range(3):
                    tp, tpb = ps_single()
                    tpv = tp.bitcast(BF16)
                    for c in range(4):
                        cc = grp * 4 + c
                        S.op("pe", TR(tpv[:, c * 128:(c + 1) * 128], ysb[:, cc * 128:(cc + 1) * 128], ident),
                             reads=ybb + [b_const], writes=tpb)
                    S.op("dve", CP(yT[:, grp * 4:grp * 4 + 4, qs],
                                   tpv[:, 0:512].rearrange("p (c t) -> p c t", c=4)), reads=tpb, writes=[b_yT])
            S.barrier()
            if blk == 0:
                pass
            S.dma("sp", DMA(ln_sb, ln1), writes=[b_ln])
            S.dma("sp", DMA(wr_sb, wr), writes=[b_wr])
            for dc in range(8):
                bi = cnt["wbr"] % 2
                cnt["wbr"] += 1
                S.dma("pool", DMA(wbr_sb[bi], wbr[dc]), writes=[b_wbr[bi]], sem="w")
                for g in range(3):
                    wi = cnt["wg"] % 4
                    cnt["wg"] += 1
                    S.dma("pool", DMA(wg_sb[wi], wg[g * 8 + dc]), writes=[b_wg[wi]], sem="w")
                    pp, pb = ps_single()
                    for k in range(8):
                        S.op("pe", MM(pp, wg_sb[wi][:, k, :], xb[:, k, :], k == 0, k == 7),
                             reads=xbb + [b_wg[wi]], writes=pb)
                    S.op("act", ACT(gate_sb[g], pp, AF.Sigmoid, bias=bg[:, g * 8 + dc:g * 8 + dc + 1]),
                         reads=pb + [b_bg], writes=[b_gate[g]])
                for g in range(3):
                    pp, pb = ps_single()
                    for c in range(4):
                        S.op("pe", MM(pp, wbr_sb[bi][:, g * 4 + c, :], yT[:, g * 4 + c, :], c == 0, c == 3),
                             reads=[b_wbr[bi], b_yT], writes=pb)
                    S.op("dve", TT(mtmp[g], pp, gate_sb[g], ALU.mult), reads=pb + [b_gate[g]], writes=[b_mtmp[g]])
                S.op("pool", TT(mtmp[0], mtmp[0], mtmp[1], ALU.add), reads=[b_mtmp[0], b_mtmp[1]],
                     writes=[b_mtmp[0]])
                S.op("pool", TT(mergedT[:, dc, :], mtmp[0], mtmp[2], ALU.add), reads=[b_mtmp[0], b_mtmp[2]],
                     writes=[b_merged])
            if debug:
                S.dma("sp", DMA(mg_dbg[:, :, blk * 512:(blk + 1) * 512], mergedT), reads=[b_merged], writes=[b_dbg])
            if blk + 1 < 4:
                load_xT(blk + 1, 1 - xi)
            for half in range(2):
                oi = cnt["wo"] % 2
                cnt["wo"] += 1
                S.dma("pool", DMA(wo_sb[oi], wo[half]), writes=[b_wo[oi]], sem="w")
                for qt in range(4):
                    jl = blk * 4 + qt
                    hi_ = cnt["xh"] % 2
                    cnt["xh"] += 1
                    S.dma("sp", DMA(xh[hi_], x_own[jl * 128:(jl + 1) * 128, half * 512:(half + 1) * 512]),
                          writes=[b_xh[hi_]])
                    pp, pb = ps_single()
                    for k in range(8):
                        S.op("pe", MM(pp, mergedT[:, k, qt * 128:(qt + 1) * 128], wo_sb[oi][:, k, :], k == 0, k == 7),
                             reads=[b_merged, b_wo[oi]], writes=pb)
                    S.op("dve", STT(tbuf[qt][:, half * 512:(half + 1) * 512], xh[hi_], ALPHA, pp, ALU.mult, ALU.add),
                         reads=pb + [b_xh[hi_]], writes=[b_t[qt]])
            for qt in range(4):
                jl = blk * 4 + qt
                si = qt % 2
                t = tbuf[qt]
                st_ = stat[si]
                S.op("act", ACT(junk, t, AF.Copy, accum_out=st_[:, 0:1]), reads=[b_t[qt]], writes=[b_junk, b_stat[si]])
                S.op("act", ACT(junk, t, AF.Square, accum_out=st_[:, 1:2]), reads=[b_t[qt]],
                     writes=[b_junk, b_stat[si]])
                S.op("dve", TS(st_[:, 2:3], st_[:, 0:1], 1.0 / D, None, ALU.mult), reads=[b_stat[si]],
                     writes=[b_stat[si]])
                S.op("dve", TT(st_[:, 3:4], st_[:, 2:3], st_[:, 2:3], ALU.mult), reads=[b_stat[si]],
                     writes=[b_stat[si]])
                S.op("dve", STT(st_[:, 4:5], st_[:, 1:2], 1.0 / D, st_[:, 3:4], ALU.mult, ALU.subtract),
                     reads=[b_stat[si]], writes=[b_stat[si]])
                S.op("act", ACT(st_[:, 5:6], st_[:, 4:5], AF.Sqrt, bias=eps_t[:, 1:2]), reads=[b_stat[si], b_const],
                     writes=[b_stat[si]])
                S.op("dve", RECIP(st_[:, 5:6], st_[:, 5:6]), reads=[b_stat[si]], writes=[b_stat[si]])
                S.op("dve", TS(t, t, st_[:, 2:3], st_[:, 5:6], ALU.subtract, ALU.mult), reads=[b_t[qt], b_stat[si]],
                     writes=[b_t[qt]])
                S.op("pool", TT(t, t, ln_sb[:, 0:D], ALU.mult), reads=[b_t[qt], b_ln], writes=[b_t[qt]])
                S.op("pool", TT(t, t, ln_sb[:, D:2 * D], ALU.add), reads=[b_t[qt], b_ln], writes=[b_t[qt]])
                S.dma("sp", DMA(x1_f[jl * 128:(jl + 1) * 128, :], t), reads=[b_t[qt]], writes=[b_x1f])
                tp2, tpb2 = ps_pair()
                for k in range(8):
                    S.op("pe", TR(tp2[:, k * 128:(k + 1) * 128], t[:, k * 128:(k + 1) * 128], idf),
                         reads=[b_t[qt], b_const], writes=tpb2)
                S.op("act", ACT(x1T_sb, tp2, AF.Copy), reads=tpb2, writes=[b_x1T])
                lp, lpb = ps_single()
                for k in range(8):
                    S.op("pe", MM(lp[:, 0:16], x1T_sb[:, k * 128:(k + 1) * 128], wr_sb[:, k, :], k == 0, k == 7),
                         reads=[b_x1T, b_wr], writes=lpb)
                ri = qt % 2
                r_ = rt[ri]
                S.op("dve", lambda e, o=r_[:, 0:1], i=lp[:, 0:16]: e.reduce_max(out=o, in_=i, axis=AX.X),
                     reads=lpb, writes=[b_rt[ri]])
                S.op("dve", TS(r_[:, 1:2], r_[:, 0:1], -1.0, None, ALU.mult), reads=[b_rt[ri]], writes=[b_rt[ri]])
                S.op("act", ACT(r_[:, 8:24], lp[:, 0:16], AF.Exp, bias=r_[:, 1:2], accum_out=r_[:, 2:3]),
                     reads=lpb + [b_rt[ri]], writes=[b_rt[ri]])
                S.op("dve", RECIP(r_[:, 3:4], r_[:, 2:3]), reads=[b_rt[ri]], writes=[b_rt[ri]])
                S.op("dve", TS(aff_own[:, jl, :], r_[:, 8:24], r_[:, 3:4], None, ALU.mult), reads=[b_rt[ri]],
                     writes=[b_aff])
                rw = rowt[ri]
                S.op("act", ACT(rw[:, 0:1024], t, AF.Copy), reads=[b_t[qt]], writes=[b_rowt[ri]])
                S.op("dve", CP(rw[:, 1024:1056].bitcast(F32), aff_own[:, jl, :]), reads=[b_aff], writes=[b_rowt[ri]])
                S.dma("sp", DMA(x1g[jl * 128:(jl + 1) * 128, :], rw), reads=[b_rowt[ri]], writes=[b_x1g])
                S.dma("sp", DMA(affbuf[jl * 128:(jl + 1) * 128, :], aff_own[:, jl, :]), reads=[b_aff],
                      writes=[b_affbuf])
            S.barrier()

        S.barrier()
        A.off = base_off0
        affall_sb = A.get([128, 32, 16], F32)
        cmp_sb = A.get([128, 512], BF16)
        lo = A.get([128, 16], F32)
        tr_ = A.get([128, 16], F32)
        cntt = A.get([128, 16], F32)
        ge = A.get([128, 16], F32)
        M_sb = A.get([128, 16, 16], BF16)
        c_incl = A.get([128, 16, 16], F32)
        ex = A.get([128, 16, 16], F32)
        iota_sb = A.get([128, 512], F32)
        zrow = A.get([128, 1056], BF16)
        cmpS = [A.get([128, 512], BF16) for _ in range(4)]
        idx_i = [A.get([128, 4], I32) for _ in range(2)]
        idx_f = [A.get([128, 8], F32) for _ in range(2)]
        slot_sb = A.get([128, 4], F32)
        Wgu_sb = [A.get([128, 8, 2048], BF16) for _ in range(2)]
        Wd_sb = [A.get([128, 8, 1024], BF16) for _ in range(2)]
        xs = [[A.get([128, 1056], BF16) for _ in range(4)] for _ in range(2)]
        xsT = [A.get([128, 8, 512], BF16) for _ in range(2)]
        sg = [A.get([128, 512], BF16) for _ in range(2)]
        hT = [A.get([128, 8, 512], BF16) for _ in range(2)]
        yo = [A.get([128, D], F32) for _ in range(3)]
        ln2_sb = A.get([128, 2 * D], F32)
        b_affall, b_cmp, b_lo, b_tr, b_cnt, b_ge, b_M, b_cinc, b_ex, b_iota, b_z = [Buf() for _ in range(11)]
        b_cmpS = [Buf() for _ in range(4)]
        b_idx = [Buf(), Buf()]
        b_Wgu = [[Buf() for _ in range(4)] for _ in range(2)]
        b_Wd = [[Buf() for _ in range(2)] for _ in range(2)]
        b_xs = [[Buf() for _ in range(4)] for _ in range(2)]
        b_xsT = [Buf(), Buf()]
        b_sg = [Buf(), Buf()]
        b_hT = [Buf(), Buf()]
        b_yo = [Buf() for _ in range(3)]
        b_ln2 = Buf()
        b_out = Buf()

        def load_w(e):
            wi = e % 2
            for qd in range(4):
                S.dma("pool", DMA(Wgu_sb[wi][:, :, qd * 512:(qd + 1) * 512], wgu[e][:, :, qd * 512:(qd + 1) * 512]),
                      writes=[b_Wgu[wi][qd]], sem="w")
            for hf in range(2):
                S.dma("pool", DMA(Wd_sb[wi][:, :, hf * 512:(hf + 1) * 512], wd[e][:, :, hf * 512:(hf + 1) * 512]),
                      writes=[b_Wd[wi][hf]], sem="w")

        load_w(0)
        S.dma("sp", DMA(iota_sb, iota), writes=[b_iota])
        S.dma("sp", DMA(ln2_sb, ln2), writes=[b_ln2])
        S.op("dve", MSET(zrow, 0.0), writes=[b_z])
        for c in range(4):
            S.dma("sp", DMA(x1g[2048 + c * 128:2048 + (c + 1) * 128, :], zrow), reads=[b_z], writes=[b_x1g])
        S.dma("sp", DMA(slot_sb, slotn), writes=[b_iota])
        S.op("dve", MSET(yo[0], 0.0), writes=[b_yo[0]])
        for j in range(16):
            S.dma("sp", DMA(ffn[j * 128:(j + 1) * 128, :], yo[0]), reads=[b_yo[0]], writes=[b_ffn])
        S.coll("pool", lambda e: e.collective_compute("AllGather", ALU.bypass,
                                                      replica_groups=[[0, 1], [2, 3], [4, 5], [6, 7]],
                                                      ins=[affbuf.opt()], outs=[affall.opt()]),
               reads=[b_affbuf], writes=[b_affall])
        S.dma("sp", DMA(affall_sb.rearrange("p t e -> p (t e)"), affall.rearrange("(p t) e -> p (t e)", t=32)),
              reads=[b_affall], writes=[b_affall])
        S.op("dve", MSET(lo, 0.0), writes=[b_lo])
        step = 0.5
        for it in range(30):
            S.op("dve", TS(tr_, lo, step, None, ALU.add), reads=[b_lo], writes=[b_tr])
            S.op("dve", TT(cmp_sb.rearrange("p (t e) -> p t e", e=16), affall_sb,
                           tr_.unsqueeze(1).to_broadcast([128, 32, 16]), ALU.is_ge), reads=[b_affall, b_tr],
                 writes=[b_cmp])
            cp_, cpb = ps_single()
            S.op("pe", MM(cp_, ones_bf, cmp_sb, True, True), reads=[b_cmp, b_const], writes=cpb)
            S.op("dve", lambda e, o=cntt, i=cp_.rearrange("p (t e) -> p e t", e=16): e.reduce_sum(out=o, in_=i, axis=AX.X),
                 reads=cpb, writes=[b_cnt])
            S.op("dve", TS(ge, cntt, float(CAP) - 0.5, None, ALU.is_ge), reads=[b_cnt], writes=[b_ge])
            S.op("dve", STT(lo, ge, step, lo, ALU.mult, ALU.add), reads=[b_ge, b_lo], writes=[b_lo])
            step *= 0.5
        if debug:
            S.dma("sp", DMA(thr_dbg, lo), reads=[b_lo], writes=[b_dbg])
            S.dma("sp", DMA(aff_dbg, aff_own.rearrange("p t e -> p (t e)")), reads=[b_aff], writes=[b_dbg])
        S.op("dve", TT(M_sb, aff_own, lo.unsqueeze(1).to_broadcast([128, 16, 16]), ALU.is_ge), reads=[b_aff, b_lo],
             writes=[b_M])
        p1, p1b = ps_single()
        S.op("pe", MM(p1[:, 0:256], tri_bf, M_sb.rearrange("p t e -> p (t e)"), True, True), reads=[b_M, b_const],
             writes=p1b)
        p2, p2b = ps_single()
        S.op("pe", MM(p2[:, 0:256], ones_bf, M_sb.rearrange("p t e -> p (t e)"), True, True), reads=[b_M, b_const],
             writes=p2b)
        S.op("dve", MSET(ex[:, 0, :], 0.0), writes=[b_ex])
        for j in range(1, 16):
            S.op("dve", TT(ex[:, j, :], ex[:, j - 1, :], p2[:, (j - 1) * 16:j * 16], ALU.add), reads=p2b + [b_ex],
                 writes=[b_ex])
        S.op("dve", TT(c_incl.rearrange("p t e -> p (t e)"), p1[:, 0:256], ex.rearrange("p t e -> p (t e)"), ALU.add),
             reads=p1b + [b_ex], writes=[b_cinc])

        def make_idx(e):
            ii = e % 2
            ip, ipb = ps_acc()
            for j in range(16):
                ci = j % 4
                eng = "dve" if j % 2 == 0 else "pool"
                S.op(eng, TS(cmpS[ci], iota_sb, c_incl[:, j, e:e + 1], None, ALU.is_ge), reads=[b_iota, b_cinc],
                     writes=[b_cmpS[ci]])
                for c in range(4):
                    S.op("pe", MM(ip[:, c:c + 1], cmpS[ci][:, c * 128:(c + 1) * 128], ones_bf[:, 0:1],
                                  j == 0 and c == 0, j == 15), reads=[b_cmpS[ci], b_const], writes=ipb)
            S.op("dve", TS(idx_f[ii][:, 4:8], ip[:, 0:4], 2047.5, None, ALU.is_ge), reads=ipb, writes=[b_idx[ii]])
            S.op("dve", TT(idx_f[ii][:, 4:8], idx_f[ii][:, 4:8], slot_sb, ALU.mult), reads=[b_idx[ii], b_iota],
                 writes=[b_idx[ii]])
            S.op("dve", TT(idx_f[ii][:, 0:4], idx_f[ii][:, 4:8], ip[:, 0:4], ALU.add), reads=ipb + [b_idx[ii]],
                 writes=[b_idx[ii]])
            S.op("dve", CP(idx_i[ii], idx_f[ii][:, 0:4]), reads=[b_idx[ii]], writes=[b_idx[ii]])
            if debug:
                S.dma("sp", DMA(idx_dbg[e], idx_i[ii]), reads=[b_idx[ii]], writes=[b_dbg])

        make_idx(0)
        yo_n = 0
        for e in range(16):
            wi = e % 2
            if e + 1 < 16:
                load_w(e + 1)
                make_idx(e + 1)
            for c in range(4):
                S.dma("pool", lambda eng, o=xs[wi][c], ix=idx_i[wi][:, c:c + 1]: eng.indirect_dma_start(
                    out=o[:, :], out_offset=None, in_=x1g[:, :],
                    in_offset=bass.IndirectOffsetOnAxis(ap=ix, axis=0)),
                    reads=[b_idx[wi], b_x1g], writes=[b_xs[wi][c]], sem="g")
            for c in range(4):
                tp, tpb = ps_single()
                tpv = tp.bitcast(BF16)
                for k in range(8):
                    S.op("pe", TR(tpv[:, k * 128:(k + 1) * 128], xs[wi][c][:, k * 128:(k + 1) * 128], ident),
                         reads=[b_xs[wi][c], b_const], writes=tpb)
                eng = "dve" if c % 2 == 0 else "act"
                if eng == "dve":
                    S.op("dve", CP(xsT[wi][:, :, c * 128:(c + 1) * 128], tpv.rearrange("p (k t) -> p k t", k=8)),
                         reads=tpb, writes=[b_xsT[wi]])
                else:
                    S.op("act", ACT(xsT[wi][:, :, c * 128:(c + 1) * 128], tpv.rearrange("p (k t) -> p k t", k=8),
                                    AF.Copy), reads=tpb, writes=[b_xsT[wi]])
            for fc in range(8):
                gp, gpb = ps_single()
                for k in range(8):
                    S.op("pe", MM(gp, Wgu_sb[wi][:, k, fc * 128:(fc + 1) * 128], xsT[wi][:, k, :], k == 0, k == 7),
                         reads=[b_Wgu[wi][fc // 4], b_xsT[wi]], writes=gpb)
                up, upb = ps_single()
                for k in range(8):
                    S.op("pe", MM(up, Wgu_sb[wi][:, k, 1024 + fc * 128:1024 + (fc + 1) * 128], xsT[wi][:, k, :],
                                  k == 0, k == 7), reads=[b_Wgu[wi][2 + fc // 4], b_xsT[wi]], writes=upb)
                si = fc % 2
                S.op("act", ACT(sg[si], gp, AF.Silu), reads=gpb, writes=[b_sg[si]])
                S.op("dve", TT(hT[wi][:, fc, :], up, sg[si], ALU.mult), reads=upb + [b_sg[si]], writes=[b_hT[wi]])
            for c in range(4):
                yi = yo_n % 3
                yo_n += 1
                gcol = xs[wi][c][:, 1024 + 2 * e:1024 + 2 * e + 2].bitcast(F32)
                for hf in range(2):
                    dp, dpb = ps_single()
                    for fc in range(8):
                        S.op("pe", MM(dp, hT[wi][:, fc, c * 128:(c + 1) * 128], Wd_sb[wi][:, fc, hf * 512:(hf + 1) * 512],
                                      fc == 0, fc == 7), reads=[b_hT[wi], b_Wd[wi][hf]], writes=dpb)
                    if hf == 0:
                        S.op("act", ACT(yo[yi][:, 0:512], dp, AF.Copy, scale=gcol), reads=dpb + [b_xs[wi][c]],
                             writes=[b_yo[yi]])
                    else:
                        S.op("dve", TS(yo[yi][:, 512:1024], dp, gcol, None, ALU.mult), reads=dpb + [b_xs[wi][c]],
                             writes=[b_yo[yi]])
                S.dma("pool", lambda eng, i_=yo[yi], ix=idx_i[wi][:, c:c + 1]: eng.indirect_dma_start(
                    out=ffn[:, :], out_offset=bass.IndirectOffsetOnAxis(ap=ix, axis=0), in_=i_[:, :], in_offset=None,
                    compute_op=ALU.add),
                    reads=[b_idx[wi], b_yo[yi], b_ffn], writes=[b_ffn], sem="g")
        S.barrier()
        A.off = base_off0
        fbuf = [A.get([128, D], F32) for _ in range(2)]
        xbuf = [A.get([128, D], F32) for _ in range(2)]
        fst = [A.get([128, 8], F32) for _ in range(2)]
        fjunk = A.get([128, D], BF16)
        ln2b = A.get([128, 2 * D], F32)
        b_fb, b_xb, b_fst = [Buf(), Buf()], [Buf(), Buf()], [Buf(), Buf()]
        b_fj, b_l2 = Buf(), Buf()
        S.dma("sp", DMA(ln2b, ln2), writes=[b_l2])
        for j in range(16):
            i = j % 2
            S.dma("sp", DMA(fbuf[i], ffn[j * 128:(j + 1) * 128, :]), reads=[b_ffn], writes=[b_fb[i]])
            S.dma("sp", DMA(xbuf[i], x1_f[j * 128:(j + 1) * 128, :]), reads=[b_x1f], writes=[b_xb[i]])
            t = fbuf[i]
            st_ = fst[i]
            S.op("dve", STT(t, xbuf[i], ALPHA, t, ALU.mult, ALU.add), reads=[b_xb[i], b_fb[i]], writes=[b_fb[i]])
            S.op("act", ACT(fjunk, t, AF.Copy, accum_out=st_[:, 0:1]), reads=[b_fb[i]], writes=[b_fj, b_fst[i]])
            S.op("act", ACT(fjunk, t, AF.Square, accum_out=st_[:, 1:2]), reads=[b_fb[i]], writes=[b_fj, b_fst[i]])
            S.op("dve", TS(st_[:, 2:3], st_[:, 0:1], 1.0 / D, None, ALU.mult), reads=[b_fst[i]], writes=[b_fst[i]])
            S.op("dve", TT(st_[:, 3:4], st_[:, 2:3], st_[:, 2:3], ALU.mult), reads=[b_fst[i]], writes=[b_fst[i]])
            S.op("dve", STT(st_[:, 4:5], st_[:, 1:2], 1.0 / D, st_[:, 3:4], ALU.mult, ALU.subtract),
                 reads=[b_fst[i]], writes=[b_fst[i]])
            S.op("act", ACT(st_[:, 5:6], st_[:, 4:5], AF.Sqrt, bias=eps_t[:, 1:2]), reads=[b_fst[i], b_const],
                 writes=[b_fst[i]])
            S.op("dve", RECIP(st_[:, 5:6], st_[:, 5:6]), reads=[b_fst[i]], writes=[b_fst[i]])
            S.op("dve", TS(t, t, st_[:, 2:3], st_[:, 5:6], ALU.subtract, ALU.mult), reads=[b_fb[i], b_fst[i]],
                 writes=[b_fb[i]])
            S.op("pool", TT(t, t, ln2b[:, 0:D], ALU.mult), reads=[b_fb[i], b_l2], writes=[b_fb[i]])
            S.op("pool", TT(t, t, ln2b[:, D:2 * D], ALU.add), reads=[b_fb[i], b_l2], writes=[b_fb[i]])
            S.dma("sp", DMA(out_d[j * 128:(j + 1) * 128, :], t), reads=[b_fb[i]], writes=[b_out])
        S.barrier()
        S.emit(st)
    return nc


def _rope_tables():
    t = np.arange(SEQ)
    row = (t // 64).astype(np.float32)
    col = (t % 64).astype(np.float32)
    inv = (np.float32(10000.0) ** (-np.arange(0, 32, 2, dtype=np.float32) / np.float32(32))).astype(np.float32)
    ang = np.concatenate([row[:, None] * inv, col[:, None] * inv], axis=-1).astype(np.float32)
    return np.cos(ang).astype(np.float32), np.sin(ang).astype(np.float32)


def _bm_tile(rpb, jg, ktg):
    if ktg < 0 or ktg > 31:
        return np.full((8, 128, 128), NEG, np.float32)
    i = np.arange(128)
    kr, kc = (i // 64)[:, None], (i % 64)[:, None]
    qr, qc = (i // 64)[None, :], (i % 64)[None, :]
    key_row = 2 * ktg + kr
    r = 2 * jg + qr
    rs = np.clip(r - 4, 0, 56)
    cs = np.clip(qc - 8, 0, 48)
    valid = (key_row >= rs) & (key_row < rs + 8) & (kc >= cs) & (kc < cs + 16)
    dr = np.clip(key_row - r + 7, 0, 14)
    dc = np.clip(kc - qc + 15, 0, 30)
    dr, dc = np.broadcast_arrays(dr, dc)
    vals = rpb[:, dr, dc]
    return np.where(valid[None], vals, np.float32(NEG)).astype(np.float32)


def _core_inputs(b, h, inp, shared):
    x = inp["x"][b]
    own = [16 * h + i for i in range(16)]
    order = own + [16 * h - 2, 16 * h - 1, 16 * h + 16, 16 * h + 17] + [16 * (1 - h) + i for i in range(16)]
    xTc = np.zeros((D, NTOK), np.float32)
    cosT = np.zeros((128, NTOK), np.float32)
    sinT = np.zeros((128, NTOK), np.float32)
    cos, sin = shared["rope"]
    pidx = (np.arange(128) % 64) // 2
    for s, t in enumerate(order):
        if 0 <= t < 32:
            xTc[:, s * 128:(s + 1) * 128] = x[t * 128:(t + 1) * 128, :].T
            cosT[:, s * 128:(s + 1) * 128] = cos[t * 128:(t + 1) * 128, :][:, pidx].T
            sinT[:, s * 128:(s + 1) * 128] = sin[t * 128:(t + 1) * 128, :][:, pidx].T
    rpb = inp["na_rpb"][0]
    bmc = np.full((5, 128, 8, 6, 128), NEG, np.float32)
    for cls, jl in enumerate([2, 0, 1, 14, 15]):
        jg = 16 * h + jl
        for ki, w in enumerate(na_wlist(jl)):
            ktg = 16 * h - 2 + w
            bmc[cls, :, :, ki, :] = _bm_tile(rpb, jg, ktg).transpose(1, 0, 2)
    d = dict(shared["common"])
    d["xT"] = np.ascontiguousarray(xTc.reshape(8, 128, NTOK).transpose(1, 0, 2))
    d["x_own"] = np.ascontiguousarray(x[h * 2048:(h + 1) * 2048])
    d["memT"] = np.ascontiguousarray(inp["mem"][b].T.reshape(8, 128, 256).transpose(1, 0, 2))
    d["bm"] = np.ascontiguousarray(bmc.reshape(5, 128, 8 * 6 * 128))
    d["cosT"] = cosT
    d["sinT"] = sinT
    return d


def _pk(w):
    return np.ascontiguousarray(w.reshape(8, 128, -1).transpose(1, 0, 2))


def _shared(inp):
    w_in = inp["w_in"][0]
    q_na, k_na, v_na = w_in[:, 0:512], w_in[:, 512:1024], w_in[:, 1024:1536]
    q_g, k_g, v_g = w_in[:, 1536:2048], w_in[:, 2048:2176], w_in[:, 2176:2304]
    q_m, gl = w_in[:, 2304:2816], w_in[:, 2816:5888]
    wA = _pk(np.concatenate([k_na, v_na, k_g, v_g], axis=1))
    chunks = [q_na[:, c * 128:(c + 1) * 128] for c in range(4)]
    for i in range(4):
        chunks.append(np.concatenate([q_g[:, i * 64:(i + 1) * 64], q_g[:, (4 + i) * 64:(5 + i) * 64]], axis=1))
    chunks += [q_m[:, c * 128:(c + 1) * 128] for c in range(4)]
    wq = np.stack([_pk(c) for c in chunks])
    wg = np.stack([_pk(gl[:, c * 128:(c + 1) * 128]) for c in range(24)])
    wb = inp["w_branch"][0]
    wbr = np.zeros((8, 128, 12, 128), np.float32)
    for dc in range(8):
        for g in range(3):
            for c in range(4):
                wbr[dc, :, g * 4 + c, :] = wb[g, c * 128:(c + 1) * 128, dc * 128:(dc + 1) * 128]
    w_out = inp["w_out"][0]
    wo = np.stack([_pk(w_out[:, hf * 512:(hf + 1) * 512]) for hf in range(2)])
    wm = _pk(inp["w_mem_kv"][0])
    ident = np.eye(128, dtype=np.float32)
    bones = np.kron(np.eye(2, dtype=np.float32), np.ones((64, 64), np.float32))
    rot = np.zeros((128, 128), np.float32)
    for i in range(64):
        rot[2 * i + 1, 2 * i] = -1.0
        rot[2 * i, 2 * i + 1] = 1.0
    gains = np.stack([np.tile(inp["gqa_q_gain"][0], 2), np.tile(inp["gqa_k_gain"][0], 2)], axis=1)
    bgate = np.ascontiguousarray(inp["b_gate"][0].reshape(24, 128).T)
    ln1 = np.concatenate([np.broadcast_to(inp["ln1_g"][0], (128, D)), np.broadcast_to(inp["ln1_b"][0], (128, D))],
                         axis=1)
    ones = np.ones((128, 128), np.float32)
    tri = np.triu(np.ones((128, 128), np.float32))
    wgu = np.ascontiguousarray(inp["w_gate_up"][0].reshape(16, 8, 128, 2048).transpose(0, 2, 1, 3))
    wd = np.ascontiguousarray(inp["w_down"][0].reshape(16, 8, 128, 1024).transpose(0, 2, 1, 3))
    ln2 = np.concatenate([np.broadcast_to(inp["ln2_g"][0], (128, D)), np.broadcast_to(inp["ln2_b"][0], (128, D))],
                         axis=1)
    common = {
        "identf": ident, "iota": np.ascontiguousarray(np.broadcast_to(np.arange(512, dtype=np.float32), (128, 512))),
        "wr": _pk(inp["w_router"][0]),
        "slotn": np.ascontiguousarray((np.arange(128, dtype=np.float32)[:, None]
                                       + 128.0 * np.arange(4, dtype=np.float32)[None, :])), "ln2": np.ascontiguousarray(ln2.astype(np.float32)),
        "wgu": wgu, "wd": wd,
        "wA": wA, "wq": wq, "wg": wg, "wbr": wbr, "wo": wo, "wm": wm,
        "cmat": np.ascontiguousarray(np.concatenate([ident, bones, rot, ones, tri], axis=1)),
        "gains": np.ascontiguousarray(gains.astype(np.float32)),
        "bgate": bgate.astype(np.float32),
        "ln1": np.ascontiguousarray(ln1.astype(np.float32)),
    }
    return {"common": common, "rope": _rope_tables()}


def run(inp, debug=False):
    inp = {k: np.asarray(v) for k, v in inp.items()}
    shared = _shared(inp)
    in_maps = [_core_inputs(c // 2, c % 2, inp, shared) for c in range(8)]
    nc = build_program(debug=debug)
    res = run_bass_kernel_spmd(nc, in_maps, core_ids=list(range(8)))
    return res


def kernel(**inputs):
    res = run(inputs)
    out = np.zeros((4, SEQ, D), np.float32)
    for c in range(8):
        b, h = c // 2, c % 2
        out[b, h * 2048:(h + 1) * 2048] = res.results[c]["out"]
    return out
```

```python
import math
from contextlib import ExitStack

import numpy as np
import concourse.bass as bass
import concourse.mybir as mybir
from concourse.bass_utils import run_bass_kernel_spmd

F32 = mybir.dt.float32
BF16 = mybir.dt.bfloat16
I32 = mybir.dt.int32
AF = mybir.ActivationFunctionType
ALU = mybir.AluOpType
AX = mybir.AxisListType

D = 1024
SEQ = 4096
NT_OWN = 16
NSLOT = 36
NTOK = NSLOT * 128
ALPHA = 2.0 ** 0.25
LN_EPS = 1e-5
RMS_EPS = 1e-6
NEG = -30000.0
NE = 16
CAP = 512


class Buf:
    __slots__ = ("w", "r")

    def __init__(self):
        self.w = None
        self.r = {}


class Sched:
    ENG = ("pe", "act", "dve", "pool", "sp")

    def __init__(self, nc):
        self.nc = nc
        self.q = {e: [] for e in self.ENG}
        self.cnt = {}
        self.waited = {}
        self.dnext = {}

    def _deps(self, reads, writes):
        deps = {}

        def add(tok):
            if tok is not None and deps.get(tok[0], 0) < tok[1]:
                deps[tok[0]] = tok[1]

        for b in reads:
            add(b.w)
        for b in writes:
            add(b.w)
            for s, v in b.r.items():
                add((s, v))
        return deps

    def _emit_waits(self, q, deps, own):
        for s, v in deps.items():
            if s == own and q == "pe":
                continue
            key = (q, s)
            if self.waited.get(key, 0) >= v:
                continue
            self.waited[key] = v
            self.q[q].append(("wait", s, v))

    def _update(self, tok, reads, writes):
        s, v = tok
        for b in reads:
            if b.r.get(s, 0) < v:
                b.r[s] = v
        for b in writes:
            b.w = tok
            b.r = {}

    def op(self, q, fn, reads=(), writes=()):
        own = "c_" + q
        self._emit_waits(q, self._deps(reads, writes), own)
        v = self.cnt.get(own, 0) + 1
        self.cnt[own] = v
        self.q[q].append(("op", fn, own, 1))
        tok = (own, v)
        self._update(tok, reads, writes)
        return tok

    NDSEM = 12

    def dma(self, q, fn, reads=(), writes=(), sem=None):
        cls = sem or q
        i = self.dnext.get(cls, 0)
        self.dnext[cls] = (i + 1) % self.NDSEM
        s = "d_%s_%d" % (cls, i)
        prev = self.cnt.get(s, 0)
        if prev:
            self._emit_waits(q, {s: prev}, None)
        self._emit_waits(q, self._deps(reads, writes), None)
        v = prev + 16
        self.cnt[s] = v
        self.q[q].append(("op", fn, s, 16))
        tok = (s, v)
        self._update(tok, reads, writes)
        return tok

    def coll(self, q, fn, reads=(), writes=()):
        s = "cc"
        self._emit_waits(q, self._deps(reads, writes), None)
        v = self.cnt.get(s, 0) + 1
        self.cnt[s] = v
        self.q[q].append(("op", fn, s, 1))
        tok = (s, v)
        self._update(tok, reads, writes)
        return tok

    def barrier(self):
        for q in self.ENG:
            self._emit_waits(q, dict(self.cnt), None)

    def emit(self, stack):
        nc = self.nc
        sems = {s: stack.enter_context(nc.semaphore(s)) for s in self.cnt}
        block = stack.enter_context(nc.Block())

        def run(q):
            def f(eng):
                for it in self.q[q]:
                    if it[0] == "wait":
                        eng.wait_ge(sems[it[1]], it[2])
                    else:
                        it[1](eng).then_inc(sems[it[2]], it[3])
            return f

        block.tensor(run("pe"))
        block.scalar(run("act"))
        block.vector(run("dve"))
        block.gpsimd(run("pool"))
        block.sync(run("sp"))


def MM(out, lhsT, rhs, start, stop):
    return lambda e: e.matmul(out, lhsT, rhs, start=start, stop=stop)


def TR(out, in_, ident):
    return lambda e: e.transpose(out, in_, ident)


def ACT(out, in_, func, **kw):
    return lambda e: e.activation(out=out, in_=in_, func=func, **kw)


def TT(out, in0, in1, op):
    return lambda e: e.tensor_tensor(out=out, in0=in0, in1=in1, op=op)


def TS(out, in0, s1, s2, op0, op1=None, **kw):
    if op1 is None:
        return lambda e: e.tensor_scalar(out, in0, s1, s2, op0, **kw)
    return lambda e: e.tensor_scalar(out, in0, s1, s2, op0, op1, **kw)


def STT(out, in0, scalar, in1, op0, op1):
    return lambda e: e.scalar_tensor_tensor(out=out, in0=in0, scalar=scalar, in1=in1, op0=op0, op1=op1)


def CP(out, in_):
    return lambda e: e.tensor_copy(out=out, in_=in_)


def RECIP(out, in_):
    return lambda e: e.reciprocal(out=out, in_=in_)


def MSET(ap, v):
    return lambda e: e.memset(ap, v)


def DMA(out, in_, **kw):
    return lambda e: e.dma_start(out=out, in_=in_, **kw)


def win_slot(w):
    if w < 2:
        return 16 + w
    if w < 18:
        return w - 2
    return w


def key_slot(kt):
    return kt if kt < 16 else kt + 4


def na_wlist(jl):
    if jl == 0:
        return list(range(0, 6))
    if jl == 15:
        return list(range(14, 20))
    return list(range(jl, jl + 5))


def na_class(jl):
    return {0: 1, 1: 2, 14: 3, 15: 4}.get(jl, 0)


def build_program(debug=False):
    nc = bass.Bass("TRN2", target_bir_lowering=False)

    def din(name, shape, dt=F32):
        return nc.dram_tensor(name, list(shape), dt, kind="ExternalInput").ap()

    def dout(name, shape, dt=F32):
        return nc.dram_tensor(name, list(shape), dt, kind="ExternalOutput").ap()

    def dint(name, shape, dt=F32):
        return nc.dram_tensor(name, list(shape), dt, kind="Internal").ap()

    xT = din("xT", [128, 8, NTOK])
    x_own = din("x_own", [2048, D])
    memT = din("memT", [128, 8, 256])
    wA = din("wA", [128, 8, 1280])
    wq = din("wq", [12, 128, 8, 128])
    wg = din("wg", [24, 128, 8, 128])
    wbr = din("wbr", [8, 128, 12, 128])
    wo = din("wo", [2, 128, 8, 512])
    wm = din("wm", [128, 8, 1024])
    bm = din("bm", [5, 128, 8 * 6 * 128])
    cosT = din("cosT", [128, NTOK])
    sinT = din("sinT", [128, NTOK])
    cmat = din("cmat", [128, 5 * 128])
    gains = din("gains", [128, 2])
    bgate = din("bgate", [128, 24])
    ln1 = din("ln1", [128, 2 * D])
    x1_f = dout("x1_f", [2048, D]) if debug else dint("x1_f", [2048, D])
    identf = din("identf", [128, 128])
    iota = din("iota", [128, 512])
    wr = din("wr", [128, 8, 16])
    slotn = din("slotn", [128, 4])
    ln2 = din("ln2", [128, 2 * D])
    wgu = din("wgu", [16, 128, 8, 2048])
    wd = din("wd", [16, 128, 8, 1024])
    out_d = dout("out", [2048, D])
    x1g = dint("x1g", [2560, 1056], BF16)
    ffn = dint("ffn", [2560, D])
    affbuf = dint("affbuf", [2048, 16])
    affall = dint("affall", [4096, 16])
    if debug:
        aff_dbg = dout("aff_dbg", [128, 16 * 16])
        thr_dbg = dout("thr_dbg", [128, 16])
        idx_dbg = dout("idx_dbg", [16, 128, 4], I32)
    if debug:
        y_dbg = dout("y_dbg", [2048, 1536], BF16)
        mg_dbg = dout("mg_dbg", [128, 8, 2048], BF16)
        qg_dbg = dout("qg_dbg", [128, 8, 512], BF16)
        qna_dbg = dout("qna_dbg", [128, 8, 512], BF16)
        kna_dbg = dout("kna_dbg", [128, 4, 2560], BF16)
        kg_dbg = dout("kg_dbg", [128, 4096], BF16)
        pt_dbg = dout("pt_dbg", [128, 768], BF16)
        vna_dbg = dout("vna_dbg", [128, 20, 520], BF16)

    S = Sched(nc)
    with ExitStack() as st:
        SBN = 94000
        sb_all = st.enter_context(nc.sbuf_tensor("sb_all", [128, SBN], BF16))
        PS = [st.enter_context(nc.psum_tensor("ps%d" % i, [128, 1024], F32)) for i in range(4)]
        bank_buf = [Buf() for _ in range(8)]

        class Alloc:
            def __init__(self):
                self.off = 0

            def get(self, shape, dt):
                n = int(np.prod(shape[1:]))
                e16 = n * 2 if dt in (F32, I32) else n
                e16 = (e16 + 1) // 2 * 2
                assert self.off + e16 <= SBN, ("SBUF overflow", self.off, e16)
                ap = sb_all[:, self.off:self.off + e16]
                self.off += e16
                if dt != BF16:
                    ap = ap.bitcast(dt)
                if len(shape) == 3:
                    ap = ap.rearrange("p (a b) -> p a b", a=shape[1])
                elif len(shape) == 4:
                    ap = ap.rearrange("p (a b c) -> p a b c", a=shape[1], b=shape[2])
                return ap

        A = Alloc()

        def bank(i):
            return PS[i // 2][:, (i % 2) * 512:(i % 2) * 512 + 512]

        rot_state = {"s": 0, "p": 0, "a": 0}

        def ps_single():
            i = rot_state["s"]
            rot_state["s"] = (i + 1) % 6
            return bank(i), [bank_buf[i]]

        def ps_pair():
            i = rot_state["p"]
            rot_state["p"] = (i + 1) % 3
            return PS[i][:, :], [bank_buf[2 * i], bank_buf[2 * i + 1]]

        def ps_acc():
            i = 6 + rot_state["a"]
            rot_state["a"] = 1 - rot_state["a"]
            return bank(i), [bank_buf[i]]

        cm = A.get([128, 640], BF16)
        ident, bones, rotm = cm[:, 0:128], cm[:, 128:256], cm[:, 256:384]
        ones_bf, tri_bf = cm[:, 384:512], cm[:, 512:640]
        idf = A.get([128, 128], F32)
        aff_own = A.get([128, 16, 16], F32)
        b_aff = Buf()
        gn = A.get([128, 2], F32)
        bg = A.get([128, 24], F32)
        eps_t = A.get([128, 2], F32)
        base_off0 = A.off
        KnaT = A.get([128, 4, 20 * 128], BF16)
        Vna = A.get([128, 20, 8 * 65], BF16)
        KgT = A.get([128, 4096], BF16)
        Vg = A.get([128, 32, 2 * 65], BF16)
        mkT = A.get([128, 4, 256], BF16)
        mv = A.get([128, 2, 4 * 129], BF16)
        yT = A.get([128, 12, 512], BF16)
        xTb = [A.get([128, 8, 512], BF16) for _ in range(2)]
        b_const, b_gn, b_bg = Buf(), Buf(), Buf()
        b_Kna = [Buf() for _ in range(20)]
        b_Vna = [Buf() for _ in range(20)]
        b_Kg = [Buf() for _ in range(32)]
        b_Vg = [Buf() for _ in range(32)]
        b_mk, b_mv, b_yT = Buf(), Buf(), Buf()
        b_xTb = [Buf(), Buf()]
        base_off = A.off

        S.dma("pool", DMA(cm, cmat), writes=[b_const])
        S.dma("sp", DMA(idf, identf), writes=[b_const])
        S.dma("sp", DMA(gn, gains), writes=[b_gn])
        S.dma("sp", DMA(bg, bgate), writes=[b_bg])
        S.op("pool", MSET(eps_t[:, 0:1], RMS_EPS), writes=[b_const])
        S.op("pool", MSET(eps_t[:, 1:2], LN_EPS), writes=[b_const])
        b_ones = Buf()
        S.op("pool", MSET(Vna, 1.0), writes=b_Vna)
        S.op("pool", MSET(Vg, 1.0), writes=b_Vg)
        S.op("pool", MSET(mv, 1.0), writes=[b_mv])

        def load_xT(blk, idx):
            buf = xTb[idx]
            S.dma("pool", DMA(buf, xT[:, :, blk * 512:(blk + 1) * 512]), writes=[b_xTb[idx]], sem="x")

        def rms_rope(zp, zp_b, gcol, tok0, tmp, dests):
            sq, rstd, qn, t1, t2, cs, sn, b_t = tmp
            S.dma("sp", DMA(cs, cosT[:, tok0:tok0 + 512]), writes=[b_t[5]])
            S.dma("sp", DMA(sn, sinT[:, tok0:tok0 + 512]), writes=[b_t[6]])
            S.op("act", ACT(sq, zp, AF.Square), reads=zp_b, writes=[b_t[0]])
            ssp, ssb = ps_single()
            S.op("pe", MM(ssp, bones, sq, True, True), reads=[b_t[0], b_const], writes=ssb)
            S.op("act", ACT(rstd, ssp, AF.Sqrt, scale=1.0 / 64.0, bias=eps_t[:, 0:1]), reads=ssb + [b_const], writes=[b_t[1]])
            S.op("dve", RECIP(rstd, rstd), reads=[b_t[1]], writes=[b_t[1]])
            S.op("dve", STT(qn, zp, gn[:, gcol:gcol + 1], rstd, ALU.mult, ALU.mult),
                 reads=zp_b + [b_t[1], b_gn], writes=[b_t[2]])
            rqp, rqb = ps_single()
            S.op("pe", MM(rqp, rotm, qn, True, True), reads=[b_t[2], b_const], writes=rqb)
            S.op("dve", TT(t1, qn, cs, ALU.mult), reads=[b_t[2], b_t[5]], writes=[b_t[3]])
            S.op("dve", TT(t2, rqp, sn, ALU.mult), reads=rqb + [b_t[6]], writes=[b_t[4]])
            for (dst, lo, hi, dbufs) in dests:
                S.op("dve", TT(dst, t1[lo:hi, :], t2[lo:hi, :], ALU.add), reads=[b_t[3], b_t[4]], writes=dbufs)

        A.off = base_off
        wA_sb = A.get([128, 8, 1280], BF16)
        wm_sb = A.get([128, 8, 1024], BF16)
        memT_sb = A.get([128, 8, 256], BF16)
        tmpA = (A.get([128, 512], BF16), A.get([128, 512], F32), A.get([128, 512], BF16),
                A.get([128, 512], F32), A.get([128, 512], F32), A.get([128, 512], F32),
                A.get([128, 512], F32), [Buf() for _ in range(7)])
        b_wA, b_wm, b_memT = Buf(), Buf(), Buf()
        S.dma("pool", DMA(wA_sb, wA), writes=[b_wA], sem="w")
        S.dma("pool", DMA(memT_sb, memT), writes=[b_memT], sem="w")
        S.dma("pool", DMA(wm_sb, wm), writes=[b_wm], sem="w")
        load_xT(0, 0)
        for blk in range(9):
            xi = blk % 2
            if blk + 1 < 9:
                load_xT(blk + 1, 1 - xi)
            xb, xbb = xTb[xi], [b_xTb[xi]]
            slots = [blk * 4 + i for i in range(4)]
            in_win = blk < 5
            in_key = blk != 4
            if in_win:
                for p in range(4):
                    pp, pb = ps_single()
                    for k in range(8):
                        S.op("pe", MM(pp, wA_sb[:, k, p * 128:(p + 1) * 128], xb[:, k, :], k == 0, k == 7),
                             reads=xbb + [b_wA], writes=pb)
                    S.op("act", ACT(KnaT[:, p, blk * 512:(blk + 1) * 512], pp, AF.Copy), reads=pb,
                         writes=[b_Kna[s] for s in slots])
                for i, s in enumerate(slots):
                    pp, pb = ps_single()
                    for k in range(8):
                        S.op("pe", MM(pp, xb[:, k, i * 128:(i + 1) * 128], wA_sb[:, k, 512:1024], k == 0, k == 7),
                             reads=xbb + [b_wA], writes=pb)
                    dst = Vna[:, s, :].rearrange("p (h e) -> p h e", h=8)[:, :, 0:64]
                    S.op("dve", CP(dst, pp.rearrange("p (h e) -> p h e", h=8)), reads=pb, writes=[b_Vna[s]])
            if in_key:
                kt0 = slots[0] if blk < 4 else slots[0] - 4
                pp, pb = ps_single()
                for k in range(8):
                    S.op("pe", MM(pp, wA_sb[:, k, 1024:1152], xb[:, k, :], k == 0, k == 7),
                         reads=xbb + [b_wA], writes=pb)
                rms_rope(pp, pb, 1, blk * 512, tmpA,
                         [(KgT[:, kt0 * 128:kt0 * 128 + 512], 0, 128, [b_Kg[kt0 + i] for i in range(4)])])
                for i in range(4):
                    pp, pb = ps_single()
                    for k in range(8):
                        S.op("pe", MM(pp[:, 0:128], xb[:, k, i * 128:(i + 1) * 128], wA_sb[:, k, 1152:1280],
                                      k == 0, k == 7), reads=xbb + [b_wA], writes=pb)
                    dst = Vg[:, kt0 + i, :].rearrange("p (h e) -> p h e", h=2)[:, :, 0:64]
                    S.op("dve", CP(dst, pp[:, 0:128].rearrange("p (h e) -> p h e", h=2)), reads=pb,
                         writes=[b_Vg[kt0 + i]])
        for h in range(4):
            pp, pb = ps_single()
            for k in range(8):
                S.op("pe", MM(pp[:, 0:256], wm_sb[:, k, h * 128:(h + 1) * 128], memT_sb[:, k, :], k == 0, k == 7),
                     reads=[b_wm, b_memT], writes=pb)
            S.op("act", ACT(mkT[:, h, :], pp[:, 0:256], AF.Copy), reads=pb, writes=[b_mk])
        for mt in range(2):
            pp, pb = ps_single()
            for k in range(8):
                S.op("pe", MM(pp, memT_sb[:, k, mt * 128:(mt + 1) * 128], wm_sb[:, k, 512:1024], k == 0, k == 7),
                     reads=[b_wm, b_memT], writes=pb)
            dst = mv[:, mt, :].rearrange("p (h e) -> p h e", h=4)[:, :, 0:128]
            S.op("dve", CP(dst, pp.rearrange("p (h e) -> p h e", h=4)), reads=pb, writes=[b_mv])
        S.barrier()

        A.off = base_off
        offB = A.off
        Qna = A.get([128, 8, 512], BF16)
        Qg = A.get([128, 8, 512], BF16)
        Qm = A.get([128, 4, 512], BF16)
        tmpB = (A.get([128, 512], BF16), A.get([128, 512], F32), A.get([128, 512], BF16),
                A.get([128, 512], F32), A.get([128, 512], F32), A.get([128, 512], F32),
                A.get([128, 512], F32), [Buf() for _ in range(7)])
        qtmp = A.get([128, 512], BF16)
        wq_sb = [A.get([128, 8, 128], BF16) for _ in range(3)]
        bm_int = A.get([128, 8, 6, 128], BF16)
        bm_brd = A.get([128, 8, 6, 128], BF16)
        PTn = [A.get([128, 768], BF16) for _ in range(2)]
        PTg = [A.get([128, 512], BF16) for _ in range(3)]
        y_sb = [A.get([128, 1536], BF16) for _ in range(2)]
        rc = [A.get([128, 8], F32) for _ in range(2)]
        endB1 = A.off
        A.off = offB
        wg_sb = [A.get([128, 8, 128], BF16) for _ in range(4)]
        wbr_sb = [A.get([128, 12, 128], BF16) for _ in range(2)]
        wo_sb = [A.get([128, 8, 512], BF16) for _ in range(2)]
        gate_sb = [A.get([128, 512], BF16) for _ in range(3)]
        mtmp = [A.get([128, 512], F32) for _ in range(3)]
        mergedT = A.get([128, 8, 512], BF16)
        tbuf = [A.get([128, D], F32) for _ in range(4)]
        xh = [A.get([128, 512], F32) for _ in range(2)]
        ln_sb = A.get([128, 2 * D], F32)
        stat = [A.get([128, 8], F32) for _ in range(2)]
        junk = A.get([128, D], BF16)
        x1T_sb = A.get([128, D], F32)
        rowt = [A.get([128, 1056], BF16) for _ in range(2)]
        wr_sb = A.get([128, 8, 16], F32)
        rt = [A.get([128, 40], F32) for _ in range(2)]
        endB2 = A.off
        A.off = max(endB1, endB2)

        b_Qna, b_Qg, b_Qm, b_qtmp = Buf(), Buf(), Buf(), Buf()
        b_wq = [Buf() for _ in range(3)]
        b_bmi, b_bmb = Buf(), Buf()
        b_PTn = [Buf(), Buf()]
        b_PTg = [Buf() for _ in range(3)]
        b_y = [Buf(), Buf()]
        b_rc = [Buf(), Buf()]
        b_wg = [Buf() for _ in range(4)]
        b_wbr = [Buf(), Buf()]
        b_wo = [Buf(), Buf()]
        b_gate = [Buf() for _ in range(3)]
        b_mtmp = [Buf() for _ in range(3)]
        b_merged = Buf()
        b_t = [Buf() for _ in range(4)]
        b_xh = [Buf(), Buf()]
        b_ln = Buf()
        b_stat = [Buf(), Buf()]
        b_junk = Buf()
        b_x1T, b_wr = Buf(), Buf()
        b_rowt = [Buf(), Buf()]
        b_rt = [Buf(), Buf()]
        b_x1g, b_affbuf, b_ffn = Buf(), Buf(), Buf()
        b_x1f = Buf()
        b_dbg = Buf()

        load_xT(0, 0)
        cnt = {"wq": 0, "wg": 0, "wbr": 0, "wo": 0, "ptn": 0, "ptg": 0, "y": 0, "xh": 0}
        for blk in range(4):
            xi = blk % 2
            xb, xbb = xTb[xi], [b_xTb[xi]]
            S.op("dve", MSET(Qna, 0.0), writes=[b_Qna])
            S.op("dve", MSET(Qg, 0.0), writes=[b_Qg])
            S.dma("pool", DMA(bm_int.rearrange("p a b c -> p (a b c)"), bm[0]), writes=[b_bmi], sem="w")
            for c in range(12):
                wi = cnt["wq"] % 3
                cnt["wq"] += 1
                S.dma("pool", DMA(wq_sb[wi], wq[c]), writes=[b_wq[wi]], sem="w")
                pp, pb = ps_single()
                for k in range(8):
                    S.op("pe", MM(pp, wq_sb[wi][:, k, :], xb[:, k, :], k == 0, k == 7),
                         reads=xbb + [b_wq[wi]], writes=pb)
                if c < 4:
                    S.op("act", ACT(Qna[0:64, 2 * c, :], pp[0:64, :], AF.Copy, scale=0.125), reads=pb, writes=[b_Qna])
                    S.op("act", ACT(Qna[64:128, 2 * c + 1, :], pp[64:128, :], AF.Copy, scale=0.125), reads=pb,
                         writes=[b_Qna])
                elif c < 8:
                    i = c - 4
                    rms_rope(pp, pb, 0, blk * 512, tmpB,
                             [(Qg[0:64, i, :], 0, 64, [b_Qg]), (Qg[64:128, 4 + i, :], 64, 128, [b_Qg])])
                else:
                    S.op("act", ACT(Qm[:, c - 8, :], pp, AF.Copy), reads=pb, writes=[b_Qm])
            if debug and blk == 0:
                S.dma("sp", DMA(qg_dbg, Qg), reads=[b_Qg], writes=[b_dbg])
                S.dma("sp", DMA(qna_dbg, Qna), reads=[b_Qna], writes=[b_dbg])
                S.dma("sp", DMA(kna_dbg, KnaT), reads=b_Kna, writes=[b_dbg])
                S.dma("sp", DMA(kg_dbg, KgT), reads=b_Kg, writes=[b_dbg])
                S.dma("sp", DMA(vna_dbg, Vna), reads=b_Vna, writes=[b_dbg])
            for qt in range(4):
                jl = blk * 4 + qt
                yi = cnt["y"] % 2
                cnt["y"] += 1
                ysb, ybb = y_sb[yi], [b_y[yi]]
                qs = slice(qt * 128, (qt + 1) * 128)
                cls = na_class(jl)
                if cls == 0:
                    bmt, bmb = bm_int, [b_bmi]
                else:
                    S.dma("pool", DMA(bm_brd.rearrange("p a b c -> p (a b c)"), bm[cls]), writes=[b_bmb], sem="w")
                    bmt, bmb = bm_brd, [b_bmb]
                wl = na_wlist(jl)
                nk = len(wl)
                accs = [ps_acc(), ps_acc()]
                def na_scores(h):
                    sp, spb = ps_pair()
                    for ki, w in enumerate(wl):
                        s_ = win_slot(w)
                        S.op("pe", MM(sp[:, ki * 128:(ki + 1) * 128], KnaT[:, h // 2, s_ * 128:(s_ + 1) * 128],
                                      Qna[:, h, qs], ki % 4 == 0, False), reads=[b_Kna[s_], b_Qna], writes=spb)
                    S.op("pe", MM(sp[:, 0:512], ident, bmt[:, h, 0:4, :], False, True), reads=bmb + [b_const],
                         writes=spb)
                    S.op("pe", MM(sp[:, 512:nk * 128], ident, bmt[:, h, 4:nk, :], False, True),
                         reads=bmb + [b_const], writes=spb)
                    return sp, spb

                nxt = na_scores(0)
                for h in range(8):
                    sp, spb = nxt
                    if h + 1 < 8:
                        nxt = na_scores(h + 1)
                    pi = cnt["ptn"] % 2
                    cnt["ptn"] += 1
                    S.op("act", ACT(PTn[pi][:, 0:nk * 128], sp[:, 0:nk * 128], AF.Exp), reads=spb,
                         writes=[b_PTn[pi]])
                    ap_, ab_ = accs[h // 4]
                    hh = h % 4
                    for ki, w in enumerate(wl):
                        s_ = win_slot(w)
                        S.op("pe", MM(ap_[:, hh * 65:hh * 65 + 65], PTn[pi][:, ki * 128:(ki + 1) * 128],
                                      Vna[:, s_, h * 65:h * 65 + 65], ki == 0 and hh == 0, ki == nk - 1),
                             reads=[b_PTn[pi], b_Vna[s_]], writes=ab_)
                for half in range(2):
                    ap_, ab_ = accs[half]
                    a3 = ap_[:, 0:260].rearrange("p (h e) -> p h e", h=4)
                    ri = half
                    S.op("dve", RECIP(rc[ri][:, 0:4], a3[:, :, 64]), reads=ab_, writes=[b_rc[ri]])
                    S.op("dve", TT(ysb[:, half * 256:(half + 1) * 256].rearrange("p (h e) -> p h e", h=4),
                                   a3[:, :, 0:64], rc[ri][:, 0:4].unsqueeze(2).to_broadcast([128, 4, 64]), ALU.mult),
                         reads=ab_ + [b_rc[ri]], writes=ybb)
                for g in range(2):
                    ap_, ab_ = ps_acc()

                    def gqa_scores(kt, g=g):
                        sp, spb = ps_single()
                        S.op("pe", MM(sp, KgT[:, kt * 128:(kt + 1) * 128], Qg[:, 4 * g:4 * g + 4, qs], True, True),
                             reads=[b_Kg[kt], b_Qg], writes=spb)
                        return sp, spb

                    pend = [gqa_scores(0), gqa_scores(1)]
                    for kt in range(32):
                        sp, spb = pend.pop(0)
                        if kt + 2 < 32:
                            pend.append(gqa_scores(kt + 2))
                        pi = cnt["ptg"] % 3
                        cnt["ptg"] += 1
                        S.op("act", ACT(PTg[pi], sp, AF.Exp, scale=0.125), reads=spb, writes=[b_PTg[pi]])
                        for hh in range(4):
                            S.op("pe", MM(ap_[:, hh * 65:hh * 65 + 65], PTg[pi][:, hh * 128:(hh + 1) * 128],
                                          Vg[:, kt, g * 65:g * 65 + 65], kt == 0 and hh == 0, kt == 31),
                                 reads=[b_PTg[pi], b_Vg[kt]], writes=ab_)
                    a3 = ap_[:, 0:260].rearrange("p (h e) -> p h e", h=4)
                    S.op("dve", RECIP(rc[g][:, 4:8], a3[:, :, 64]), reads=ab_, writes=[b_rc[g]])
                    S.op("dve", TT(ysb[:, 512 + g * 256:512 + (g + 1) * 256].rearrange("p (h e) -> p h e", h=4),
                                   a3[:, :, 0:64], rc[g][:, 4:8].unsqueeze(2).to_broadcast([128, 4, 64]), ALU.mult),
                         reads=ab_ + [b_rc[g]], writes=ybb)
                for hp in range(2):
                    ap_, ab_ = ps_acc()

                    def mem_scores(i, hp=hp):
                        h_, mt_ = hp * 2 + i // 2, i % 2
                        sp, spb = ps_single()
                        S.op("pe", MM(sp[:, 0:128], mkT[:, h_, mt_ * 128:(mt_ + 1) * 128], Qm[:, h_, qs], True, True),
                             reads=[b_mk, b_Qm], writes=spb)
                        return sp, spb

                    pend = [mem_scores(0), mem_scores(1)]
                    for i in range(4):
                        hq, mt = i // 2, i % 2
                        h = hp * 2 + hq
                        sp, spb = pend.pop(0)
                        if i + 2 < 4:
                            pend.append(mem_scores(i + 2))
                        pi = cnt["ptg"] % 3
                        cnt["ptg"] += 1
                        S.op("act", ACT(PTg[pi][:, 0:128], sp[:, 0:128], AF.Exp, scale=1.0 / math.sqrt(128.0)),
                             reads=spb, writes=[b_PTg[pi]])
                        S.op("pe", MM(ap_[:, hq * 129:hq * 129 + 129], PTg[pi][:, 0:128],
                                      mv[:, mt, h * 129:h * 129 + 129], mt == 0 and hq == 0, mt == 1),
                             reads=[b_PTg[pi], b_mv], writes=ab_)
                    a3 = ap_[:, 0:258].rearrange("p (h e) -> p h e", h=2)
                    S.op("dve", RECIP(rc[hp][:, 0:2], a3[:, :, 128]), reads=ab_, writes=[b_rc[hp]])
                    S.op("dve", TT(ysb[:, 1024 + hp * 256:1024 + (hp + 1) * 256].rearrange("p (h e) -> p h e", h=2),
                                   a3[:, :, 0:128], rc[hp][:, 0:2].unsqueeze(2).to_broadcast([128, 2, 128]),
                                   ALU.mult), reads=ab_ + [b_rc[hp]], writes=ybb)
                if debug:
                    S.dma("sp", DMA(y_dbg[jl * 128:(jl + 1) * 128, :], ysb), reads=ybb, writes=[b_dbg])
                for grp in range(3):
                    tp, tpb = ps_single()
                    tpv = tp.bitcast(BF16)
                    for c in range(4):
                        cc = grp * 4 + c
                        S.op("pe", TR(tpv[:, c * 128:(c + 1) * 128], ysb[:, cc * 128:(cc + 1) * 128], ident),
                             reads=ybb + [b_const], writes=tpb)
                    S.op("dve", CP(yT[:, grp * 4:grp * 4 + 4, qs],
                                   tpv[:, 0:512].rearrange("p (c t) -> p c t", c=4)), reads=tpb, writes=[b_yT])
            S.barrier()
            if blk == 0:
                pass
            S.dma("sp", DMA(ln_sb, ln1), writes=[b_ln])
            S.dma("sp", DMA(wr_sb, wr), writes=[b_wr])
            for dc in range(8):
                bi = cnt["wbr"] % 2
                cnt["wbr"] += 1
                S.dma("pool", DMA(wbr_sb[bi], wbr[dc]), writes=[b_wbr[bi]], sem="w")
                for g in range(3):
                    wi = cnt["wg"] % 4
                    cnt["wg"] += 1
                    S.dma("pool", DMA(wg_sb[wi], wg[g * 8 + dc]), writes=[b_wg[wi]], sem="w")
                    pp, pb = ps_single()
                    for k in range(8):
                        S.op("pe", MM(pp, wg_sb[wi][:, k, :], xb[:, k, :], k == 0, k == 7),
                             reads=xbb + [b_wg[wi]], writes=pb)
                    S.op("act", ACT(gate_sb[g], pp, AF.Sigmoid, bias=bg[:, g * 8 + dc:g * 8 + dc + 1]),
                         reads=pb + [b_bg], writes=[b_gate[g]])
                for g in range(3):
                    pp, pb = ps_single()
                    for c in range(4):
                        S.op("pe", MM(pp, wbr_sb[bi][:, g * 4 + c, :], yT[:, g * 4 + c, :], c == 0, c == 3),
                             reads=[b_wbr[bi], b_yT], writes=pb)
                    S.op("dve", TT(mtmp[g], pp, gate_sb[g], ALU.mult), reads=pb + [b_gate[g]], writes=[b_mtmp[g]])
                S.op("dve", TT(mtmp[0], mtmp[0], mtmp[1], ALU.add), reads=[b_mtmp[0], b_mtmp[1]],
                     writes=[b_mtmp[0]])
                S.op("dve", TT(mergedT[:, dc, :], mtmp[0], mtmp[2], ALU.add), reads=[b_mtmp[0], b_mtmp[2]],
                     writes=[b_merged])
            if debug:
                S.dma("sp", DMA(mg_dbg[:, :, blk * 512:(blk + 1) * 512], mergedT), reads=[b_merged], writes=[b_dbg])
            if blk + 1 < 4:
                load_xT(blk + 1, 1 - xi)
            for half in range(2):
                oi = cnt["wo"] % 2
                cnt["wo"] += 1
                S.dma("pool", DMA(wo_sb[oi], wo[half]), writes=[b_wo[oi]], sem="w")
                for qt in range(4):
                    jl = blk * 4 + qt
                    hi_ = cnt["xh"] % 2
                    cnt["xh"] += 1
                    S.dma("sp", DMA(xh[hi_], x_own[jl * 128:(jl + 1) * 128, half * 512:(half + 1) * 512]),
                          writes=[b_xh[hi_]])
                    pp, pb = ps_single()
                    for k in range(8):
                        S.op("pe", MM(pp, mergedT[:, k, qt * 128:(qt + 1) * 128], wo_sb[oi][:, k, :], k == 0, k == 7),
                             reads=[b_merged, b_wo[oi]], writes=pb)
                    S.op("dve", STT(tbuf[qt][:, half * 512:(half + 1) * 512], xh[hi_], ALPHA, pp, ALU.mult, ALU.add),
                         reads=pb + [b_xh[hi_]], writes=[b_t[qt]])
            for qt in range(4):
                jl = blk * 4 + qt
                si = qt % 2
                t = tbuf[qt]
                st_ = stat[si]
                S.op("act", ACT(junk, t, AF.Copy, accum_out=st_[:, 0:1]), reads=[b_t[qt]], writes=[b_junk, b_stat[si]])
                S.op("act", ACT(junk, t, AF.Square, accum_out=st_[:, 1:2]), reads=[b_t[qt]],
                     writes=[b_junk, b_stat[si]])
                S.op("dve", TS(st_[:, 2:3], st_[:, 0:1], 1.0 / D, None, ALU.mult), reads=[b_stat[si]],
                     writes=[b_stat[si]])
                S.op("dve", TT(st_[:, 3:4], st_[:, 2:3], st_[:, 2:3], ALU.mult), reads=[b_stat[si]],
                     writes=[b_stat[si]])
                S.op("dve", STT(st_[:, 4:5], st_[:, 1:2], 1.0 / D, st_[:, 3:4], ALU.mult, ALU.subtract),
                     reads=[b_stat[si]], writes=[b_stat[si]])
                S.op("act", ACT(st_[:, 5:6], st_[:, 4:5], AF.Sqrt, bias=eps_t[:, 1:2]), reads=[b_stat[si], b_const],
                     writes=[b_stat[si]])
                S.op("dve", RECIP(st_[:, 5:6], st_[:, 5:6]), reads=[b_stat[si]], writes=[b_stat[si]])
                S.op("dve", TS(t, t, st_[:, 2:3], st_[:, 5:6], ALU.subtract, ALU.mult), reads=[b_t[qt], b_stat[si]],
                     writes=[b_t[qt]])
                S.op("dve", TT(t, t, ln_sb[:, 0:D], ALU.mult), reads=[b_t[qt], b_ln], writes=[b_t[qt]])
                S.op("dve", TT(t, t, ln_sb[:, D:2 * D], ALU.add), reads=[b_t[qt], b_ln], writes=[b_t[qt]])
                S.dma("sp", DMA(x1_f[jl * 128:(jl + 1) * 128, :], t), reads=[b_t[qt]], writes=[b_x1f])
                tp2, tpb2 = ps_pair()
                for k in range(8):
                    S.op("pe", TR(tp2[:, k * 128:(k + 1) * 128], t[:, k * 128:(k + 1) * 128], idf),
                         reads=[b_t[qt], b_const], writes=tpb2)
                S.op("act", ACT(x1T_sb, tp2, AF.Copy), reads=tpb2, writes=[b_x1T])
                lp, lpb = ps_single()
                for k in range(8):
                    S.op("pe", MM(lp[:, 0:16], x1T_sb[:, k * 128:(k + 1) * 128], wr_sb[:, k, :], k == 0, k == 7),
                         reads=[b_x1T, b_wr], writes=lpb)
                ri = qt % 2
                r_ = rt[ri]
                S.op("dve", lambda e, o=r_[:, 0:1], i=lp[:, 0:16]: e.reduce_max(out=o, in_=i, axis=AX.X),
                     reads=lpb, writes=[b_rt[ri]])
                S.op("dve", TS(r_[:, 1:2], r_[:, 0:1], -1.0, None, ALU.mult), reads=[b_rt[ri]], writes=[b_rt[ri]])
                S.op("act", ACT(r_[:, 8:24], lp[:, 0:16], AF.Exp, bias=r_[:, 1:2], accum_out=r_[:, 2:3]),
                     reads=lpb + [b_rt[ri]], writes=[b_rt[ri]])
                S.op("dve", RECIP(r_[:, 3:4], r_[:, 2:3]), reads=[b_rt[ri]], writes=[b_rt[ri]])
                S.op("dve", TS(aff_own[:, jl, :], r_[:, 8:24], r_[:, 3:4], None, ALU.mult), reads=[b_rt[ri]],
                     writes=[b_aff])
                rw = rowt[ri]
                S.op("act", ACT(rw[:, 0:1024], t, AF.Copy), reads=[b_t[qt]], writes=[b_rowt[ri]])
                S.op("dve", CP(rw[:, 1024:1056].bitcast(F32), aff_own[:, jl, :]), reads=[b_aff], writes=[b_rowt[ri]])
                S.dma("sp", DMA(x1g[jl * 128:(jl + 1) * 128, :], rw), reads=[b_rowt[ri]], writes=[b_x1g])
                S.dma("sp", DMA(affbuf[jl * 128:(jl + 1) * 128, :], aff_own[:, jl, :]), reads=[b_aff],
                      writes=[b_affbuf])
            S.barrier()

        S.barrier()
        A.off = base_off0
        affall_sb = A.get([128, 32, 16], F32)
        cmp_sb = A.get([128, 512], BF16)
        lo = A.get([128, 16], F32)
        tr_ = A.get([128, 16], F32)
        cntt = A.get([128, 16], F32)
        ge = A.get([128, 16], F32)
        M_sb = A.get([128, 16, 16], BF16)
        c_incl = A.get([128, 16, 16], F32)
        ex = A.get([128, 16, 16], F32)
        iota_sb = A.get([128, 512], F32)
        zrow = A.get([128, 1056], BF16)
        cmpS = [A.get([128, 512], BF16) for _ in range(4)]
        idx_i = [A.get([128, 4], I32) for _ in range(2)]
        idx_f = [A.get([128, 8], F32) for _ in range(2)]
        slot_sb = A.get([128, 4], F32)
        Wgu_sb = [A.get([128, 8, 2048], BF16) for _ in range(2)]
        Wd_sb = [A.get([128, 8, 1024], BF16) for _ in range(2)]
        xs = [[A.get([128, 1056], BF16) for _ in range(4)] for _ in range(2)]
        xsT = [A.get([128, 8, 512], BF16) for _ in range(2)]
        sg = [A.get([128, 512], BF16) for _ in range(2)]
        hT = [A.get([128, 8, 512], BF16) for _ in range(2)]
        yo = [A.get([128, D], F32) for _ in range(3)]
        ln2_sb = A.get([128, 2 * D], F32)
        b_affall, b_cmp, b_lo, b_tr, b_cnt, b_ge, b_M, b_cinc, b_ex, b_iota, b_z = [Buf() for _ in range(11)]
        b_cmpS = [Buf() for _ in range(4)]
        b_idx = [Buf(), Buf()]
        b_Wgu = [[Buf() for _ in range(4)] for _ in range(2)]
        b_Wd = [[Buf() for _ in range(2)] for _ in range(2)]
        b_xs = [[Buf() for _ in range(4)] for _ in range(2)]
        b_xsT = [Buf(), Buf()]
        b_sg = [Buf(), Buf()]
        b_hT = [Buf(), Buf()]
        b_yo = [Buf() for _ in range(3)]
        b_ln2 = Buf()
        b_out = Buf()

        def load_w(e):
            wi = e % 2
            for qd in range(4):
                S.dma("pool", DMA(Wgu_sb[wi][:, :, qd * 512:(qd + 1) * 512], wgu[e][:, :, qd * 512:(qd + 1) * 512]),
                      writes=[b_Wgu[wi][qd]], sem="w")
            for hf in range(2):
                S.dma("pool", DMA(Wd_sb[wi][:, :, hf * 512:(hf + 1) * 512], wd[e][:, :, hf * 512:(hf + 1) * 512]),
                      writes=[b_Wd[wi][hf]], sem="w")

        load_w(0)
        S.dma("sp", DMA(iota_sb, iota), writes=[b_iota])
        S.dma("sp", DMA(ln2_sb, ln2), writes=[b_ln2])
        S.op("dve", MSET(zrow, 0.0), writes=[b_z])
        for c in range(4):
            S.dma("sp", DMA(x1g[2048 + c * 128:2048 + (c + 1) * 128, :], zrow), reads=[b_z], writes=[b_x1g])
        S.dma("sp", DMA(slot_sb, slotn), writes=[b_iota])
        S.op("dve", MSET(yo[0], 0.0), writes=[b_yo[0]])
        for j in range(16):
            S.dma("sp", DMA(ffn[j * 128:(j + 1) * 128, :], yo[0]), reads=[b_yo[0]], writes=[b_ffn])
        S.coll("pool", lambda e: e.collective_compute("AllGather", ALU.bypass,
                                                      replica_groups=[[0, 1], [2, 3], [4, 5], [6, 7]],
                                                      ins=[affbuf.opt()], outs=[affall.opt()]),
               reads=[b_affbuf], writes=[b_affall])
        S.dma("sp", DMA(affall_sb.rearrange("p t e -> p (t e)"), affall.rearrange("(p t) e -> p (t e)", t=32)),
              reads=[b_affall], writes=[b_affall])
        S.op("dve", MSET(lo, 0.0), writes=[b_lo])
        step = 0.5
        for it in range(30):
            S.op("dve", TS(tr_, lo, step, None, ALU.add), reads=[b_lo], writes=[b_tr])
            S.op("dve", TT(cmp_sb.rearrange("p (t e) -> p t e", e=16), affall_sb,
                           tr_.unsqueeze(1).to_broadcast([128, 32, 16]), ALU.is_ge), reads=[b_affall, b_tr],
                 writes=[b_cmp])
            cp_, cpb = ps_single()
            S.op("pe", MM(cp_, ones_bf, cmp_sb, True, True), reads=[b_cmp, b_const], writes=cpb)
            S.op("dve", lambda e, o=cntt, i=cp_.rearrange("p (t e) -> p e t", e=16): e.reduce_sum(out=o, in_=i, axis=AX.X),
                 reads=cpb, writes=[b_cnt])
            S.op("dve", TS(ge, cntt, float(CAP) - 0.5, None, ALU.is_ge), reads=[b_cnt], writes=[b_ge])
            S.op("dve", STT(lo, ge, step, lo, ALU.mult, ALU.add), reads=[b_ge, b_lo], writes=[b_lo])
            step *= 0.5
        if debug:
            S.dma("sp", DMA(thr_dbg, lo), reads=[b_lo], writes=[b_dbg])
            S.dma("sp", DMA(aff_dbg, aff_own.rearrange("p t e -> p (t e)")), reads=[b_aff], writes=[b_dbg])
        S.op("dve", TT(M_sb, aff_own, lo.unsqueeze(1).to_broadcast([128, 16, 16]), ALU.is_ge), reads=[b_aff, b_lo],
             writes=[b_M])
        p1, p1b = ps_single()
        S.op("pe", MM(p1[:, 0:256], tri_bf, M_sb.rearrange("p t e -> p (t e)"), True, True), reads=[b_M, b_const],
             writes=p1b)
        p2, p2b = ps_single()
        S.op("pe", MM(p2[:, 0:256], ones_bf, M_sb.rearrange("p t e -> p (t e)"), True, True), reads=[b_M, b_const],
             writes=p2b)
        S.op("dve", MSET(ex[:, 0, :], 0.0), writes=[b_ex])
        for j in range(1, 16):
            S.op("dve", TT(ex[:, j, :], ex[:, j - 1, :], p2[:, (j - 1) * 16:j * 16], ALU.add), reads=p2b + [b_ex],
                 writes=[b_ex])
        S.op("dve", TT(c_incl.rearrange("p t e -> p (t e)"), p1[:, 0:256], ex.rearrange("p t e -> p (t e)"), ALU.add),
             reads=p1b + [b_ex], writes=[b_cinc])

        def make_idx(e):
            ii = e % 2
            ip, ipb = ps_acc()
            for j in range(16):
                ci = j % 4
                eng = "dve"
                S.op(eng, TS(cmpS[ci], iota_sb, c_incl[:, j, e:e + 1], None, ALU.is_ge), reads=[b_iota, b_cinc],
                     writes=[b_cmpS[ci]])
                for c in range(4):
                    S.op("pe", MM(ip[:, c:c + 1], cmpS[ci][:, c * 128:(c + 1) * 128], ones_bf[:, 0:1],
                                  j == 0 and c == 0, j == 15), reads=[b_cmpS[ci], b_const], writes=ipb)
            S.op("dve", TS(idx_f[ii][:, 4:8], ip[:, 0:4], 2047.5, None, ALU.is_ge), reads=ipb, writes=[b_idx[ii]])
            S.op("dve", TT(idx_f[ii][:, 4:8], idx_f[ii][:, 4:8], slot_sb, ALU.mult), reads=[b_idx[ii], b_iota],
                 writes=[b_idx[ii]])
            S.op("dve", TT(idx_f[ii][:, 0:4], idx_f[ii][:, 4:8], ip[:, 0:4], ALU.add), reads=ipb + [b_idx[ii]],
                 writes=[b_idx[ii]])
            S.op("dve", CP(idx_i[ii], idx_f[ii][:, 0:4]), reads=[b_idx[ii]], writes=[b_idx[ii]])
            if debug:
                S.dma("sp", DMA(idx_dbg[e], idx_i[ii]), reads=[b_idx[ii]], writes=[b_dbg])

        make_idx(0)
        yo_n = 0
        for e in range(16):
            wi = e % 2
            if e + 1 < 16:
                load_w(e + 1)
                make_idx(e + 1)
            for c in range(4):
                S.dma("pool", lambda eng, o=xs[wi][c], ix=idx_i[wi][:, c:c + 1]: eng.indirect_dma_start(
                    out=o[:, :], out_offset=None, in_=x1g[:, :],
                    in_offset=bass.IndirectOffsetOnAxis(ap=ix, axis=0)),
                    reads=[b_idx[wi], b_x1g], writes=[b_xs[wi][c]], sem="g")
            for c in range(4):
                tp, tpb = ps_single()
                tpv = tp.bitcast(BF16)
                for k in range(8):
                    S.op("pe", TR(tpv[:, k * 128:(k + 1) * 128], xs[wi][c][:, k * 128:(k + 1) * 128], ident),
                         reads=[b_xs[wi][c], b_const], writes=tpb)
                eng = "dve" if c % 2 == 0 else "act"
                if eng == "dve":
                    S.op("dve", CP(xsT[wi][:, :, c * 128:(c + 1) * 128], tpv.rearrange("p (k t) -> p k t", k=8)),
                         reads=tpb, writes=[b_xsT[wi]])
                else:
                    S.op("act", ACT(xsT[wi][:, :, c * 128:(c + 1) * 128], tpv.rearrange("p (k t) -> p k t", k=8),
                                    AF.Copy), reads=tpb, writes=[b_xsT[wi]])
            for fc in range(8):
                gp, gpb = ps_single()
                for k in range(8):
                    S.op("pe", MM(gp, Wgu_sb[wi][:, k, fc * 128:(fc + 1) * 128], xsT[wi][:, k, :], k == 0, k == 7),
                         reads=[b_Wgu[wi][fc // 4], b_xsT[wi]], writes=gpb)
                up, upb = ps_single()
                for k in range(8):
                    S.op("pe", MM(up, Wgu_sb[wi][:, k, 1024 + fc * 128:1024 + (fc + 1) * 128], xsT[wi][:, k, :],
                                  k == 0, k == 7), reads=[b_Wgu[wi][2 + fc // 4], b_xsT[wi]], writes=upb)
                si = fc % 2
                S.op("act", ACT(sg[si], gp, AF.Silu), reads=gpb, writes=[b_sg[si]])
                S.op("dve", TT(hT[wi][:, fc, :], up, sg[si], ALU.mult), reads=upb + [b_sg[si]], writes=[b_hT[wi]])
            for c in range(4):
                yi = yo_n % 3
                yo_n += 1
                gcol = xs[wi][c][:, 1024 + 2 * e:1024 + 2 * e + 2].bitcast(F32)
                for hf in range(2):
                    dp, dpb = ps_single()
                    for fc in range(8):
                        S.op("pe", MM(dp, hT[wi][:, fc, c * 128:(c + 1) * 128], Wd_sb[wi][:, fc, hf * 512:(hf + 1) * 512],
                                      fc == 0, fc == 7), reads=[b_hT[wi], b_Wd[wi][hf]], writes=dpb)
                    if hf == 0:
                        S.op("act", ACT(yo[yi][:, 0:512], dp, AF.Copy, scale=gcol), reads=dpb + [b_xs[wi][c]],
                             writes=[b_yo[yi]])
                    else:
                        S.op("dve", TS(yo[yi][:, 512:1024], dp, gcol, None, ALU.mult), reads=dpb + [b_xs[wi][c]],
                             writes=[b_yo[yi]])
                S.dma("pool", lambda eng, i_=yo[yi], ix=idx_i[wi][:, c:c + 1]: eng.indirect_dma_start(
                    out=ffn[:, :], out_offset=bass.IndirectOffsetOnAxis(ap=ix, axis=0), in_=i_[:, :], in_offset=None,
                    compute_op=ALU.add),
                    reads=[b_idx[wi], b_yo[yi], b_ffn], writes=[b_ffn], sem="g")
        S.barrier()
        A.off = base_off0
        fbuf = [A.get([128, D], F32) for _ in range(2)]
        xbuf = [A.get([128, D], F32) for _ in range(2)]
        fst = [A.get([128, 8], F32) for _ in range(2)]
        fjunk = A.get([128, D], BF16)
        ln2b = A.get([128, 2 * D], F32)
        b_fb, b_xb, b_fst = [Buf(), Buf()], [Buf(), Buf()], [Buf(), Buf()]
        b_fj, b_l2 = Buf(), Buf()
        S.dma("sp", DMA(ln2b, ln2), writes=[b_l2])
        for j in range(16):
            i = j % 2
            S.dma("sp", DMA(fbuf[i], ffn[j * 128:(j + 1) * 128, :]), reads=[b_ffn], writes=[b_fb[i]])
            S.dma("sp", DMA(xbuf[i], x1_f[j * 128:(j + 1) * 128, :]), reads=[b_x1f], writes=[b_xb[i]])
            t = fbuf[i]
            st_ = fst[i]
            S.op("dve", STT(t, xbuf[i], ALPHA, t, ALU.mult, ALU.add), reads=[b_xb[i], b_fb[i]], writes=[b_fb[i]])
            S.op("act", ACT(fjunk, t, AF.Copy, accum_out=st_[:, 0:1]), reads=[b_fb[i]], writes=[b_fj, b_fst[i]])
            S.op("act", ACT(fjunk, t, AF.Square, accum_out=st_[:, 1:2]), reads=[b_fb[i]], writes=[b_fj, b_fst[i]])
            S.op("dve", TS(st_[:, 2:3], st_[:, 0:1], 1.0 / D, None, ALU.mult), reads=[b_fst[i]], writes=[b_fst[i]])
            S.op("dve", TT(st_[:, 3:4], st_[:, 2:3], st_[:, 2:3], ALU.mult), reads=[b_fst[i]], writes=[b_fst[i]])
            S.op("dve", STT(st_[:, 4:5], st_[:, 1:2], 1.0 / D, st_[:, 3:4], ALU.mult, ALU.subtract),
                 reads=[b_fst[i]], writes=[b_fst[i]])
            S.op("act", ACT(st_[:, 5:6], st_[:, 4:5], AF.Sqrt, bias=eps_t[:, 1:2]), reads=[b_fst[i], b_const],
                 writes=[b_fst[i]])
            S.op("dve", RECIP(st_[:, 5:6], st_[:, 5:6]), reads=[b_fst[i]], writes=[b_fst[i]])
            S.op("dve", TS(t, t, st_[:, 2:3], st_[:, 5:6], ALU.subtract, ALU.mult), reads=[b_fb[i], b_fst[i]],
                 writes=[b_fb[i]])
            S.op("dve", TT(t, t, ln2b[:, 0:D], ALU.mult), reads=[b_fb[i], b_l2], writes=[b_fb[i]])
            S.op("dve", TT(t, t, ln2b[:, D:2 * D], ALU.add), reads=[b_fb[i], b_l2], writes=[b_fb[i]])
            S.dma("sp", DMA(out_d[j * 128:(j + 1) * 128, :], t), reads=[b_fb[i]], writes=[b_out])
        S.barrier()
        S.emit(st)
    return nc


def _rope_tables():
    t = np.arange(SEQ)
    row = (t // 64).astype(np.float32)
    col = (t % 64).astype(np.float32)
    inv = (np.float32(10000.0) ** (-np.arange(0, 32, 2, dtype=np.float32) / np.float32(32))).astype(np.float32)
    ang = np.concatenate([row[:, None] * inv, col[:, None] * inv], axis=-1).astype(np.float32)
    return np.cos(ang).astype(np.float32), np.sin(ang).astype(np.float32)


def _bm_tile(rpb, jg, ktg):
    if ktg < 0 or ktg > 31:
        return np.full((8, 128, 128), NEG, np.float32)
    i = np.arange(128)
    kr, kc = (i // 64)[:, None], (i % 64)[:, None]
    qr, qc = (i // 64)[None, :], (i % 64)[None, :]
    key_row = 2 * ktg + kr
    r = 2 * jg + qr
    rs = np.clip(r - 4, 0, 56)
    cs = np.clip(qc - 8, 0, 48)
    valid = (key_row >= rs) & (key_row < rs + 8) & (kc >= cs) & (kc < cs + 16)
    dr = np.clip(key_row - r + 7, 0, 14)
    dc = np.clip(kc - qc + 15, 0, 30)
    dr, dc = np.broadcast_arrays(dr, dc)
    vals = rpb[:, dr, dc]
    return np.where(valid[None], vals, np.float32(NEG)).astype(np.float32)


def _core_inputs(b, h, inp, shared):
    x = inp["x"][b]
    own = [16 * h + i for i in range(16)]
    order = own + [16 * h - 2, 16 * h - 1, 16 * h + 16, 16 * h + 17] + [16 * (1 - h) + i for i in range(16)]
    xTc = np.zeros((D, NTOK), np.float32)
    cosT = np.zeros((128, NTOK), np.float32)
    sinT = np.zeros((128, NTOK), np.float32)
    cos, sin = shared["rope"]
    pidx = (np.arange(128) % 64) // 2
    for s, t in enumerate(order):
        if 0 <= t < 32:
            xTc[:, s * 128:(s + 1) * 128] = x[t * 128:(t + 1) * 128, :].T
            cosT[:, s * 128:(s + 1) * 128] = cos[t * 128:(t + 1) * 128, :][:, pidx].T
            sinT[:, s * 128:(s + 1) * 128] = sin[t * 128:(t + 1) * 128, :][:, pidx].T
    rpb = inp["na_rpb"][0]
    bmc = np.full((5, 128, 8, 6, 128), NEG, np.float32)
    for cls, jl in enumerate([2, 0, 1, 14, 15]):
        jg = 16 * h + jl
        for ki, w in enumerate(na_wlist(jl)):
            ktg = 16 * h - 2 + w
            bmc[cls, :, :, ki, :] = _bm_tile(rpb, jg, ktg).transpose(1, 0, 2)
    d = dict(shared["common"])
    d["xT"] = np.ascontiguousarray(xTc.reshape(8, 128, NTOK).transpose(1, 0, 2))
    d["x_own"] = np.ascontiguousarray(x[h * 2048:(h + 1) * 2048])
    d["memT"] = np.ascontiguousarray(inp["mem"][b].T.reshape(8, 128, 256).transpose(1, 0, 2))
    d["bm"] = np.ascontiguousarray(bmc.reshape(5, 128, 8 * 6 * 128))
    d["cosT"] = cosT
    d["sinT"] = sinT
    return d


def _pk(w):
    return np.ascontiguousarray(w.reshape(8, 128, -1).transpose(1, 0, 2))


def _shared(inp):
    w_in = inp["w_in"][0]
    q_na, k_na, v_na = w_in[:, 0:512], w_in[:, 512:1024], w_in[:, 1024:1536]
    q_g, k_g, v_g = w_in[:, 1536:2048], w_in[:, 2048:2176], w_in[:, 2176:2304]
    q_m, gl = w_in[:, 2304:2816], w_in[:, 2816:5888]
    wA = _pk(np.concatenate([k_na, v_na, k_g, v_g], axis=1))
    chunks = [q_na[:, c * 128:(c + 1) * 128] for c in range(4)]
    for i in range(4):
        chunks.append(np.concatenate([q_g[:, i * 64:(i + 1) * 64], q_g[:, (4 + i) * 64:(5 + i) * 64]], axis=1))
    chunks += [q_m[:, c * 128:(c + 1) * 128] for c in range(4)]
    wq = np.stack([_pk(c) for c in chunks])
    wg = np.stack([_pk(gl[:, c * 128:(c + 1) * 128]) for c in range(24)])
    wb = inp["w_branch"][0]
    wbr = np.zeros((8, 128, 12, 128), np.float32)
    for dc in range(8):
        for g in range(3):
            for c in range(4):
                wbr[dc, :, g * 4 + c, :] = wb[g, c * 128:(c + 1) * 128, dc * 128:(dc + 1) * 128]
    w_out = inp["w_out"][0]
    wo = np.stack([_pk(w_out[:, hf * 512:(hf + 1) * 512]) for hf in range(2)])
    wm = _pk(inp["w_mem_kv"][0])
    ident = np.eye(128, dtype=np.float32)
    bones = np.kron(np.eye(2, dtype=np.float32), np.ones((64, 64), np.float32))
    rot = np.zeros((128, 128), np.float32)
    for i in range(64):
        rot[2 * i + 1, 2 * i] = -1.0
        rot[2 * i, 2 * i + 1] = 1.0
    gains = np.stack([np.tile(inp["gqa_q_gain"][0], 2), np.tile(inp["gqa_k_gain"][0], 2)], axis=1)
    bgate = np.ascontiguousarray(inp["b_gate"][0].reshape(24, 128).T)
    ln1 = np.concatenate([np.broadcast_to(inp["ln1_g"][0], (128, D)), np.broadcast_to(inp["ln1_b"][0], (128, D))],
                         axis=1)
    ones = np.ones((128, 128), np.float32)
    tri = np.triu(np.ones((128, 128), np.float32))
    wgu = np.ascontiguousarray(inp["w_gate_up"][0].reshape(16, 8, 128, 2048).transpose(0, 2, 1, 3))
    wd = np.ascontiguousarray(inp["w_down"][0].reshape(16, 8, 128, 1024).transpose(0, 2, 1, 3))
    ln2 = np.concatenate([np.broadcast_to(inp["ln2_g"][0], (128, D)), np.broadcast_to(inp["ln2_b"][0], (128, D))],
                         axis=1)
    common = {
        "identf": ident, "iota": np.ascontiguousarray(np.broadcast_to(np.arange(512, dtype=np.float32), (128, 512))),
        "wr": _pk(inp["w_router"][0]),
        "slotn": np.ascontiguousarray((np.arange(128, dtype=np.float32)[:, None]
                                       + 128.0 * np.arange(4, dtype=np.float32)[None, :])), "ln2": np.ascontiguousarray(ln2.astype(np.float32)),
        "wgu": wgu, "wd": wd,
        "wA": wA, "wq": wq, "wg": wg, "wbr": wbr, "wo": wo, "wm": wm,
        "cmat": np.ascontiguousarray(np.concatenate([ident, bones, rot, ones, tri], axis=1)),
        "gains": np.ascontiguousarray(gains.astype(np.float32)),
        "bgate": bgate.astype(np.float32),
        "ln1": np.ascontiguousarray(ln1.astype(np.float32)),
    }
    return {"common": common, "rope": _rope_tables()}


def run(inp, debug=False):
    inp = {k: np.asarray(v) for k, v in inp.items()}
    shared = _shared(inp)
    in_maps = [_core_inputs(c // 2, c % 2, inp, shared) for c in range(8)]
    nc = build_program(debug=debug)
    res = run_bass_kernel_spmd(nc, in_maps, core_ids=list(range(8)))
    return res


def kernel(**inputs):
    res = run(inputs)
    out = np.zeros((4, SEQ, D), np.float32)
    for c in range(8):
        b, h = c // 2, c % 2
        out[b, h * 2048:(h + 1) * 2048] = res.results[c]["out"]
    return out
```

```python
import math
from contextlib import ExitStack

import numpy as np
import concourse.bass as bass
import concourse.mybir as mybir
from concourse.bass_utils import run_bass_kernel_spmd

F32 = mybir.dt.float32
BF16 = mybir.dt.bfloat16
I32 = mybir.dt.int32
AF = mybir.ActivationFunctionType
ALU = mybir.AluOpType
AX = mybir.AxisListType

D = 1024
SEQ = 4096
NT_OWN = 16
NSLOT = 36
NTOK = NSLOT * 128
ALPHA = 2.0 ** 0.25
LN_EPS = 1e-5
RMS_EPS = 1e-6
NEG = -30000.0
NE = 16
CAP = 512


class Buf:
    __slots__ = ("w", "r")

    def __init__(self):
        self.w = None
        self.r = {}


class Sched:
    ENG = ("pe", "act", "dve", "pool", "sp")

    def __init__(self, nc):
        self.nc = nc
        self.q = {e: [] for e in self.ENG}
        self.cnt = {}
        self.waited = {}
        self.dnext = {}

    def _deps(self, reads, writes):
        deps = {}

        def add(tok):
            if tok is not None and deps.get(tok[0], 0) < tok[1]:
                deps[tok[0]] = tok[1]

        for b in reads:
            add(b.w)
        for b in writes:
            add(b.w)
            for s, v in b.r.items():
                add((s, v))
        return deps

    def _emit_waits(self, q, deps, own):
        for s, v in deps.items():
            if s == own and q == "pe":
                continue
            key = (q, s)
            if self.waited.get(key, 0) >= v:
                continue
            self.waited[key] = v
            self.q[q].append(("wait", s, v))

    def _update(self, tok, reads, writes):
        s, v = tok
        for b in reads:
            if b.r.get(s, 0) < v:
                b.r[s] = v
        for b in writes:
            b.w = tok
            b.r = {}

    def op(self, q, fn, reads=(), writes=()):
        own = "c_" + q
        self._emit_waits(q, self._deps(reads, writes), own)
        v = self.cnt.get(own, 0) + 1
        self.cnt[own] = v
        self.q[q].append(("op", fn, own, 1))
        tok = (own, v)
        self._update(tok, reads, writes)
        return tok

    NDSEM = 12

    def dma(self, q, fn, reads=(), writes=(), sem=None):
        cls = sem or q
        i = self.dnext.get(cls, 0)
        self.dnext[cls] = (i + 1) % self.NDSEM
        s = "d_%s_%d" % (cls, i)
        prev = self.cnt.get(s, 0)
        if prev:
            self._emit_waits(q, {s: prev}, None)
        self._emit_waits(q, self._deps(reads, writes), None)
        v = prev + 16
        self.cnt[s] = v
        self.q[q].append(("op", fn, s, 16))
        tok = (s, v)
        self._update(tok, reads, writes)
        return tok

    def coll(self, q, fn, reads=(), writes=()):
        s = "cc"
        self._emit_waits(q, self._deps(reads, writes), None)
        v = self.cnt.get(s, 0) + 1
        self.cnt[s] = v
        self.q[q].append(("op", fn, s, 1))
        tok = (s, v)
        self._update(tok, reads, writes)
        return tok

    def barrier(self):
        for q in self.ENG:
            self._emit_waits(q, dict(self.cnt), None)

    def emit(self, stack):
        nc = self.nc
        sems = {s: stack.enter_context(nc.semaphore(s)) for s in self.cnt}
        block = stack.enter_context(nc.Block())

        def run(q):
            def f(eng):
                for it in self.q[q]:
                    if it[0] == "wait":
                        eng.wait_ge(sems[it[1]], it[2])
                    else:
                        it[1](eng).then_inc(sems[it[2]], it[3])
            return f

        block.tensor(run("pe"))
        block.scalar(run("act"))
        block.vector(run("dve"))
        block.gpsimd(run("pool"))
        block.sync(run("sp"))


def MM(out, lhsT, rhs, start, stop):
    return lambda e: e.matmul(out, lhsT, rhs, start=start, stop=stop)


def TR(out, in_, ident):
    return lambda e: e.transpose(out, in_, ident)


def ACT(out, in_, func, **kw):
    return lambda e: e.activation(out=out, in_=in_, func=func, **kw)


def TT(out, in0, in1, op):
    return lambda e: e.tensor_tensor(out=out, in0=in0, in1=in1, op=op)


def TS(out, in0, s1, s2, op0, op1=None, **kw):
    if op1 is None:
        return lambda e: e.tensor_scalar(out, in0, s1, s2, op0, **kw)
    return lambda e: e.tensor_scalar(out, in0, s1, s2, op0, op1, **kw)


def STT(out, in0, scalar, in1, op0, op1):
    return lambda e: e.scalar_tensor_tensor(out=out, in0=in0, scalar=scalar, in1=in1, op0=op0, op1=op1)


def CP(out, in_):
    return lambda e: e.tensor_copy(out=out, in_=in_)


def RECIP(out, in_):
    return lambda e: e.reciprocal(out=out, in_=in_)


def MSET(ap, v):
    return lambda e: e.memset(ap, v)


def DMA(out, in_, **kw):
    return lambda e: e.dma_start(out=out, in_=in_, **kw)


def win_slot(w):
    if w < 2:
        return 16 + w
    if w < 18:
        return w - 2
    return w


def key_slot(kt):
    return kt if kt < 16 else kt + 4


def na_wlist(jl):
    if jl == 0:
        return list(range(0, 6))
    if jl == 15:
        return list(range(14, 20))
    return list(range(jl, jl + 5))


def na_class(jl):
    return {0: 1, 1: 2, 14: 3, 15: 4}.get(jl, 0)


def build_program(debug=False):
    nc = bass.Bass("TRN2", target_bir_lowering=False)

    def din(name, shape, dt=F32):
        return nc.dram_tensor(name, list(shape), dt, kind="ExternalInput").ap()

    def dout(name, shape, dt=F32):
        return nc.dram_tensor(name, list(shape), dt, kind="ExternalOutput").ap()

    def dint(name, shape, dt=F32):
        return nc.dram_tensor(name, list(shape), dt, kind="Internal").ap()

    xT = din("xT", [128, 8, NTOK])
    x_own = din("x_own", [2048, D])
    memT = din("memT", [128, 8, 256])
    wA = din("wA", [128, 8, 1280])
    wq = din("wq", [12, 128, 8, 128])
    wg = din("wg", [24, 128, 8, 128])
    wbr = din("wbr", [8, 128, 12, 128])
    wo = din("wo", [2, 128, 8, 512])
    wm = din("wm", [128, 8, 1024])
    bm = din("bm", [5, 128, 8 * 6 * 128])
    cosT = din("cosT", [128, NTOK])
    sinT = din("sinT", [128, NTOK])
    cmat = din("cmat", [128, 5 * 128])
    gains = din("gains", [128, 2])
    bgate = din("bgate", [128, 24])
    ln1 = din("ln1", [128, 2 * D])
    x1_f = dout("x1_f", [2048, D]) if debug else dint("x1_f", [2048, D])
    identf = din("identf", [128, 128])
    iota = din("iota", [128, 512])
    wr = din("wr", [128, 8, 16])
    slotn = din("slotn", [128, 4])
    ln2 = din("ln2", [128, 2 * D])
    wgu = din("wgu", [16, 128, 8, 2048])
    wd = din("wd", [16, 128, 8, 1024])
    out_d = dout("out", [2048, D])
    x1g = dint("x1g", [2560, 1056], BF16)
    ffn = dint("ffn", [2560, D])
    affbuf = dint("affbuf", [2048, 16])
    affall = dint("affall", [4096, 16])
    if debug:
        aff_dbg = dout("aff_dbg", [128, 16 * 16])
        thr_dbg = dout("thr_dbg", [128, 16])
        idx_dbg = dout("idx_dbg", [16, 128, 4], I32)
    if debug:
        y_dbg = dout("y_dbg", [2048, 1536], BF16)
        mg_dbg = dout("mg_dbg", [128, 8, 2048], BF16)
        qg_dbg = dout("qg_dbg", [128, 8, 512], BF16)
        qna_dbg = dout("qna_dbg", [128, 8, 512], BF16)
        kna_dbg = dout("kna_dbg", [128, 4, 2560], BF16)
        kg_dbg = dout("kg_dbg", [128, 4096], BF16)
        pt_dbg = dout("pt_dbg", [128, 768], BF16)
        vna_dbg = dout("vna_dbg", [128, 20, 520], BF16)

    S = Sched(nc)
    with ExitStack() as st:
        SBN = 94000
        sb_all = st.enter_context(nc.sbuf_tensor("sb_all", [128, SBN], BF16))
        PS = [st.enter_context(nc.psum_tensor("ps%d" % i, [128, 1024], F32)) for i in range(4)]
        bank_buf = [Buf() for _ in range(8)]

        class Alloc:
            def __init__(self):
                self.off = 0

            def get(self, shape, dt):
                n = int(np.prod(shape[1:]))
                e16 = n * 2 if dt in (F32, I32) else n
                e16 = (e16 + 1) // 2 * 2
                assert self.off + e16 <= SBN, ("SBUF overflow", self.off, e16)
                ap = sb_all[:, self.off:self.off + e16]
                self.off += e16
                if dt != BF16:
                    ap = ap.bitcast(dt)
                if len(shape) == 3:
                    ap = ap.rearrange("p (a b) -> p a b", a=shape[1])
                elif len(shape) == 4:
                    ap = ap.rearrange("p (a b c) -> p a b c", a=shape[1], b=shape[2])
                return ap

        A = Alloc()

        def bank(i):
            return PS[i // 2][:, (i % 2) * 512:(i % 2) * 512 + 512]

        rot_state = {"s": 0, "p": 0, "a": 0}

        def ps_single():
            i = rot_state["s"]
            rot_state["s"] = (i + 1) % 6
            return bank(i), [bank_buf[i]]

        def ps_pair():
            i = rot_state["p"]
            rot_state["p"] = (i + 1) % 3
            return PS[i][:, :], [bank_buf[2 * i], bank_buf[2 * i + 1]]

        def ps_acc():
            i = 6 + rot_state["a"]
            rot_state["a"] = 1 - rot_state["a"]
            return bank(i), [bank_buf[i]]

        cm = A.get([128, 640], BF16)
        ident, bones, rotm = cm[:, 0:128], cm[:, 128:256], cm[:, 256:384]
        ones_bf, tri_bf = cm[:, 384:512], cm[:, 512:640]
        idf = A.get([128, 128], F32)
        aff_own = A.get([128, 16, 16], F32)
        b_aff = Buf()
        gn = A.get([128, 2], F32)
        bg = A.get([128, 24], F32)
        eps_t = A.get([128, 2], F32)
        base_off0 = A.off
        KnaT = A.get([128, 4, 20 * 128], BF16)
        Vna = A.get([128, 20, 8 * 65], BF16)
        KgT = A.get([128, 4096], BF16)
        Vg = A.get([128, 32, 2 * 65], BF16)
        mkT = A.get([128, 4, 256], BF16)
        mv = A.get([128, 2, 4 * 129], BF16)
        yT = A.get([128, 12, 512], BF16)
        xTb = [A.get([128, 8, 512], BF16) for _ in range(2)]
        b_const, b_gn, b_bg = Buf(), Buf(), Buf()
        b_Kna = [Buf() for _ in range(20)]
        b_Vna = [Buf() for _ in range(20)]
        b_Kg = [Buf() for _ in range(32)]
        b_Vg = [Buf() for _ in range(32)]
        b_mk, b_mv, b_yT = Buf(), Buf(), Buf()
        b_xTb = [Buf(), Buf()]
        base_off = A.off

        S.dma("pool", DMA(cm, cmat), writes=[b_const])
        S.dma("sp", DMA(idf, identf), writes=[b_const])
        S.dma("sp", DMA(gn, gains), writes=[b_gn])
        S.dma("sp", DMA(bg, bgate), writes=[b_bg])
        S.op("pool", MSET(eps_t[:, 0:1], RMS_EPS), writes=[b_const])
        S.op("pool", MSET(eps_t[:, 1:2], LN_EPS), writes=[b_const])
        b_ones = Buf()
        S.op("pool", MSET(Vna, 1.0), writes=b_Vna)
        S.op("pool", MSET(Vg, 1.0), writes=b_Vg)
        S.op("pool", MSET(mv, 1.0), writes=[b_mv])

        def load_xT(blk, idx):
            buf = xTb[idx]
            S.dma("pool", DMA(buf, xT[:, :, blk * 512:(blk + 1) * 512]), writes=[b_xTb[idx]], sem="x")

        def rms_rope(zp, zp_b, gcol, tok0, tmp, dests):
            sq, rstd, qn, t1, t2, cs, sn, b_t = tmp
            S.dma("sp", DMA(cs, cosT[:, tok0:tok0 + 512]), writes=[b_t[5]])
            S.dma("sp", DMA(sn, sinT[:, tok0:tok0 + 512]), writes=[b_t[6]])
            S.op("act", ACT(sq, zp, AF.Square), reads=zp_b, writes=[b_t[0]])
            ssp, ssb = ps_single()
            S.op("pe", MM(ssp, bones, sq, True, True), reads=[b_t[0], b_const], writes=ssb)
            S.op("act", ACT(rstd, ssp, AF.Sqrt, scale=1.0 / 64.0, bias=eps_t[:, 0:1]), reads=ssb + [b_const], writes=[b_t[1]])
            S.op("dve", RECIP(rstd, rstd), reads=[b_t[1]], writes=[b_t[1]])
            S.op("dve", STT(qn, zp, gn[:, gcol:gcol + 1], rstd, ALU.mult, ALU.mult),
                 reads=zp_b + [b_t[1], b_gn], writes=[b_t[2]])
            rqp, rqb = ps_single()
            S.op("pe", MM(rqp, rotm, qn, True, True), reads=[b_t[2], b_const], writes=rqb)
            S.op("dve", TT(t1, qn, cs, ALU.mult), reads=[b_t[2], b_t[5]], writes=[b_t[3]])
            S.op("dve", TT(t2, rqp, sn, ALU.mult), reads=rqb + [b_t[6]], writes=[b_t[4]])
            for (dst, lo, hi, dbufs) in dests:
                S.op("dve", TT(dst, t1[lo:hi, :], t2[lo:hi, :], ALU.add), reads=[b_t[3], b_t[4]], writes=dbufs)

        A.off = base_off
        wA_sb = A.get([128, 8, 1280], BF16)
        wm_sb = A.get([128, 8, 1024], BF16)
        memT_sb = A.get([128, 8, 256], BF16)
        tmpA = (A.get([128, 512], BF16), A.get([128, 512], F32), A.get([128, 512], BF16),
                A.get([128, 512], F32), A.get([128, 512], F32), A.get([128, 512], F32),
                A.get([128, 512], F32), [Buf() for _ in range(7)])
        b_wA, b_wm, b_memT = Buf(), Buf(), Buf()
        S.dma("pool", DMA(wA_sb, wA), writes=[b_wA], sem="w")
        S.dma("pool", DMA(memT_sb, memT), writes=[b_memT], sem="w")
        S.dma("pool", DMA(wm_sb, wm), writes=[b_wm], sem="w")
        load_xT(0, 0)
        for blk in range(9):
            xi = blk % 2
            if blk + 1 < 9:
                load_xT(blk + 1, 1 - xi)
            xb, xbb = xTb[xi], [b_xTb[xi]]
            slots = [blk * 4 + i for i in range(4)]
            in_win = blk < 5
            in_key = blk != 4
            if in_win:
                for p in range(4):
                    pp, pb = ps_single()
                    for k in range(8):
                        S.op("pe", MM(pp, wA_sb[:, k, p * 128:(p + 1) * 128], xb[:, k, :], k == 0, k == 7),
                             reads=xbb + [b_wA], writes=pb)
                    S.op("act", ACT(KnaT[:, p, blk * 512:(blk + 1) * 512], pp, AF.Copy), reads=pb,
                         writes=[b_Kna[s] for s in slots])
                for i, s in enumerate(slots):
                    pp, pb = ps_single()
                    for k in range(8):
                        S.op("pe", MM(pp, xb[:, k, i * 128:(i + 1) * 128], wA_sb[:, k, 512:1024], k == 0, k == 7),
                             reads=xbb + [b_wA], writes=pb)
                    dst = Vna[:, s, :].rearrange("p (h e) -> p h e", h=8)[:, :, 0:64]
                    S.op("dve", CP(dst, pp.rearrange("p (h e) -> p h e", h=8)), reads=pb, writes=[b_Vna[s]])
            if in_key:
                kt0 = slots[0] if blk < 4 else slots[0] - 4
                pp, pb = ps_single()
                for k in range(8):
                    S.op("pe", MM(pp, wA_sb[:, k, 1024:1152], xb[:, k, :], k == 0, k == 7),
                         reads=xbb + [b_wA], writes=pb)
                rms_rope(pp, pb, 1, blk * 512, tmpA,
                         [(KgT[:, kt0 * 128:kt0 * 128 + 512], 0, 128, [b_Kg[kt0 + i] for i in range(4)])])
                for i in range(4):
                    pp, pb = ps_single()
                    for k in range(8):
                        S.op("pe", MM(pp[:, 0:128], xb[:, k, i * 128:(i + 1) * 128], wA_sb[:, k, 1152:1280],
                                      k == 0, k == 7), reads=xbb + [b_wA], writes=pb)
                    dst = Vg[:, kt0 + i, :].rearrange("p (h e) -> p h e", h=2)[:, :, 0:64]
                    S.op("dve", CP(dst, pp[:, 0:128].rearrange("p (h e) -> p h e", h=2)), reads=pb,
                         writes=[b_Vg[kt0 + i]])
        for h in range(4):
            pp, pb = ps_single()
            for k in range(8):
                S.op("pe", MM(pp[:, 0:256], wm_sb[:, k, h * 128:(h + 1) * 128], memT_sb[:, k, :], k == 0, k == 7),
                     reads=[b_wm, b_memT], writes=pb)
            S.op("act", ACT(mkT[:, h, :], pp[:, 0:256], AF.Copy), reads=pb, writes=[b_mk])
        for mt in range(2):
            pp, pb = ps_single()
            for k in range(8):
                S.op("pe", MM(pp, memT_sb[:, k, mt * 128:(mt + 1) * 128], wm_sb[:, k, 512:1024], k == 0, k == 7),
                     reads=[b_wm, b_memT], writes=pb)
            dst = mv[:, mt, :].rearrange("p (h e) -> p h e", h=4)[:, :, 0:128]
            S.op("dve", CP(dst, pp.rearrange("p (h e) -> p h e", h=4)), reads=pb, writes=[b_mv])
        S.barrier()

        A.off = base_off
        offB = A.off
        Qna = A.get([128, 8, 512], BF16)
        Qg = A.get([128, 8, 512], BF16)
        Qm = A.get([128, 4, 512], BF16)
        tmpB = (A.get([128, 512], BF16), A.get([128, 512], F32), A.get([128, 512], BF16),
                A.get([128, 512], F32), A.get([128, 512], F32), A.get([128, 512], F32),
                A.get([128, 512], F32), [Buf() for _ in range(7)])
        qtmp = A.get([128, 512], BF16)
        wq_sb = [A.get([128, 8, 128], BF16) for _ in range(3)]
        bm_int = A.get([128, 8, 6, 128], BF16)
        bm_brd = A.get([128, 8, 6, 128], BF16)
        PTn = [A.get([128, 768], BF16) for _ in range(2)]
        PTg = [A.get([128, 512], BF16) for _ in range(3)]
        y_sb = [A.get([128, 1536], BF16) for _ in range(2)]
        rc = [A.get([128, 8], F32) for _ in range(2)]
        endB1 = A.off
        A.off = offB
        wg_sb = [A.get([128, 8, 128], BF16) for _ in range(4)]
        wbr_sb = [A.get([128, 12, 128], BF16) for _ in range(2)]
        wo_sb = [A.get([128, 8, 512], BF16) for _ in range(2)]
        gate_sb = [A.get([128, 512], BF16) for _ in range(3)]
        mtmp = [A.get([128, 512], F32) for _ in range(3)]
        mergedT = A.get([128, 8, 512], BF16)
        tbuf = [A.get([128, D], F32) for _ in range(4)]
        xh = [A.get([128, 512], F32) for _ in range(2)]
        ln_sb = A.get([128, 2 * D], F32)
        stat = [A.get([128, 8], F32) for _ in range(2)]
        junk = A.get([128, D], BF16)
        x1T_sb = A.get([128, D], F32)
        rowt = [A.get([128, 1056], BF16) for _ in range(2)]
        wr_sb = A.get([128, 8, 16], F32)
        rt = [A.get([128, 40], F32) for _ in range(2)]
        endB2 = A.off
        A.off = max(endB1, endB2)

        b_Qna, b_Qg, b_Qm, b_qtmp = Buf(), Buf(), Buf(), Buf()
        b_wq = [Buf() for _ in range(3)]
        b_bmi, b_bmb = Buf(), Buf()
        b_PTn = [Buf(), Buf()]
        b_PTg = [Buf() for _ in range(3)]
        b_y = [Buf(), Buf()]
        b_rc = [Buf(), Buf()]
        b_wg = [Buf() for _ in range(4)]
        b_wbr = [Buf(), Buf()]
        b_wo = [Buf(), Buf()]
        b_gate = [Buf() for _ in range(3)]
        b_mtmp = [Buf() for _ in range(3)]
        b_merged = Buf()
        b_t = [Buf() for _ in range(4)]
        b_xh = [Buf(), Buf()]
        b_ln = Buf()
        b_stat = [Buf(), Buf()]
        b_junk = Buf()
        b_x1T, b_wr = Buf(), Buf()
        b_rowt = [Buf(), Buf()]
        b_rt = [Buf(), Buf()]
        b_x1g, b_affbuf, b_ffn = Buf(), Buf(), Buf()
        b_x1f = Buf()
        b_dbg = Buf()

        load_xT(0, 0)
        cnt = {"wq": 0, "wg": 0, "wbr": 0, "wo": 0, "ptn": 0, "ptg": 0, "y": 0, "xh": 0}
        for blk in range(4):
            xi = blk % 2
            xb, xbb = xTb[xi], [b_xTb[xi]]
            S.op("dve", MSET(Qna, 0.0), writes=[b_Qna])
            S.op("dve", MSET(Qg, 0.0), writes=[b_Qg])
            S.dma("pool", DMA(bm_int.rearrange("p a b c -> p (a b c)"), bm[0]), writes=[b_bmi], sem="w")
            for c in range(12):
                wi = cnt["wq"] % 3
                cnt["wq"] += 1
                S.dma("pool", DMA(wq_sb[wi], wq[c]), writes=[b_wq[wi]], sem="w")
                pp, pb = ps_single()
                for k in range(8):
                    S.op("pe", MM(pp, wq_sb[wi][:, k, :], xb[:, k, :], k == 0, k == 7),
                         reads=xbb + [b_wq[wi]], writes=pb)
                if c < 4:
                    S.op("act", ACT(Qna[0:64, 2 * c, :], pp[0:64, :], AF.Copy, scale=0.125), reads=pb, writes=[b_Qna])
                    S.op("act", ACT(Qna[64:128, 2 * c + 1, :], pp[64:128, :], AF.Copy, scale=0.125), reads=pb,
                         writes=[b_Qna])
                elif c < 8:
                    i = c - 4
                    rms_rope(pp, pb, 0, blk * 512, tmpB,
                             [(Qg[0:64, i, :], 0, 64, [b_Qg]), (Qg[64:128, 4 + i, :], 64, 128, [b_Qg])])
                else:
                    S.op("act", ACT(Qm[:, c - 8, :], pp, AF.Copy), reads=pb, writes=[b_Qm])
            if debug and blk == 0:
                S.dma("sp", DMA(qg_dbg, Qg), reads=[b_Qg], writes=[b_dbg])
                S.dma("sp", DMA(qna_dbg, Qna), reads=[b_Qna], writes=[b_dbg])
                S.dma("sp", DMA(kna_dbg, KnaT), reads=b_Kna, writes=[b_dbg])
                S.dma("sp", DMA(kg_dbg, KgT), reads=b_Kg, writes=[b_dbg])
                S.dma("sp", DMA(vna_dbg, Vna), reads=b_Vna, writes=[b_dbg])
            for qt in range(4):
                jl = blk * 4 + qt
                yi = cnt["y"] % 2
                cnt["y"] += 1
                ysb, ybb = y_sb[yi], [b_y[yi]]
                qs = slice(qt * 128, (qt + 1) * 128)
                cls = na_class(jl)
                if cls == 0:
                    bmt, bmb = bm_int, [b_bmi]
                else:
                    S.dma("pool", DMA(bm_brd.rearrange("p a b c -> p (a b c)"), bm[cls]), writes=[b_bmb], sem="w")
                    bmt, bmb = bm_brd, [b_bmb]
                wl = na_wlist(jl)
                nk = len(wl)
                accs = [ps_acc(), ps_acc()]
                def na_scores(h):
                    sp, spb = ps_pair()
                    for ki, w in enumerate(wl):
                        s_ = win_slot(w)
                        S.op("pe", MM(sp[:, ki * 128:(ki + 1) * 128], KnaT[:, h // 2, s_ * 128:(s_ + 1) * 128],
                                      Qna[:, h, qs], ki % 4 == 0, False), reads=[b_Kna[s_], b_Qna], writes=spb)
                    S.op("pe", MM(sp[:, 0:512], ident, bmt[:, h, 0:4, :], False, True), reads=bmb + [b_const],
                         writes=spb)
                    S.op("pe", MM(sp[:, 512:nk * 128], ident, bmt[:, h, 4:nk, :], False, True),
                         reads=bmb + [b_const], writes=spb)
                    return sp, spb

                nxt = na_scores(0)
                for h in range(8):
                    sp, spb = nxt
                    if h + 1 < 8:
                        nxt = na_scores(h + 1)
                    pi = cnt["ptn"] % 2
                    cnt["ptn"] += 1
                    S.op("act", ACT(PTn[pi][:, 0:nk * 128], sp[:, 0:nk * 128], AF.Exp), reads=spb,
                         writes=[b_PTn[pi]])
                    ap_, ab_ = accs[h // 4]
                    hh = h % 4
                    for ki, w in enumerate(wl):
                        s_ = win_slot(w)
                        S.op("pe", MM(ap_[:, hh * 65:hh * 65 + 65], PTn[pi][:, ki * 128:(ki + 1) * 128],
                                      Vna[:, s_, h * 65:h * 65 + 65], ki == 0 and hh == 0, ki == nk - 1),
                             reads=[b_PTn[pi], b_Vna[s_]], writes=ab_)
                for half in range(2):
                    ap_, ab_ = accs[half]
                    a3 = ap_[:, 0:260].rearrange("p (h e) -> p h e", h=4)
                    ri = half
                    S.op("dve", RECIP(rc[ri][:, 0:4], a3[:, :, 64]), reads=ab_, writes=[b_rc[ri]])
                    S.op("dve", TT(ysb[:, half * 256:(half + 1) * 256].rearrange("p (h e) -> p h e", h=4),
                                   a3[:, :, 0:64], rc[ri][:, 0:4].unsqueeze(2).to_broadcast([128, 4, 64]), ALU.mult),
                         reads=ab_ + [b_rc[ri]], writes=ybb)
                for g in range(2):
                    ap_, ab_ = ps_acc()

                    def gqa_scores(kt, g=g):
                        sp, spb = ps_single()
                        S.op("pe", MM(sp, KgT[:, kt * 128:(kt + 1) * 128], Qg[:, 4 * g:4 * g + 4, qs], True, True),
                             reads=[b_Kg[kt], b_Qg], writes=spb)
                        return sp, spb

                    pend = [gqa_scores(0), gqa_scores(1)]
                    for kt in range(32):
                        sp, spb = pend.pop(0)
                        if kt + 2 < 32:
                            pend.append(gqa_scores(kt + 2))
                        pi = cnt["ptg"] % 3
                        cnt["ptg"] += 1
                        S.op("act", ACT(PTg[pi], sp, AF.Exp, scale=0.125), reads=spb, writes=[b_PTg[pi]])
                        for hh in range(4):
                            S.op("pe", MM(ap_[:, hh * 65:hh * 65 + 65], PTg[pi][:, hh * 128:(hh + 1) * 128],
                                          Vg[:, kt, g * 65:g * 65 + 65], kt == 0 and hh == 0, kt == 31),
                                 reads=[b_PTg[pi], b_Vg[kt]], writes=ab_)
                    a3 = ap_[:, 0:260].rearrange("p (h e) -> p h e", h=4)
                    S.op("dve", RECIP(rc[g][:, 4:8], a3[:, :, 64]), reads=ab_, writes=[b_rc[g]])
                    S.op("dve", TT(ysb[:, 512 + g * 256:512 + (g + 1) * 256].rearrange("p (h e) -> p h e", h=4),
                                   a3[:, :, 0:64], rc[g][:, 4:8].unsqueeze(2).to_broadcast([128, 4, 64]), ALU.mult),
                         reads=ab_ + [b_rc[g]], writes=ybb)
                for hp in range(2):
                    ap_, ab_ = ps_acc()

                    def mem_scores(i, hp=hp):
                        h_, mt_ = hp * 2 + i // 2, i % 2
                        sp, spb = ps_single()
                        S.op("pe", MM(sp[:, 0:128], mkT[:, h_, mt_ * 128:(mt_ + 1) * 128], Qm[:, h_, qs], True, True),
                             reads=[b_mk, b_Qm], writes=spb)
                        return sp, spb

                    pend = [mem_scores(0), mem_scores(1)]
                    for i in range(4):
                        hq, mt = i // 2, i % 2
                        h = hp * 2 + hq
                        sp, spb = pend.pop(0)
                        if i + 2 < 4:
                            pend.append(mem_scores(i + 2))
                        pi = cnt["ptg"] % 3
                        cnt["ptg"] += 1
                        S.op("act", ACT(PTg[pi][:, 0:128], sp[:, 0:128], AF.Exp, scale=1.0 / math.sqrt(128.0)),
                             reads=spb, writes=[b_PTg[pi]])
                        S.op("pe", MM(ap_[:, hq * 129:hq * 129 + 129], PTg[pi][:, 0:128],
                                      mv[:, mt, h * 129:h * 129 + 129], mt == 0 and hq == 0, mt == 1),
                             reads=[b_PTg[pi], b_mv], writes=ab_)
                    a3 = ap_[:, 0:258].rearrange("p (h e) -> p h e", h=2)
                    S.op("dve", RECIP(rc[hp][:, 0:2], a3[:, :, 128]), reads=ab_, writes=[b_rc[hp]])
                    S.op("dve", TT(ysb[:, 1024 + hp * 256:1024 + (hp + 1) * 256].rearrange("p (h e) -> p h e", h=2),
                                   a3[:, :, 0:128], rc[hp][:, 0:2].unsqueeze(2).to_broadcast([128, 2, 128]),
                                   ALU.mult), reads=ab_ + [b_rc[hp]], writes=ybb)
                if debug:
                    S.dma("sp", DMA(y_dbg[jl * 128:(jl + 1) * 128, :], ysb), reads=ybb, writes=[b_dbg])
                for grp in range(3):
                    tp, tpb = ps_single()
                    tpv = tp.bitcast(BF16)
                    for c in range(4):
                        cc = grp * 4 + c
                        S.op("pe", TR(tpv[:, c * 128:(c + 1) * 128], ysb[:, cc * 128:(cc + 1) * 128], ident),
                             reads=ybb + [b_const], writes=tpb)
                    S.op("dve", CP(yT[:, grp * 4:grp * 4 + 4, qs],
                                   tpv[:, 0:512].rearrange("p (c t) -> p c t", c=4)), reads=tpb, writes=[b_yT])
            S.barrier()
            if blk == 0:
                pass
            S.dma("sp", DMA(ln_sb, ln1), writes=[b_ln])
            S.dma("sp", DMA(wr_sb, wr), writes=[b_wr])
            for dc in range(8):
                bi = cnt["wbr"] % 2
                cnt["wbr"] += 1
                S.dma("pool", DMA(wbr_sb[bi], wbr[dc]), writes=[b_wbr[bi]], sem="w")
                for g in range(3):
                    wi = cnt["wg"] % 4
                    cnt["wg"] += 1
                    S.dma("pool", DMA(wg_sb[wi], wg[g * 8 + dc]), writes=[b_wg[wi]], sem="w")
                    pp, pb = ps_single()
                    for k in range(8):
                        S.op("pe", MM(pp, wg_sb[wi][:, k, :], xb[:, k, :], k == 0, k == 7),
                             reads=xbb + [b_wg[wi]], writes=pb)
                    S.op("act", ACT(gate_sb[g], pp, AF.Sigmoid, bias=bg[:, g * 8 + dc:g * 8 + dc + 1]),
                         reads=pb + [b_bg], writes=[b_gate[g]])
                for g in range(3):
                    pp, pb = ps_single()
                    for c in range(4):
                        S.op("pe", MM(pp, wbr_sb[bi][:, g * 4 + c, :], yT[:, g * 4 + c, :], c == 0, c == 3),
                             reads=[b_wbr[bi], b_yT], writes=pb)
                    S.op("dve", TT(mtmp[g], pp, gate_sb[g], ALU.mult), reads=pb + [b_gate[g]], writes=[b_mtmp[g]])
                S.op("dve", TT(mtmp[0], mtmp[0], mtmp[1], ALU.add), reads=[b_mtmp[0], b_mtmp[1]],
                     writes=[b_mtmp[0]])
                S.op("dve", TT(mergedT[:, dc, :], mtmp[0], mtmp[2], ALU.add), reads=[b_mtmp[0], b_mtmp[2]],
                     writes=[b_merged])
            if debug:
                S.dma("sp", DMA(mg_dbg[:, :, blk * 512:(blk + 1) * 512], mergedT), reads=[b_merged], writes=[b_dbg])
            if blk + 1 < 4:
                load_xT(blk + 1, 1 - xi)
            for half in range(2):
                oi = cnt["wo"] % 2
                cnt["wo"] += 1
                S.dma("pool", DMA(wo_sb[oi], wo[half]), writes=[b_wo[oi]], sem="w")
                for qt in range(4):
                    jl = blk * 4 + qt
                    hi_ = cnt["xh"] % 2
                    cnt["xh"] += 1
                    S.dma("sp", DMA(xh[hi_], x_own[jl * 128:(jl + 1) * 128, half * 512:(half + 1) * 512]),
                          writes=[b_xh[hi_]])
                    pp, pb = ps_single()
                    for k in range(8):
                        S.op("pe", MM(pp, mergedT[:, k, qt * 128:(qt + 1) * 128], wo_sb[oi][:, k, :], k == 0, k == 7),
                             reads=[b_merged, b_wo[oi]], writes=pb)
                    S.op("dve", STT(tbuf[qt][:, half * 512:(half + 1) * 512], xh[hi_], ALPHA, pp, ALU.mult, ALU.add),
                         reads=pb + [b_xh[hi_]], writes=[b_t[qt]])
            for qt in range(4):
                jl = blk * 4 + qt
                si = qt % 2
                t = tbuf[qt]
                st_ = stat[si]
                S.op("act", ACT(junk, t, AF.Copy, accum_out=st_[:, 0:1]), reads=[b_t[qt]], writes=[b_junk, b_stat[si]])
                S.op("act", ACT(junk, t, AF.Square, accum_out=st_[:, 1:2]), reads=[b_t[qt]],
                     writes=[b_junk, b_stat[si]])
                S.op("dve", TS(st_[:, 2:3], st_[:, 0:1], 1.0 / D, None, ALU.mult), reads=[b_stat[si]],
                     writes=[b_stat[si]])
                S.op("dve", TT(st_[:, 3:4], st_[:, 2:3], st_[:, 2:3], ALU.mult), reads=[b_stat[si]],
                     writes=[b_stat[si]])
                S.op("dve", STT(st_[:, 4:5], st_[:, 1:2], 1.0 / D, st_[:, 3:4], ALU.mult, ALU.subtract),
                     reads=[b_stat[si]], writes=[b_stat[si]])
                S.op("act", ACT(st_[:, 5:6], st_[:, 4:5], AF.Sqrt, bias=eps_t[:, 1:2]), reads=[b_stat[si], b_const],
                     writes=[b_stat[si]])
                S.op("dve", RECIP(st_[:, 5:6], st_[:, 5:6]), reads=[b_stat[si]], writes=[b_stat[si]])
                S.op("dve", TS(t, t, st_[:, 2:3], st_[:, 5:6], ALU.subtract, ALU.mult), reads=[b_t[qt], b_stat[si]],
                     writes=[b_t[qt]])
                S.op("dve", TT(t, t, ln_sb[:, 0:D], ALU.mult), reads=[b_t[qt], b_ln], writes=[b_t[qt]])
                S.op("dve", TT(t, t, ln_sb[:, D:2 * D], ALU.add), reads=[b_t[qt], b_ln], writes=[b_t[qt]])
                S.dma("sp", DMA(x1_f[jl * 128:(jl + 1) * 128, :], t), reads=[b_t[qt]], writes=[b_x1f])
                tp2, tpb2 = ps_pair()
                for k in range(8):
                    S.op("pe", TR(tp2[:, k * 128:(k + 1) * 128], t[:, k * 128:(k + 1) * 128], idf),
                         reads=[b_t[qt], b_const], writes=tpb2)
                S.op("act", ACT(x1T_sb, tp2, AF.Copy), reads=tpb2, writes=[b_x1T])
                lp, lpb = ps_single()
                for k in range(8):
                    S.op("pe", MM(lp[:, 0:16], x1T_sb[:, k * 128:(k + 1) * 128], wr_sb[:, k, :], k == 0, k == 7),
                         reads=[b_x1T, b_wr], writes=lpb)
                ri = qt % 2
                r_ = rt[ri]
                S.op("dve", lambda e, o=r_[:, 0:1], i=lp[:, 0:16]: e.reduce_max(out=o, in_=i, axis=AX.X),
                     reads=lpb, writes=[b_rt[ri]])
                S.op("dve", TS(r_[:, 1:2], r_[:, 0:1], -1.0, None, ALU.mult), reads=[b_rt[ri]], writes=[b_rt[ri]])
                S.op("act", ACT(r_[:, 8:24], lp[:, 0:16], AF.Exp, bias=r_[:, 1:2], accum_out=r_[:, 2:3]),
                     reads=lpb + [b_rt[ri]], writes=[b_rt[ri]])
                S.op("dve", RECIP(r_[:, 3:4], r_[:, 2:3]), reads=[b_rt[ri]], writes=[b_rt[ri]])
                S.op("dve", TS(aff_own[:, jl, :], r_[:, 8:24], r_[:, 3:4], None, ALU.mult), reads=[b_rt[ri]],
                     writes=[b_aff])
                rw = rowt[ri]
                S.op("act", ACT(rw[:, 0:1024], t, AF.Copy), reads=[b_t[qt]], writes=[b_rowt[ri]])
                S.op("dve", CP(rw[:, 1024:1056].bitcast(F32), aff_own[:, jl, :]), reads=[b_aff], writes=[b_rowt[ri]])
                S.dma("sp", DMA(x1g[jl * 128:(jl + 1) * 128, :], rw), reads=[b_rowt[ri]], writes=[b_x1g])
                S.dma("sp", DMA(affbuf[jl * 128:(jl + 1) * 128, :], aff_own[:, jl, :]), reads=[b_aff],
                      writes=[b_affbuf])
            S.barrier()

        S.barrier()
        A.off = base_off0
        affall_sb = A.get([128, 32, 16], F32)
        cmp_sb = A.get([128, 512], BF16)
        lo = A.get([128, 16], F32)
        tr_ = A.get([128, 16], F32)
        cntt = A.get([128, 16], F32)
        ge = A.get([128, 16], F32)
        M_sb = A.get([128, 16, 16], BF16)
        c_incl = A.get([128, 16, 16], F32)
        ex = A.get([128, 16, 16], F32)
        iota_sb = A.get([128, 512], F32)
        zrow = A.get([128, 1056], BF16)
        cmpS = [A.get([128, 512], BF16) for _ in range(4)]
        idx_i = [A.get([128, 4], I32) for _ in range(3)]
        idx_f = [A.get([128, 8], F32) for _ in range(3)]
        slot_sb = A.get([128, 4], F32)
        Wgu_sb = [A.get([128, 8, 2048], BF16) for _ in range(2)]
        Wd_sb = [A.get([128, 8, 1024], BF16) for _ in range(2)]
        xs = [[A.get([128, 1056], BF16) for _ in range(4)] for _ in range(2)]
        xsT = [A.get([128, 8, 512], BF16) for _ in range(2)]
        sg = [A.get([128, 512], BF16) for _ in range(2)]
        hT = [A.get([128, 8, 512], BF16) for _ in range(2)]
        yo = [A.get([128, D], F32) for _ in range(3)]
        ln2_sb = A.get([128, 2 * D], F32)
        b_affall, b_cmp, b_lo, b_tr, b_cnt, b_ge, b_M, b_cinc, b_ex, b_iota, b_z = [Buf() for _ in range(11)]
        b_cmpS = [Buf() for _ in range(4)]
        b_idx = [Buf(), Buf(), Buf()]
        b_Wgu = [[Buf() for _ in range(4)] for _ in range(2)]
        b_Wd = [[Buf() for _ in range(2)] for _ in range(2)]
        b_xs = [[Buf() for _ in range(4)] for _ in range(2)]
        b_xsT = [Buf(), Buf()]
        b_sg = [Buf(), Buf()]
        b_hT = [Buf(), Buf()]
        b_yo = [Buf() for _ in range(3)]
        b_ln2 = Buf()
        b_out = Buf()

        def load_w(e):
            wi = e % 2
            for qd in range(4):
                S.dma("pool", DMA(Wgu_sb[wi][:, :, qd * 512:(qd + 1) * 512], wgu[e][:, :, qd * 512:(qd + 1) * 512]),
                      writes=[b_Wgu[wi][qd]], sem="w")
            for hf in range(2):
                S.dma("pool", DMA(Wd_sb[wi][:, :, hf * 512:(hf + 1) * 512], wd[e][:, :, hf * 512:(hf + 1) * 512]),
                      writes=[b_Wd[wi][hf]], sem="w")

        load_w(0)
        S.dma("sp", DMA(iota_sb, iota), writes=[b_iota])
        S.dma("sp", DMA(ln2_sb, ln2), writes=[b_ln2])
        S.op("dve", MSET(zrow, 0.0), writes=[b_z])
        for c in range(4):
            S.dma("sp", DMA(x1g[2048 + c * 128:2048 + (c + 1) * 128, :], zrow), reads=[b_z], writes=[b_x1g])
        S.dma("sp", DMA(slot_sb, slotn), writes=[b_iota])
        S.op("dve", MSET(yo[0], 0.0), writes=[b_yo[0]])
        for j in range(16):
            S.dma("sp", DMA(ffn[j * 128:(j + 1) * 128, :], yo[0]), reads=[b_yo[0]], writes=[b_ffn])
        S.coll("pool", lambda e: e.collective_compute("AllGather", ALU.bypass,
                                                      replica_groups=[[0, 1], [2, 3], [4, 5], [6, 7]],
                                                      ins=[affbuf.opt()], outs=[affall.opt()]),
               reads=[b_affbuf], writes=[b_affall])
        S.dma("sp", DMA(affall_sb.rearrange("p t e -> p (t e)"), affall.rearrange("(p t) e -> p (t e)", t=32)),
              reads=[b_affall], writes=[b_affall])
        S.op("dve", MSET(lo, 0.0), writes=[b_lo])
        step = 0.5
        for it in range(30):
            S.op("dve", TS(tr_, lo, step, None, ALU.add), reads=[b_lo], writes=[b_tr])
            S.op("dve", TT(cmp_sb.rearrange("p (t e) -> p t e", e=16), affall_sb,
                           tr_.unsqueeze(1).to_broadcast([128, 32, 16]), ALU.is_ge), reads=[b_affall, b_tr],
                 writes=[b_cmp])
            cp_, cpb = ps_single()
            S.op("pe", MM(cp_, ones_bf, cmp_sb, True, True), reads=[b_cmp, b_const], writes=cpb)
            S.op("dve", lambda e, o=cntt, i=cp_.rearrange("p (t e) -> p e t", e=16): e.reduce_sum(out=o, in_=i, axis=AX.X),
                 reads=cpb, writes=[b_cnt])
            S.op("dve", TS(ge, cntt, float(CAP) - 0.5, None, ALU.is_ge), reads=[b_cnt], writes=[b_ge])
            S.op("dve", STT(lo, ge, step, lo, ALU.mult, ALU.add), reads=[b_ge, b_lo], writes=[b_lo])
            step *= 0.5
        if debug:
            S.dma("sp", DMA(thr_dbg, lo), reads=[b_lo], writes=[b_dbg])
            S.dma("sp", DMA(aff_dbg, aff_own.rearrange("p t e -> p (t e)")), reads=[b_aff], writes=[b_dbg])
        S.op("dve", TT(M_sb, aff_own, lo.unsqueeze(1).to_broadcast([128, 16, 16]), ALU.is_ge), reads=[b_aff, b_lo],
             writes=[b_M])
        p1, p1b = ps_single()
        S.op("pe", MM(p1[:, 0:256], tri_bf, M_sb.rearrange("p t e -> p (t e)"), True, True), reads=[b_M, b_const],
             writes=p1b)
        p2, p2b = ps_single()
        S.op("pe", MM(p2[:, 0:256], ones_bf, M_sb.rearrange("p t e -> p (t e)"), True, True), reads=[b_M, b_const],
             writes=p2b)
        S.op("dve", MSET(ex[:, 0, :], 0.0), writes=[b_ex])
        for j in range(1, 16):
            S.op("dve", TT(ex[:, j, :], ex[:, j - 1, :], p2[:, (j - 1) * 16:j * 16], ALU.add), reads=p2b + [b_ex],
                 writes=[b_ex])
        S.op("dve", TT(c_incl.rearrange("p t e -> p (t e)"), p1[:, 0:256], ex.rearrange("p t e -> p (t e)"), ALU.add),
             reads=p1b + [b_ex], writes=[b_cinc])

        def make_idx(e):
            ii = e % 3
            ip, ipb = ps_acc()
            for j in range(16):
                ci = j % 4
                eng = "dve"
                S.op(eng, TS(cmpS[ci], iota_sb, c_incl[:, j, e:e + 1], None, ALU.is_ge), reads=[b_iota, b_cinc],
                     writes=[b_cmpS[ci]])
                for c in range(4):
                    S.op("pe", MM(ip[:, c:c + 1], cmpS[ci][:, c * 128:(c + 1) * 128], ones_bf[:, 0:1],
                                  j == 0 and c == 0, j == 15), reads=[b_cmpS[ci], b_const], writes=ipb)
            S.op("dve", TS(idx_f[ii][:, 4:8], ip[:, 0:4], 2047.5, None, ALU.is_ge), reads=ipb, writes=[b_idx[ii]])
            S.op("dve", TT(idx_f[ii][:, 4:8], idx_f[ii][:, 4:8], slot_sb, ALU.mult), reads=[b_idx[ii], b_iota],
                 writes=[b_idx[ii]])
            S.op("dve", TT(idx_f[ii][:, 0:4], idx_f[ii][:, 4:8], ip[:, 0:4], ALU.add), reads=ipb + [b_idx[ii]],
                 writes=[b_idx[ii]])
            S.op("dve", CP(idx_i[ii], idx_f[ii][:, 0:4]), reads=[b_idx[ii]], writes=[b_idx[ii]])
            if debug:
                S.dma("sp", DMA(idx_dbg[e], idx_i[ii]), reads=[b_idx[ii]], writes=[b_dbg])

        def gather(e):
            wi_ = e % 2
            for c in range(4):
                S.dma("pool", lambda eng, o=xs[wi_][c], ix=idx_i[e % 3][:, c:c + 1]: eng.indirect_dma_start(
                    out=o[:, :], out_offset=None, in_=x1g[:, :],
                    in_offset=bass.IndirectOffsetOnAxis(ap=ix, axis=0)),
                    reads=[b_idx[e % 3], b_x1g], writes=[b_xs[wi_][c]], sem="g")

        make_idx(0)
        gather(0)
        yo_n = 0
        b_sc = [[Buf() for _ in range(4)] for _ in range(16)]
        for e in range(16):
            wi = e % 2
            if e + 1 < 16:
                load_w(e + 1)
                make_idx(e + 1)
                gather(e + 1)
            for c in range(4):
                tp, tpb = ps_single()
                tpv = tp.bitcast(BF16)
                for k in range(8):
                    S.op("pe", TR(tpv[:, k * 128:(k + 1) * 128], xs[wi][c][:, k * 128:(k + 1) * 128], ident),
                         reads=[b_xs[wi][c], b_const], writes=tpb)
                eng = "dve" if c % 2 == 0 else "act"
                if eng == "dve":
                    S.op("dve", CP(xsT[wi][:, :, c * 128:(c + 1) * 128], tpv.rearrange("p (k t) -> p k t", k=8)),
                         reads=tpb, writes=[b_xsT[wi]])
                else:
                    S.op("act", ACT(xsT[wi][:, :, c * 128:(c + 1) * 128], tpv.rearrange("p (k t) -> p k t", k=8),
                                    AF.Copy), reads=tpb, writes=[b_xsT[wi]])
            for fc in range(8):
                gp, gpb = ps_single()
                for k in range(8):
                    S.op("pe", MM(gp, Wgu_sb[wi][:, k, fc * 128:(fc + 1) * 128], xsT[wi][:, k, :], k == 0, k == 7),
                         reads=[b_Wgu[wi][fc // 4], b_xsT[wi]], writes=gpb)
                up, upb = ps_single()
                for k in range(8):
                    S.op("pe", MM(up, Wgu_sb[wi][:, k, 1024 + fc * 128:1024 + (fc + 1) * 128], xsT[wi][:, k, :],
                                  k == 0, k == 7), reads=[b_Wgu[wi][2 + fc // 4], b_xsT[wi]], writes=upb)
                si = fc % 2
                S.op("act", ACT(sg[si], gp, AF.Silu), reads=gpb, writes=[b_sg[si]])
                S.op("dve", TT(hT[wi][:, fc, :], up, sg[si], ALU.mult), reads=upb + [b_sg[si]], writes=[b_hT[wi]])
            for c in range(4):
                yi = yo_n % 3
                yo_n += 1
                gcol = xs[wi][c][:, 1024 + 2 * e:1024 + 2 * e + 2].bitcast(F32)
                for hf in range(2):
                    dp, dpb = ps_single()
                    for fc in range(8):
                        S.op("pe", MM(dp, hT[wi][:, fc, c * 128:(c + 1) * 128], Wd_sb[wi][:, fc, hf * 512:(hf + 1) * 512],
                                      fc == 0, fc == 7), reads=[b_hT[wi], b_Wd[wi][hf]], writes=dpb)
                    if hf == 0:
                        S.op("act", ACT(yo[yi][:, 0:512], dp, AF.Copy, scale=gcol), reads=dpb + [b_xs[wi][c]],
                             writes=[b_yo[yi]])
                    else:
                        S.op("dve", TS(yo[yi][:, 512:1024], dp, gcol, None, ALU.mult), reads=dpb + [b_xs[wi][c]],
                             writes=[b_yo[yi]])
                S.dma("pool", lambda eng, i_=yo[yi], ix=idx_i[e % 3][:, c:c + 1]: eng.indirect_dma_start(
                    out=ffn[:, :], out_offset=bass.IndirectOffsetOnAxis(ap=ix, axis=0), in_=i_[:, :], in_offset=None,
                    compute_op=ALU.add),
                    reads=[b_idx[e % 3], b_yo[yi], b_ffn] + (b_sc[e - 1] if e else []), writes=[b_sc[e][c]], sem="g")
        S.barrier()
        A.off = base_off0
        fbuf = [A.get([128, D], F32) for _ in range(2)]
        xbuf = [A.get([128, D], F32) for _ in range(2)]
        fst = [A.get([128, 8], F32) for _ in range(2)]
        fjunk = A.get([128, D], BF16)
        ln2b = A.get([128, 2 * D], F32)
        b_fb, b_xb, b_fst = [Buf(), Buf()], [Buf(), Buf()], [Buf(), Buf()]
        b_fj, b_l2 = Buf(), Buf()
        S.dma("sp", DMA(ln2b, ln2), writes=[b_l2])
        for j in range(16):
            i = j % 2
            S.dma("sp", DMA(fbuf[i], ffn[j * 128:(j + 1) * 128, :]), reads=[b_ffn] + b_sc[15], writes=[b_fb[i]])
            S.dma("sp", DMA(xbuf[i], x1_f[j * 128:(j + 1) * 128, :]), reads=[b_x1f], writes=[b_xb[i]])
            t = fbuf[i]
            st_ = fst[i]
            S.op("dve", STT(t, xbuf[i], ALPHA, t, ALU.mult, ALU.add), reads=[b_xb[i], b_fb[i]], writes=[b_fb[i]])
            S.op("act", ACT(fjunk, t, AF.Copy, accum_out=st_[:, 0:1]), reads=[b_fb[i]], writes=[b_fj, b_fst[i]])
            S.op("act", ACT(fjunk, t, AF.Square, accum_out=st_[:, 1:2]), reads=[b_fb[i]], writes=[b_fj, b_fst[i]])
            S.op("dve", TS(st_[:, 2:3], st_[:, 0:1], 1.0 / D, None, ALU.mult), reads=[b_fst[i]], writes=[b_fst[i]])
            S.op("dve", TT(st_[:, 3:4], st_[:, 2:3], st_[:, 2:3], ALU.mult), reads=[b_fst[i]], writes=[b_fst[i]])
            S.op("dve", STT(st_[:, 4:5], st_[:, 1:2], 1.0 / D, st_[:, 3:4], ALU.mult, ALU.subtract),
                 reads=[b_fst[i]], writes=[b_fst[i]])
            S.op("act", ACT(st_[:, 5:6], st_[:, 4:5], AF.Sqrt, bias=eps_t[:, 1:2]), reads=[b_fst[i], b_const],
                 writes=[b_fst[i]])
            S.op("dve", RECIP(st_[:, 5:6], st_[:, 5:6]), reads=[b_fst[i]], writes=[b_fst[i]])
            S.op("dve", TS(t, t, st_[:, 2:3], st_[:, 5:6], ALU.subtract, ALU.mult), reads=[b_fb[i], b_fst[i]],
                 writes=[b_fb[i]])
            S.op("dve", TT(t, t, ln2b[:, 0:D], ALU.mult), reads=[b_fb[i], b_l2], writes=[b_fb[i]])
            S.op("dve", TT(t, t, ln2b[:, D:2 * D], ALU.add), reads=[b_fb[i], b_l2], writes=[b_fb[i]])
            S.dma("sp", DMA(out_d[j * 128:(j + 1) * 128, :], t), reads=[b_fb[i]], writes=[b_out])
        S.barrier()
        S.emit(st)
    return nc


def _rope_tables():
    t = np.arange(SEQ)
    row = (t // 64).astype(np.float32)
    col = (t % 64).astype(np.float32)
    inv = (np.float32(10000.0) ** (-np.arange(0, 32, 2, dtype=np.float32) / np.float32(32))).astype(np.float32)
    ang = np.concatenate([row[:, None] * inv, col[:, None] * inv], axis=-1).astype(np.float32)
    return np.cos(ang).astype(np.float32), np.sin(ang).astype(np.float32)


def _bm_tile(rpb, jg, ktg):
    if ktg < 0 or ktg > 31:
        return np.full((8, 128, 128), NEG, np.float32)
    i = np.arange(128)
    kr, kc = (i // 64)[:, None], (i % 64)[:, None]
    qr, qc = (i // 64)[None, :], (i % 64)[None, :]
    key_row = 2 * ktg + kr
    r = 2 * jg + qr
    rs = np.clip(r - 4, 0, 56)
    cs = np.clip(qc - 8, 0, 48)
    valid = (key_row >= rs) & (key_row < rs + 8) & (kc >= cs) & (kc < cs + 16)
    dr = np.clip(key_row - r + 7, 0, 14)
    dc = np.clip(kc - qc + 15, 0, 30)
    dr, dc = np.broadcast_arrays(dr, dc)
    vals = rpb[:, dr, dc]
    return np.where(valid[None], vals, np.float32(NEG)).astype(np.float32)


def _core_inputs(b, h, inp, shared):
    x = inp["x"][b]
    own = [16 * h + i for i in range(16)]
    order = own + [16 * h - 2, 16 * h - 1, 16 * h + 16, 16 * h + 17] + [16 * (1 - h) + i for i in range(16)]
    xTc = np.zeros((D, NTOK), np.float32)
    cosT = np.zeros((128, NTOK), np.float32)
    sinT = np.zeros((128, NTOK), np.float32)
    cos, sin = shared["rope"]
    pidx = (np.arange(128) % 64) // 2
    for s, t in enumerate(order):
        if 0 <= t < 32:
            xTc[:, s * 128:(s + 1) * 128] = x[t * 128:(t + 1) * 128, :].T
            cosT[:, s * 128:(s + 1) * 128] = cos[t * 128:(t + 1) * 128, :][:, pidx].T
            sinT[:, s * 128:(s + 1) * 128] = sin[t * 128:(t + 1) * 128, :][:, pidx].T
    rpb = inp["na_rpb"][0]
    bmc = np.full((5, 128, 8, 6, 128), NEG, np.float32)
    for cls, jl in enumerate([2, 0, 1, 14, 15]):
        jg = 16 * h + jl
        for ki, w in enumerate(na_wlist(jl)):
            ktg = 16 * h - 2 + w
            bmc[cls, :, :, ki, :] = _bm_tile(rpb, jg, ktg).transpose(1, 0, 2)
    d = dict(shared["common"])
    d["xT"] = np.ascontiguousarray(xTc.reshape(8, 128, NTOK).transpose(1, 0, 2))
    d["x_own"] = np.ascontiguousarray(x[h * 2048:(h + 1) * 2048])
    d["memT"] = np.ascontiguousarray(inp["mem"][b].T.reshape(8, 128, 256).transpose(1, 0, 2))
    d["bm"] = np.ascontiguousarray(bmc.reshape(5, 128, 8 * 6 * 128))
    d["cosT"] = cosT
    d["sinT"] = sinT
    return d


def _pk(w):
    return np.ascontiguousarray(w.reshape(8, 128, -1).transpose(1, 0, 2))


def _shared(inp):
    w_in = inp["w_in"][0]
    q_na, k_na, v_na = w_in[:, 0:512], w_in[:, 512:1024], w_in[:, 1024:1536]
    q_g, k_g, v_g = w_in[:, 1536:2048], w_in[:, 2048:2176], w_in[:, 2176:2304]
    q_m, gl = w_in[:, 2304:2816], w_in[:, 2816:5888]
    wA = _pk(np.concatenate([k_na, v_na, k_g, v_g], axis=1))
    chunks = [q_na[:, c * 128:(c + 1) * 128] for c in range(4)]
    for i in range(4):
        chunks.append(np.concatenate([q_g[:, i * 64:(i + 1) * 64], q_g[:, (4 + i) * 64:(5 + i) * 64]], axis=1))
    chunks += [q_m[:, c * 128:(c + 1) * 128] for c in range(4)]
    wq = np.stack([_pk(c) for c in chunks])
    wg = np.stack([_pk(gl[:, c * 128:(c + 1) * 128]) for c in range(24)])
    wb = inp["w_branch"][0]
    wbr = np.zeros((8, 128, 12, 128), np.float32)
    for dc in range(8):
        for g in range(3):
            for c in range(4):
                wbr[dc, :, g * 4 + c, :] = wb[g, c * 128:(c + 1) * 128, dc * 128:(dc + 1) * 128]
    w_out = inp["w_out"][0]
    wo = np.stack([_pk(w_out[:, hf * 512:(hf + 1) * 512]) for hf in range(2)])
    wm = _pk(inp["w_mem_kv"][0])
    ident = np.eye(128, dtype=np.float32)
    bones = np.kron(np.eye(2, dtype=np.float32), np.ones((64, 64), np.float32))
    rot = np.zeros((128, 128), np.float32)
    for i in range(64):
        rot[2 * i + 1, 2 * i] = -1.0
        rot[2 * i, 2 * i + 1] = 1.0
    gains = np.stack([np.tile(inp["gqa_q_gain"][0], 2), np.tile(inp["gqa_k_gain"][0], 2)], axis=1)
    bgate = np.ascontiguousarray(inp["b_gate"][0].reshape(24, 128).T)
    ln1 = np.concatenate([np.broadcast_to(inp["ln1_g"][0], (128, D)), np.broadcast_to(inp["ln1_b"][0], (128, D))],
                         axis=1)
    ones = np.ones((128, 128), np.float32)
    tri = np.triu(np.ones((128, 128), np.float32))
    wgu = np.ascontiguousarray(inp["w_gate_up"][0].reshape(16, 8, 128, 2048).transpose(0, 2, 1, 3))
    wd = np.ascontiguousarray(inp["w_down"][0].reshape(16, 8, 128, 1024).transpose(0, 2, 1, 3))
    ln2 = np.concatenate([np.broadcast_to(inp["ln2_g"][0], (128, D)), np.broadcast_to(inp["ln2_b"][0], (128, D))],
                         axis=1)
    common = {
        "identf": ident, "iota": np.ascontiguousarray(np.broadcast_to(np.arange(512, dtype=np.float32), (128, 512))),
        "wr": _pk(inp["w_router"][0]),
        "slotn": np.ascontiguousarray((np.arange(128, dtype=np.float32)[:, None]
                                       + 128.0 * np.arange(4, dtype=np.float32)[None, :])), "ln2": np.ascontiguousarray(ln2.astype(np.float32)),
        "wgu": wgu, "wd": wd,
        "wA": wA, "wq": wq, "wg": wg, "wbr": wbr, "wo": wo, "wm": wm,
        "cmat": np.ascontiguousarray(np.concatenate([ident, bones, rot, ones, tri], axis=1)),
        "gains": np.ascontiguousarray(gains.astype(np.float32)),
        "bgate": bgate.astype(np.float32),
        "ln1": np.ascontiguousarray(ln1.astype(np.float32)),
    }
    return {"common": common, "rope": _rope_tables()}


def run(inp, debug=False):
    inp = {k: np.asarray(v) for k, v in inp.items()}
    shared = _shared(inp)
    in_maps = [_core_inputs(c // 2, c % 2, inp, shared) for c in range(8)]
    nc = build_program(debug=debug)
    res = run_bass_kernel_spmd(nc, in_maps, core_ids=list(range(8)))
    return res


def kernel(**inputs):
    res = run(inputs)
    out = np.zeros((4, SEQ, D), np.float32)
    for c in range(8):
        b, h = c // 2, c % 2
        out[b, h * 2048:(h + 1) * 2048] = res.results[c]["out"]
    return out
```

```python
import math
from contextlib import ExitStack

import numpy as np
import concourse.bass as bass
import concourse.mybir as mybir
from concourse.bass_utils import run_bass_kernel_spmd

F32 = mybir.dt.float32
BF16 = mybir.dt.bfloat16
I32 = mybir.dt.int32
AF = mybir.ActivationFunctionType
ALU = mybir.AluOpType
AX = mybir.AxisListType

D = 1024
SEQ = 4096
NT_OWN = 16
NSLOT = 36
NTOK = NSLOT * 128
ALPHA = 2.0 ** 0.25
LN_EPS = 1e-5
RMS_EPS = 1e-6
NEG = -30000.0
NE = 16
CAP = 512


class Buf:
    __slots__ = ("w", "r")

    def __init__(self):
        self.w = None
        self.r = {}


class Sched:
    ENG = ("pe", "act", "dve", "pool", "sp")

    def __init__(self, nc):
        self.nc = nc
        self.q = {e: [] for e in self.ENG}
        self.cnt = {}
        self.waited = {}
        self.dnext = {}

    def _deps(self, reads, writes):
        deps = {}

        def add(tok):
            if tok is not None and deps.get(tok[0], 0) < tok[1]:
                deps[tok[0]] = tok[1]

        for b in reads:
            add(b.w)
        for b in writes:
            add(b.w)
            for s, v in b.r.items():
                add((s, v))
        return deps

    def _emit_waits(self, q, deps, own):
        for s, v in deps.items():
            if s == own and q == "pe":
                continue
            key = (q, s)
            if self.waited.get(key, 0) >= v:
                continue
            self.waited[key] = v
            self.q[q].append(("wait", s, v))

    def _update(self, tok, reads, writes):
        s, v = tok
        for b in reads:
            if b.r.get(s, 0) < v:
                b.r[s] = v
        for b in writes:
            b.w = tok
            b.r = {}

    def op(self, q, fn, reads=(), writes=()):
        own = "c_" + q
        self._emit_waits(q, self._deps(reads, writes), own)
        v = self.cnt.get(own, 0) + 1
        self.cnt[own] = v
        self.q[q].append(("op", fn, own, 1))
        tok = (own, v)
        self._update(tok, reads, writes)
        return tok

    NDSEM = 12

    def dma(self, q, fn, reads=(), writes=(), sem=None):
        cls = sem or q
        i = self.dnext.get(cls, 0)
        self.dnext[cls] = (i + 1) % self.NDSEM
        s = "d_%s_%d" % (cls, i)
        prev = self.cnt.get(s, 0)
        if prev:
            self._emit_waits(q, {s: prev}, None)
        self._emit_waits(q, self._deps(reads, writes), None)
        v = prev + 16
        self.cnt[s] = v
        self.q[q].append(("op", fn, s, 16))
        tok = (s, v)
        self._update(tok, reads, writes)
        return tok

    def coll(self, q, fn, reads=(), writes=()):
        s = "cc"
        self._emit_waits(q, self._deps(reads, writes), None)
        v = self.cnt.get(s, 0) + 1
        self.cnt[s] = v
        self.q[q].append(("op", fn, s, 1))
        tok = (s, v)
        self._update(tok, reads, writes)
        return tok

    def barrier(self):
        for q in self.ENG:
            self._emit_waits(q, dict(self.cnt), None)

    def emit(self, stack):
        nc = self.nc
        sems = {s: stack.enter_context(nc.semaphore(s)) for s in self.cnt}
        block = stack.enter_context(nc.Block())

        def run(q):
            def f(eng):
                for it in self.q[q]:
                    if it[0] == "wait":
                        eng.wait_ge(sems[it[1]], it[2])
                    else:
                        it[1](eng).then_inc(sems[it[2]], it[3])
            return f

        block.tensor(run("pe"))
        block.scalar(run("act"))
        block.vector(run("dve"))
        block.gpsimd(run("pool"))
        block.sync(run("sp"))


def MM(out, lhsT, rhs, start, stop):
    return lambda e: e.matmul(out, lhsT, rhs, start=start, stop=stop)


def TR(out, in_, ident):
    return lambda e: e.transpose(out, in_, ident)


def ACT(out, in_, func, **kw):
    return lambda e: e.activation(out=out, in_=in_, func=func, **kw)


def TT(out, in0, in1, op):
    return lambda e: e.tensor_tensor(out=out, in0=in0, in1=in1, op=op)


def TS(out, in0, s1, s2, op0, op1=None, **kw):
    if op1 is None:
        return lambda e: e.tensor_scalar(out, in0, s1, s2, op0, **kw)
    return lambda e: e.tensor_scalar(out, in0, s1, s2, op0, op1, **kw)


def STT(out, in0, scalar, in1, op0, op1):
    return lambda e: e.scalar_tensor_tensor(out=out, in0=in0, scalar=scalar, in1=in1, op0=op0, op1=op1)


def CP(out, in_):
    return lambda e: e.tensor_copy(out=out, in_=in_)


def RECIP(out, in_):
    return lambda e: e.reciprocal(out=out, in_=in_)


def MSET(ap, v):
    return lambda e: e.memset(ap, v)


def DMA(out, in_, **kw):
    return lambda e: e.dma_start(out=out, in_=in_, **kw)


def win_slot(w):
    if w < 2:
        return 16 + w
    if w < 18:
        return w - 2
    return w


def key_slot(kt):
    return kt if kt < 16 else kt + 4


def na_wlist(jl):
    if jl == 0:
        return list(range(0, 6))
    if jl == 15:
        return list(range(14, 20))
    return list(range(jl, jl + 5))


def na_class(jl):
    return {0: 1, 1: 2, 14: 3, 15: 4}.get(jl, 0)


def build_program(debug=False):
    nc = bass.Bass("TRN2", target_bir_lowering=False)

    def din(name, shape, dt=F32):
        return nc.dram_tensor(name, list(shape), dt, kind="ExternalInput").ap()

    def dout(name, shape, dt=F32):
        return nc.dram_tensor(name, list(shape), dt, kind="ExternalOutput").ap()

    def dint(name, shape, dt=F32):
        return nc.dram_tensor(name, list(shape), dt, kind="Internal").ap()

    xT = din("xT", [128, 8, NTOK])
    x_own = din("x_own", [2048, D])
    memT = din("memT", [128, 8, 256])
    wA = din("wA", [128, 8, 1280])
    wq = din("wq", [12, 128, 8, 128])
    wg = din("wg", [24, 128, 8, 128])
    wbr = din("wbr", [8, 128, 12, 128])
    wo = din("wo", [2, 128, 8, 512])
    wm = din("wm", [128, 8, 1024])
    bm = din("bm", [5, 128, 8 * 6 * 128])
    cosT = din("cosT", [128, NTOK])
    sinT = din("sinT", [128, NTOK])
    cmat = din("cmat", [128, 5 * 128])
    gains = din("gains", [128, 2])
    bgate = din("bgate", [128, 24])
    ln1 = din("ln1", [128, 2 * D])
    x1_f = dout("x1_f", [2048, D]) if debug else dint("x1_f", [2048, D])
    identf = din("identf", [128, 128])
    iota = din("iota", [128, 512])
    wr = din("wr", [128, 8, 16])
    slotn = din("slotn", [128, 4])
    ln2 = din("ln2", [128, 2 * D])
    wgu = din("wgu", [16, 128, 8, 2048])
    wd = din("wd", [16, 128, 8, 1024])
    out_d = dout("out", [2048, D])
    x1g = dint("x1g", [2560, 1056], BF16)
    ffn = dint("ffn", [2560, D])
    affbuf = dint("affbuf", [2048, 16])
    affall = dint("affall", [4096, 16])
    if debug:
        aff_dbg = dout("aff_dbg", [128, 16 * 16])
        thr_dbg = dout("thr_dbg", [128, 16])
        idx_dbg = dout("idx_dbg", [16, 128, 4], I32)
    if debug:
        y_dbg = dout("y_dbg", [2048, 1536], BF16)
        mg_dbg = dout("mg_dbg", [128, 8, 2048], BF16)
        qg_dbg = dout("qg_dbg", [128, 8, 512], BF16)
        qna_dbg = dout("qna_dbg", [128, 8, 512], BF16)
        kna_dbg = dout("kna_dbg", [128, 4, 2560], BF16)
        kg_dbg = dout("kg_dbg", [128, 4096], BF16)
        pt_dbg = dout("pt_dbg", [128, 768], BF16)
        vna_dbg = dout("vna_dbg", [128, 20, 520], BF16)

    S = Sched(nc)
    with ExitStack() as st:
        SBN = 94000
        sb_all = st.enter_context(nc.sbuf_tensor("sb_all", [128, SBN], BF16))
        PS = [st.enter_context(nc.psum_tensor("ps%d" % i, [128, 1024], F32)) for i in range(4)]
        bank_buf = [Buf() for _ in range(8)]

        class Alloc:
            def __init__(self):
                self.off = 0

            def get(self, shape, dt):
                n = int(np.prod(shape[1:]))
                e16 = n * 2 if dt in (F32, I32) else n
                e16 = (e16 + 1) // 2 * 2
                assert self.off + e16 <= SBN, ("SBUF overflow", self.off, e16)
                ap = sb_all[:, self.off:self.off + e16]
                self.off += e16
                if dt != BF16:
                    ap = ap.bitcast(dt)
                if len(shape) == 3:
                    ap = ap.rearrange("p (a b) -> p a b", a=shape[1])
                elif len(shape) == 4:
                    ap = ap.rearrange("p (a b c) -> p a b c", a=shape[1], b=shape[2])
                return ap

        A = Alloc()

        def bank(i):
            return PS[i // 2][:, (i % 2) * 512:(i % 2) * 512 + 512]

        rot_state = {"s": 0, "p": 0, "a": 0}

        def ps_single():
            i = rot_state["s"]
            rot_state["s"] = (i + 1) % 6
            return bank(i), [bank_buf[i]]

        def ps_pair():
            i = rot_state["p"]
            rot_state["p"] = (i + 1) % 3
            return PS[i][:, :], [bank_buf[2 * i], bank_buf[2 * i + 1]]

        def ps_acc():
            i = 6 + rot_state["a"]
            rot_state["a"] = 1 - rot_state["a"]
            return bank(i), [bank_buf[i]]

        cm = A.get([128, 640], BF16)
        ident, bones, rotm = cm[:, 0:128], cm[:, 128:256], cm[:, 256:384]
        ones_bf, tri_bf = cm[:, 384:512], cm[:, 512:640]
        idf = A.get([128, 128], F32)
        aff_own = A.get([128, 16, 16], F32)
        b_aff = Buf()
        gn = A.get([128, 2], F32)
        bg = A.get([128, 24], F32)
        eps_t = A.get([128, 2], F32)
        base_off0 = A.off
        KnaT = A.get([128, 4, 20 * 128], BF16)
        Vna = A.get([128, 20, 8 * 65], BF16)
        KgT = A.get([128, 4096], BF16)
        Vg = A.get([128, 32, 2 * 65], BF16)
        mkT = A.get([128, 4, 256], BF16)
        mv = A.get([128, 2, 4 * 129], BF16)
        yT = A.get([128, 12, 512], BF16)
        xTb = [A.get([128, 8, 512], BF16) for _ in range(2)]
        b_const, b_gn, b_bg = Buf(), Buf(), Buf()
        b_Kna = [Buf() for _ in range(20)]
        b_Vna = [Buf() for _ in range(20)]
        b_Kg = [Buf() for _ in range(32)]
        b_Vg = [Buf() for _ in range(32)]
        b_mk, b_mv, b_yT = Buf(), Buf(), Buf()
        b_xTb = [Buf(), Buf()]
        base_off = A.off

        S.dma("pool", DMA(cm, cmat), writes=[b_const])
        S.dma("sp", DMA(idf, identf), writes=[b_const])
        S.dma("sp", DMA(gn, gains), writes=[b_gn])
        S.dma("sp", DMA(bg, bgate), writes=[b_bg])
        S.op("pool", MSET(eps_t[:, 0:1], RMS_EPS), writes=[b_const])
        S.op("pool", MSET(eps_t[:, 1:2], LN_EPS), writes=[b_const])
        b_ones = Buf()
        S.op("pool", MSET(Vna, 1.0), writes=b_Vna)
        S.op("pool", MSET(Vg, 1.0), writes=b_Vg)
        S.op("pool", MSET(mv, 1.0), writes=[b_mv])

        def load_xT(blk, idx):
            buf = xTb[idx]
            S.dma("pool", DMA(buf, xT[:, :, blk * 512:(blk + 1) * 512]), writes=[b_xTb[idx]], sem="x")

        def rms_rope(zp, zp_b, gcol, tok0, tmp, dests):
            sq, rstd, qn, t1, t2, cs, sn, b_t = tmp
            S.dma("sp", DMA(cs, cosT[:, tok0:tok0 + 512]), writes=[b_t[5]])
            S.dma("sp", DMA(sn, sinT[:, tok0:tok0 + 512]), writes=[b_t[6]])
            S.op("act", ACT(sq, zp, AF.Square), reads=zp_b, writes=[b_t[0]])
            ssp, ssb = ps_single()
            S.op("pe", MM(ssp, bones, sq, True, True), reads=[b_t[0], b_const], writes=ssb)
            S.op("act", ACT(rstd, ssp, AF.Sqrt, scale=1.0 / 64.0, bias=eps_t[:, 0:1]), reads=ssb + [b_const], writes=[b_t[1]])
            S.op("dve", RECIP(rstd, rstd), reads=[b_t[1]], writes=[b_t[1]])
            S.op("dve", STT(qn, zp, gn[:, gcol:gcol + 1], rstd, ALU.mult, ALU.mult),
                 reads=zp_b + [b_t[1], b_gn], writes=[b_t[2]])
            rqp, rqb = ps_single()
            S.op("pe", MM(rqp, rotm, qn, True, True), reads=[b_t[2], b_const], writes=rqb)
            S.op("dve", TT(t1, qn, cs, ALU.mult), reads=[b_t[2], b_t[5]], writes=[b_t[3]])
            S.op("dve", TT(t2, rqp, sn, ALU.mult), reads=rqb + [b_t[6]], writes=[b_t[4]])
            for (dst, lo, hi, dbufs) in dests:
                S.op("dve", TT(dst, t1[lo:hi, :], t2[lo:hi, :], ALU.add), reads=[b_t[3], b_t[4]], writes=dbufs)

        A.off = base_off
        wA_sb = A.get([128, 8, 1280], BF16)
        wm_sb = A.get([128, 8, 1024], BF16)
        memT_sb = A.get([128, 8, 256], BF16)
        tmpA = (A.get([128, 512], BF16), A.get([128, 512], F32), A.get([128, 512], BF16),
                A.get([128, 512], F32), A.get([128, 512], F32), A.get([128, 512], F32),
                A.get([128, 512], F32), [Buf() for _ in range(7)])
        b_wA, b_wm, b_memT = Buf(), Buf(), Buf()
        S.dma("pool", DMA(wA_sb, wA), writes=[b_wA], sem="w")
        S.dma("pool", DMA(memT_sb, memT), writes=[b_memT], sem="w")
        S.dma("pool", DMA(wm_sb, wm), writes=[b_wm], sem="w")
        load_xT(0, 0)
        for blk in range(9):
            xi = blk % 2
            if blk + 1 < 9:
                load_xT(blk + 1, 1 - xi)
            xb, xbb = xTb[xi], [b_xTb[xi]]
            slots = [blk * 4 + i for i in range(4)]
            in_win = blk < 5
            in_key = blk != 4
            if in_win:
                for p in range(4):
                    pp, pb = ps_single()
                    for k in range(8):
                        S.op("pe", MM(pp, wA_sb[:, k, p * 128:(p + 1) * 128], xb[:, k, :], k == 0, k == 7),
                             reads=xbb + [b_wA], writes=pb)
                    S.op("act", ACT(KnaT[:, p, blk * 512:(blk + 1) * 512], pp, AF.Copy), reads=pb,
                         writes=[b_Kna[s] for s in slots])
                for i, s in enumerate(slots):
                    pp, pb = ps_single()
                    for k in range(8):
                        S.op("pe", MM(pp, xb[:, k, i * 128:(i + 1) * 128], wA_sb[:, k, 512:1024], k == 0, k == 7),
                             reads=xbb + [b_wA], writes=pb)
                    dst = Vna[:, s, :].rearrange("p (h e) -> p h e", h=8)[:, :, 0:64]
                    S.op("dve", CP(dst, pp.rearrange("p (h e) -> p h e", h=8)), reads=pb, writes=[b_Vna[s]])
            if in_key:
                kt0 = slots[0] if blk < 4 else slots[0] - 4
                pp, pb = ps_single()
                for k in range(8):
                    S.op("pe", MM(pp, wA_sb[:, k, 1024:1152], xb[:, k, :], k == 0, k == 7),
                         reads=xbb + [b_wA], writes=pb)
                rms_rope(pp, pb, 1, blk * 512, tmpA,
                         [(KgT[:, kt0 * 128:kt0 * 128 + 512], 0, 128, [b_Kg[kt0 + i] for i in range(4)])])
                for i in range(4):
                    pp, pb = ps_single()
                    for k in range(8):
                        S.op("pe", MM(pp[:, 0:128], xb[:, k, i * 128:(i + 1) * 128], wA_sb[:, k, 1152:1280],
                                      k == 0, k == 7), reads=xbb + [b_wA], writes=pb)
                    dst = Vg[:, kt0 + i, :].rearrange("p (h e) -> p h e", h=2)[:, :, 0:64]
                    S.op("dve", CP(dst, pp[:, 0:128].rearrange("p (h e) -> p h e", h=2)), reads=pb,
                         writes=[b_Vg[kt0 + i]])
        for h in range(4):
            pp, pb = ps_single()
            for k in range(8):
                S.op("pe", MM(pp[:, 0:256], wm_sb[:, k, h * 128:(h + 1) * 128], memT_sb[:, k, :], k == 0, k == 7),
                     reads=[b_wm, b_memT], writes=pb)
            S.op("act", ACT(mkT[:, h, :], pp[:, 0:256], AF.Copy), reads=pb, writes=[b_mk])
        for mt in range(2):
            pp, pb = ps_single()
            for k in range(8):
                S.op("pe", MM(pp, memT_sb[:, k, mt * 128:(mt + 1) * 128], wm_sb[:, k, 512:1024], k == 0, k == 7),
                     reads=[b_wm, b_memT], writes=pb)
            dst = mv[:, mt, :].rearrange("p (h e) -> p h e", h=4)[:, :, 0:128]
            S.op("dve", CP(dst, pp.rearrange("p (h e) -> p h e", h=4)), reads=pb, writes=[b_mv])
        S.barrier()

        A.off = base_off
        offB = A.off
        Qna = A.get([128, 8, 512], BF16)
        Qg = A.get([128, 8, 512], BF16)
        Qm = A.get([128, 4, 512], BF16)
        tmpB = (A.get([128, 512], BF16), A.get([128, 512], F32), A.get([128, 512], BF16),
                A.get([128, 512], F32), A.get([128, 512], F32), A.get([128, 512], F32),
                A.get([128, 512], F32), [Buf() for _ in range(7)])
        qtmp = A.get([128, 512], BF16)
        wq_sb = [A.get([128, 8, 128], BF16) for _ in range(3)]
        bm_int = A.get([128, 8, 6, 128], BF16)
        bm_brd = A.get([128, 8, 6, 128], BF16)
        PTn = [A.get([128, 768], BF16) for _ in range(2)]
        PTg = [A.get([128, 512], BF16) for _ in range(3)]
        y_sb = [A.get([128, 1536], BF16) for _ in range(2)]
        rc = [A.get([128, 8], F32) for _ in range(2)]
        endB1 = A.off
        A.off = offB
        wg_sb = [A.get([128, 8, 128], BF16) for _ in range(4)]
        wbr_sb = [A.get([128, 12, 128], BF16) for _ in range(2)]
        wo_sb = [A.get([128, 8, 512], BF16) for _ in range(2)]
        gate_sb = [A.get([128, 512], BF16) for _ in range(3)]
        mtmp = [A.get([128, 512], F32) for _ in range(3)]
        mergedT = A.get([128, 8, 512], BF16)
        tbuf = [A.get([128, D], F32) for _ in range(4)]
        xh = [A.get([128, 512], F32) for _ in range(2)]
        ln_sb = A.get([128, 2 * D], F32)
        stat = [A.get([128, 8], F32) for _ in range(2)]
        junk = A.get([128, D], BF16)
        x1T_sb = A.get([128, D], F32)
        rowt = [A.get([128, 1056], BF16) for _ in range(2)]
        wr_sb = A.get([128, 8, 16], F32)
        rt = [A.get([128, 40], F32) for _ in range(2)]
        endB2 = A.off
        A.off = max(endB1, endB2)

        b_Qna, b_Qg, b_Qm, b_qtmp = Buf(), Buf(), Buf(), Buf()
        b_wq = [Buf() for _ in range(3)]
        b_bmi, b_bmb = Buf(), Buf()
        b_PTn = [Buf(), Buf()]
        b_PTg = [Buf() for _ in range(3)]
        b_y = [Buf(), Buf()]
        b_rc = [Buf(), Buf()]
        b_wg = [Buf() for _ in range(4)]
        b_wbr = [Buf(), Buf()]
        b_wo = [Buf(), Buf()]
        b_gate = [Buf() for _ in range(3)]
        b_mtmp = [Buf() for _ in range(3)]
        b_merged = Buf()
        b_t = [Buf() for _ in range(4)]
        b_xh = [Buf(), Buf()]
        b_ln = Buf()
        b_stat = [Buf(), Buf()]
        b_junk = Buf()
        b_x1T, b_wr = Buf(), Buf()
        b_rowt = [Buf(), Buf()]
        b_rt = [Buf(), Buf()]
        b_x1g, b_affbuf, b_ffn = Buf(), Buf(), Buf()
        b_x1f = Buf()
        b_dbg = Buf()

        load_xT(0, 0)
        cnt = {"wq": 0, "wg": 0, "wbr": 0, "wo": 0, "ptn": 0, "ptg": 0, "y": 0, "xh": 0}
        for blk in range(4):
            xi = blk % 2
            xb, xbb = xTb[xi], [b_xTb[xi]]
            S.op("dve", MSET(Qna, 0.0), writes=[b_Qna])
            S.op("dve", MSET(Qg, 0.0), writes=[b_Qg])
            S.dma("pool", DMA(bm_int.rearrange("p a b c -> p (a b c)"), bm[0]), writes=[b_bmi], sem="w")
            for c in range(12):
                wi = cnt["wq"] % 3
                cnt["wq"] += 1
                S.dma("pool", DMA(wq_sb[wi], wq[c]), writes=[b_wq[wi]], sem="w")
                pp, pb = ps_single()
                for k in range(8):
                    S.op("pe", MM(pp, wq_sb[wi][:, k, :], xb[:, k, :], k == 0, k == 7),
                         reads=xbb + [b_wq[wi]], writes=pb)
                if c < 4:
                    S.op("act", ACT(Qna[0:64, 2 * c, :], pp[0:64, :], AF.Copy, scale=0.125), reads=pb, writes=[b_Qna])
                    S.op("act", ACT(Qna[64:128, 2 * c + 1, :], pp[64:128, :], AF.Copy, scale=0.125), reads=pb,
                         writes=[b_Qna])
                elif c < 8:
                    i = c - 4
                    rms_rope(pp, pb, 0, blk * 512, tmpB,
                             [(Qg[0:64, i, :], 0, 64, [b_Qg]), (Qg[64:128, 4 + i, :], 64, 128, [b_Qg])])
                else:
                    S.op("act", ACT(Qm[:, c - 8, :], pp, AF.Copy), reads=pb, writes=[b_Qm])
            if debug and blk == 0:
                S.dma("sp", DMA(qg_dbg, Qg), reads=[b_Qg], writes=[b_dbg])
                S.dma("sp", DMA(qna_dbg, Qna), reads=[b_Qna], writes=[b_dbg])
                S.dma("sp", DMA(kna_dbg, KnaT), reads=b_Kna, writes=[b_dbg])
                S.dma("sp", DMA(kg_dbg, KgT), reads=b_Kg, writes=[b_dbg])
                S.dma("sp", DMA(vna_dbg, Vna), reads=b_Vna, writes=[b_dbg])
            for qt in range(4):
                jl = blk * 4 + qt
                yi = cnt["y"] % 2
                cnt["y"] += 1
                ysb, ybb = y_sb[yi], [b_y[yi]]
                qs = slice(qt * 128, (qt + 1) * 128)
                cls = na_class(jl)
                if cls == 0:
                    bmt, bmb = bm_int, [b_bmi]
                else:
                    S.dma("pool", DMA(bm_brd.rearrange("p a b c -> p (a b c)"), bm[cls]), writes=[b_bmb], sem="w")
                    bmt, bmb = bm_brd, [b_bmb]
                wl = na_wlist(jl)
                nk = len(wl)
                accs = [ps_acc(), ps_acc()]
                def na_scores(h):
                    sp, spb = ps_pair()
                    for ki, w in enumerate(wl):
                        s_ = win_slot(w)
                        S.op("pe", MM(sp[:, ki * 128:(ki + 1) * 128], KnaT[:, h // 2, s_ * 128:(s_ + 1) * 128],
                                      Qna[:, h, qs], ki % 4 == 0, False), reads=[b_Kna[s_], b_Qna], writes=spb)
                    S.op("pe", MM(sp[:, 0:512], ident, bmt[:, h, 0:4, :], False, True), reads=bmb + [b_const],
                         writes=spb)
                    S.op("pe", MM(sp[:, 512:nk * 128], ident, bmt[:, h, 4:nk, :], False, True),
                         reads=bmb + [b_const], writes=spb)
                    return sp, spb

                nxt = na_scores(0)
                for h in range(8):
                    sp, spb = nxt
                    if h + 1 < 8:
                        nxt = na_scores(h + 1)
                    pi = cnt["ptn"] % 2
                    cnt["ptn"] += 1
                    S.op("act", ACT(PTn[pi][:, 0:nk * 128], sp[:, 0:nk * 128], AF.Exp), reads=spb,
                         writes=[b_PTn[pi]])
                    ap_, ab_ = accs[h // 4]
                    hh = h % 4
                    for ki, w in enumerate(wl):
                        s_ = win_slot(w)
                        S.op("pe", MM(ap_[:, hh * 65:hh * 65 + 65], PTn[pi][:, ki * 128:(ki + 1) * 128],
                                      Vna[:, s_, h * 65:h * 65 + 65], ki == 0 and hh == 0, ki == nk - 1),
                             reads=[b_PTn[pi], b_Vna[s_]], writes=ab_)
                for half in range(2):
                    ap_, ab_ = accs[half]
                    a3 = ap_[:, 0:260].rearrange("p (h e) -> p h e", h=4)
                    ri = half
                    S.op("dve", RECIP(rc[ri][:, 0:4], a3[:, :, 64]), reads=ab_, writes=[b_rc[ri]])
                    S.op("dve", TT(ysb[:, half * 256:(half + 1) * 256].rearrange("p (h e) -> p h e", h=4),
                                   a3[:, :, 0:64], rc[ri][:, 0:4].unsqueeze(2).to_broadcast([128, 4, 64]), ALU.mult),
                         reads=ab_ + [b_rc[ri]], writes=ybb)
                for g in range(2):
                    ap_, ab_ = ps_acc()

                    def gqa_scores(kt, g=g):
                        sp, spb = ps_single()
                        S.op("pe", MM(sp, KgT[:, kt * 128:(kt + 1) * 128], Qg[:, 4 * g:4 * g + 4, qs], True, True),
                             reads=[b_Kg[kt], b_Qg], writes=spb)
                        return sp, spb

                    pend = [gqa_scores(0), gqa_scores(1)]
                    for kt in range(32):
                        sp, spb = pend.pop(0)
                        if kt + 2 < 32:
                            pend.append(gqa_scores(kt + 2))
                        pi = cnt["ptg"] % 3
                        cnt["ptg"] += 1
                        S.op("act", ACT(PTg[pi], sp, AF.Exp, scale=0.125), reads=spb, writes=[b_PTg[pi]])
                        for hh in range(4):
                            S.op("pe", MM(ap_[:, hh * 65:hh * 65 + 65], PTg[pi][:, hh * 128:(hh + 1) * 128],
                                          Vg[:, kt, g * 65:g * 65 + 65], kt == 0 and hh == 0, kt == 31),
                                 reads=[b_PTg[pi], b_Vg[kt]], writes=ab_)
                    a3 = ap_[:, 0:260].rearrange("p (h e) -> p h e", h=4)
                    S.op("dve", RECIP(rc[g][:, 4:8], a3[:, :, 64]), reads=ab_, writes=[b_rc[g]])
                    S.op("dve", TT(ysb[:, 512 + g * 256:512 + (g + 1) * 256].rearrange("p (h e) -> p h e", h=4),
                                   a3[:, :, 0:64], rc[g][:, 4:8].unsqueeze(2).to_broadcast([128, 4, 64]), ALU.mult),
                         reads=ab_ + [b_rc[g]], writes=ybb)
                for hp in range(2):
                    ap_, ab_ = ps_acc()

                    def mem_scores(i, hp=hp):
                        h_, mt_ = hp * 2 + i // 2, i % 2
                        sp, spb = ps_single()
                        S.op("pe", MM(sp[:, 0:128], mkT[:, h_, mt_ * 128:(mt_ + 1) * 128], Qm[:, h_, qs], True, True),
                             reads=[b_mk, b_Qm], writes=spb)
                        return sp, spb

                    pend = [mem_scores(0), mem_scores(1)]
                    for i in range(4):
                        hq, mt = i // 2, i % 2
                        h = hp * 2 + hq
                        sp, spb = pend.pop(0)
                        if i + 2 < 4:
                            pend.append(mem_scores(i + 2))
                        pi = cnt["ptg"] % 3
                        cnt["ptg"] += 1
                        S.op("act", ACT(PTg[pi][:, 0:128], sp[:, 0:128], AF.Exp, scale=1.0 / math.sqrt(128.0)),
                             reads=spb, writes=[b_PTg[pi]])
                        S.op("pe", MM(ap_[:, hq * 129:hq * 129 + 129], PTg[pi][:, 0:128],
                                      mv[:, mt, h * 129:h * 129 + 129], mt == 0 and hq == 0, mt == 1),
                             reads=[b_PTg[pi], b_mv], writes=ab_)
                    a3 = ap_[:, 0:258].rearrange("p (h e) -> p h e", h=2)
                    S.op("dve", RECIP(rc[hp][:, 0:2], a3[:, :, 128]), reads=ab_, writes=[b_rc[hp]])
                    S.op("dve", TT(ysb[:, 1024 + hp * 256:1024 + (hp + 1) * 256].rearrange("p (h e) -> p h e", h=2),
                                   a3[:, :, 0:128], rc[hp][:, 0:2].unsqueeze(2).to_broadcast([128, 2, 128]),
                                   ALU.mult), reads=ab_ + [b_rc[hp]], writes=ybb)
                if debug:
                    S.dma("sp", DMA(y_dbg[jl * 128:(jl + 1) * 128, :], ysb), reads=ybb, writes=[b_dbg])
                for grp in range(3):
                    tp, tpb = ps_single()
                    tpv = tp.bitcast(BF16)
                    for c in range(4):
                        cc = grp * 4 + c
                        S.op("pe", TR(tpv[:, c * 128:(c + 1) * 128], ysb[:, cc * 128:(cc + 1) * 128], ident),
                             reads=ybb + [b_const], writes=tpb)
                    S.op("dve", CP(yT[:, grp * 4:grp * 4 + 4, qs],
                                   tpv[:, 0:512].rearrange("p (c t) -> p c t", c=4)), reads=tpb, writes=[b_yT])
            S.barrier()
            if blk == 0:
                pass
            S.dma("sp", DMA(ln_sb, ln1), writes=[b_ln])
            S.dma("sp", DMA(wr_sb, wr), writes=[b_wr])
            for dc in range(8):
                bi = cnt["wbr"] % 2
                cnt["wbr"] += 1
                S.dma("pool", DMA(wbr_sb[bi], wbr[dc]), writes=[b_wbr[bi]], sem="w")
                for g in range(3):
                    wi = cnt["wg"] % 4
                    cnt["wg"] += 1
                    S.dma("pool", DMA(wg_sb[wi], wg[g * 8 + dc]), writes=[b_wg[wi]], sem="w")
                    pp, pb = ps_single()
                    for k in range(8):
                        S.op("pe", MM(pp, wg_sb[wi][:, k, :], xb[:, k, :], k == 0, k == 7),
                             reads=xbb + [b_wg[wi]], writes=pb)
                    S.op("act", ACT(gate_sb[g], pp, AF.Sigmoid, bias=bg[:, g * 8 + dc:g * 8 + dc + 1]),
                         reads=pb + [b_bg], writes=[b_gate[g]])
                for g in range(3):
                    pp, pb = ps_single()
                    for c in range(4):
                        S.op("pe", MM(pp, wbr_sb[bi][:, g * 4 + c, :], yT[:, g * 4 + c, :], c == 0, c == 3),
                             reads=[b_wbr[bi], b_yT], writes=pb)
                    S.op("dve", TT(mtmp[g], pp, gate_sb[g], ALU.mult), reads=pb + [b_gate[g]], writes=[b_mtmp[g]])
                S.op("dve", TT(mtmp[0], mtmp[0], mtmp[1], ALU.add), reads=[b_mtmp[0], b_mtmp[1]],
                     writes=[b_mtmp[0]])
                S.op("dve", TT(mergedT[:, dc, :], mtmp[0], mtmp[2], ALU.add), reads=[b_mtmp[0], b_mtmp[2]],
                     writes=[b_merged])
            if debug:
                S.dma("sp", DMA(mg_dbg[:, :, blk * 512:(blk + 1) * 512], mergedT), reads=[b_merged], writes=[b_dbg])
            if blk + 1 < 4:
                load_xT(blk + 1, 1 - xi)
            for half in range(2):
                oi = cnt["wo"] % 2
                cnt["wo"] += 1
                S.dma("pool", DMA(wo_sb[oi], wo[half]), writes=[b_wo[oi]], sem="w")
                for qt in range(4):
                    jl = blk * 4 + qt
                    hi_ = cnt["xh"] % 2
                    cnt["xh"] += 1
                    S.dma("sp", DMA(xh[hi_], x_own[jl * 128:(jl + 1) * 128, half * 512:(half + 1) * 512]),
                          writes=[b_xh[hi_]])
                    pp, pb = ps_single()
                    for k in range(8):
                        S.op("pe", MM(pp, mergedT[:, k, qt * 128:(qt + 1) * 128], wo_sb[oi][:, k, :], k == 0, k == 7),
                             reads=[b_merged, b_wo[oi]], writes=pb)
                    S.op("dve", STT(tbuf[qt][:, half * 512:(half + 1) * 512], xh[hi_], ALPHA, pp, ALU.mult, ALU.add),
                         reads=pb + [b_xh[hi_]], writes=[b_t[qt]])
            for qt in range(4):
                jl = blk * 4 + qt
                si = qt % 2
                t = tbuf[qt]
                st_ = stat[si]
                S.op("act", ACT(junk, t, AF.Copy, accum_out=st_[:, 0:1]), reads=[b_t[qt]], writes=[b_junk, b_stat[si]])
                S.op("act", ACT(junk, t, AF.Square, accum_out=st_[:, 1:2]), reads=[b_t[qt]],
                     writes=[b_junk, b_stat[si]])
                S.op("dve", TS(st_[:, 2:3], st_[:, 0:1], 1.0 / D, None, ALU.mult), reads=[b_stat[si]],
                     writes=[b_stat[si]])
                S.op("dve", TT(st_[:, 3:4], st_[:, 2:3], st_[:, 2:3], ALU.mult), reads=[b_stat[si]],
                     writes=[b_stat[si]])
                S.op("dve", STT(st_[:, 4:5], st_[:, 1:2], 1.0 / D, st_[:, 3:4], ALU.mult, ALU.subtract),
                     reads=[b_stat[si]], writes=[b_stat[si]])
                S.op("act", ACT(st_[:, 5:6], st_[:, 4:5], AF.Sqrt, bias=eps_t[:, 1:2]), reads=[b_stat[si], b_const],
                     writes=[b_stat[si]])
                S.op("dve", RECIP(st_[:, 5:6], st_[:, 5:6]), reads=[b_stat[si]], writes=[b_stat[si]])
                S.op("dve", TS(t, t, st_[:, 2:3], st_[:, 5:6], ALU.subtract, ALU.mult), reads=[b_t[qt], b_stat[si]],
                     writes=[b_t[qt]])
                S.op("dve", TT(t, t, ln_sb[:, 0:D], ALU.mult), reads=[b_t[qt], b_ln], writes=[b_t[qt]])
                S.op("dve", TT(t, t, ln_sb[:, D:2 * D], ALU.add), reads=[b_t[qt], b_ln], writes=[b_t[qt]])
                S.dma("sp", DMA(x1_f[jl * 128:(jl + 1) * 128, :], t), reads=[b_t[qt]], writes=[b_x1f])
                tp2, tpb2 = ps_pair()
                for k in range(8):
                    S.op("pe", TR(tp2[:, k * 128:(k + 1) * 128], t[:, k * 128:(k + 1) * 128], idf),
                         reads=[b_t[qt], b_const], writes=tpb2)
                S.op("act", ACT(x1T_sb, tp2, AF.Copy), reads=tpb2, writes=[b_x1T])
                lp, lpb = ps_single()
                for k in range(8):
                    S.op("pe", MM(lp[:, 0:16], x1T_sb[:, k * 128:(k + 1) * 128], wr_sb[:, k, :], k == 0, k == 7),
                         reads=[b_x1T, b_wr], writes=lpb)
                ri = qt % 2
                r_ = rt[ri]
                S.op("dve", lambda e, o=r_[:, 0:1], i=lp[:, 0:16]: e.reduce_max(out=o, in_=i, axis=AX.X),
                     reads=lpb, writes=[b_rt[ri]])
                S.op("dve", TS(r_[:, 1:2], r_[:, 0:1], -1.0, None, ALU.mult), reads=[b_rt[ri]], writes=[b_rt[ri]])
                S.op("act", ACT(r_[:, 8:24], lp[:, 0:16], AF.Exp, bias=r_[:, 1:2], accum_out=r_[:, 2:3]),
                     reads=lpb + [b_rt[ri]], writes=[b_rt[ri]])
                S.op("dve", RECIP(r_[:, 3:4], r_[:, 2:3]), reads=[b_rt[ri]], writes=[b_rt[ri]])
                S.op("dve", TS(aff_own[:, jl, :], r_[:, 8:24], r_[:, 3:4], None, ALU.mult), reads=[b_rt[ri]],
                     writes=[b_aff])
                rw = rowt[ri]
                S.op("act", ACT(rw[:, 0:1024], t, AF.Copy), reads=[b_t[qt]], writes=[b_rowt[ri]])
                S.op("dve", CP(rw[:, 1024:1056].bitcast(F32), aff_own[:, jl, :]), reads=[b_aff], writes=[b_rowt[ri]])
                S.dma("sp", DMA(x1g[jl * 128:(jl + 1) * 128, :], rw), reads=[b_rowt[ri]], writes=[b_x1g])
                S.dma("sp", DMA(affbuf[jl * 128:(jl + 1) * 128, :], aff_own[:, jl, :]), reads=[b_aff],
                      writes=[b_affbuf])
            S.barrier()

        S.barrier()
        A.off = base_off0
        affall_sb = A.get([128, 32, 16], F32)
        cmp_sb = A.get([128, 512], BF16)
        lo = A.get([128, 16], F32)
        tr_ = A.get([128, 16], F32)
        cntt = A.get([128, 16], F32)
        ge = A.get([128, 16], F32)
        M_sb = A.get([128, 16, 16], BF16)
        c_incl = A.get([128, 16, 16], F32)
        ex = A.get([128, 16, 16], F32)
        iota_sb = A.get([128, 512], F32)
        zrow = A.get([128, 1056], BF16)
        cmpS = [A.get([128, 512], BF16) for _ in range(4)]
        idx_i = [A.get([128, 4], I32) for _ in range(3)]
        idx_f = [A.get([128, 8], F32) for _ in range(3)]
        slot_sb = A.get([128, 4], F32)
        Wgu_sb = [A.get([128, 8, 2048], BF16) for _ in range(2)]
        Wd_sb = [A.get([128, 8, 1024], BF16) for _ in range(2)]
        xs = [[A.get([128, 1056], BF16) for _ in range(4)] for _ in range(2)]
        xsT = [A.get([128, 8, 512], BF16) for _ in range(2)]
        sg = [A.get([128, 512], BF16) for _ in range(2)]
        hT = [A.get([128, 8, 512], BF16) for _ in range(2)]
        yo = [A.get([128, D], F32) for _ in range(3)]
        ln2_sb = A.get([128, 2 * D], F32)
        b_affall, b_cmp, b_lo, b_tr, b_cnt, b_ge, b_M, b_cinc, b_ex, b_iota, b_z = [Buf() for _ in range(11)]
        b_cmpS = [Buf() for _ in range(4)]
        b_idx = [Buf(), Buf(), Buf()]
        b_Wgu = [[Buf() for _ in range(4)] for _ in range(2)]
        b_Wd = [[Buf() for _ in range(2)] for _ in range(2)]
        b_xs = [[Buf() for _ in range(4)] for _ in range(2)]
        b_xsT = [Buf(), Buf()]
        b_sg = [Buf(), Buf()]
        b_hT = [Buf(), Buf()]
        b_yo = [Buf() for _ in range(3)]
        b_ln2 = Buf()
        b_out = Buf()

        def load_w(e):
            wi = e % 2
            for qd in range(4):
                S.dma("pool", DMA(Wgu_sb[wi][:, :, qd * 512:(qd + 1) * 512], wgu[e][:, :, qd * 512:(qd + 1) * 512]),
                      writes=[b_Wgu[wi][qd]], sem="w")
            for hf in range(2):
                S.dma("pool", DMA(Wd_sb[wi][:, :, hf * 512:(hf + 1) * 512], wd[e][:, :, hf * 512:(hf + 1) * 512]),
                      writes=[b_Wd[wi][hf]], sem="w")

        load_w(0)
        S.dma("sp", DMA(iota_sb, iota), writes=[b_iota])
        S.dma("sp", DMA(ln2_sb, ln2), writes=[b_ln2])
        S.op("dve", MSET(zrow, 0.0), writes=[b_z])
        for c in range(4):
            S.dma("sp", DMA(x1g[2048 + c * 128:2048 + (c + 1) * 128, :], zrow), reads=[b_z], writes=[b_x1g])
        S.dma("sp", DMA(slot_sb, slotn), writes=[b_iota])
        S.op("dve", MSET(yo[0], 0.0), writes=[b_yo[0]])
        for j in range(16):
            S.dma("sp", DMA(ffn[j * 128:(j + 1) * 128, :], yo[0]), reads=[b_yo[0]], writes=[b_ffn])
        S.coll("pool", lambda e: e.collective_compute("AllGather", ALU.bypass,
                                                      replica_groups=[[0, 1], [2, 3], [4, 5], [6, 7]],
                                                      ins=[affbuf.opt()], outs=[affall.opt()]),
               reads=[b_affbuf], writes=[b_affall])
        S.dma("sp", DMA(affall_sb.rearrange("p t e -> p (t e)"), affall.rearrange("(p t) e -> p (t e)", t=32)),
              reads=[b_affall], writes=[b_affall])
        S.op("dve", MSET(lo, 0.0), writes=[b_lo])
        step = 0.5
        for it in range(30):
            S.op("dve", TS(tr_, lo, step, None, ALU.add), reads=[b_lo], writes=[b_tr])
            S.op("dve", TT(cmp_sb.rearrange("p (t e) -> p t e", e=16), affall_sb,
                           tr_.unsqueeze(1).to_broadcast([128, 32, 16]), ALU.is_ge), reads=[b_affall, b_tr],
                 writes=[b_cmp])
            cp_, cpb = ps_single()
            S.op("pe", MM(cp_, ones_bf, cmp_sb, True, True), reads=[b_cmp, b_const], writes=cpb)
            S.op("dve", lambda e, o=cntt, i=cp_.rearrange("p (t e) -> p e t", e=16): e.reduce_sum(out=o, in_=i, axis=AX.X),
                 reads=cpb, writes=[b_cnt])
            S.op("dve", TS(ge, cntt, float(CAP) - 0.5, None, ALU.is_ge), reads=[b_cnt], writes=[b_ge])
            S.op("dve", STT(lo, ge, step, lo, ALU.mult, ALU.add), reads=[b_ge, b_lo], writes=[b_lo])
            step *= 0.5
        if debug:
            S.dma("sp", DMA(thr_dbg, lo), reads=[b_lo], writes=[b_dbg])
            S.dma("sp", DMA(aff_dbg, aff_own.rearrange("p t e -> p (t e)")), reads=[b_aff], writes=[b_dbg])
        S.op("dve", TT(M_sb, aff_own, lo.unsqueeze(1).to_broadcast([128, 16, 16]), ALU.is_ge), reads=[b_aff, b_lo],
             writes=[b_M])
        p1, p1b = ps_single()
        S.op("pe", MM(p1[:, 0:256], tri_bf, M_sb.rearrange("p t e -> p (t e)"), True, True), reads=[b_M, b_const],
             writes=p1b)
        p2, p2b = ps_single()
        S.op("pe", MM(p2[:, 0:256], ones_bf, M_sb.rearrange("p t e -> p (t e)"), True, True), reads=[b_M, b_const],
             writes=p2b)
        S.op("dve", MSET(ex[:, 0, :], 0.0), writes=[b_ex])
        for j in range(1, 16):
            S.op("dve", TT(ex[:, j, :], ex[:, j - 1, :], p2[:, (j - 1) * 16:j * 16], ALU.add), reads=p2b + [b_ex],
                 writes=[b_ex])
        S.op("dve", TT(c_incl.rearrange("p t e -> p (t e)"), p1[:, 0:256], ex.rearrange("p t e -> p (t e)"), ALU.add),
             reads=p1b + [b_ex], writes=[b_cinc])

        def make_idx(e):
            ii = e % 3
            ip, ipb = ps_acc()
            for j in range(16):
                ci = j % 4
                eng = "dve"
                S.op(eng, TS(cmpS[ci], iota_sb, c_incl[:, j, e:e + 1], None, ALU.is_ge), reads=[b_iota, b_cinc],
                     writes=[b_cmpS[ci]])
                for c in range(4):
                    S.op("pe", MM(ip[:, c:c + 1], cmpS[ci][:, c * 128:(c + 1) * 128], ones_bf[:, 0:1],
                                  j == 0 and c == 0, j == 15), reads=[b_cmpS[ci], b_const], writes=ipb)
            S.op("dve", TS(idx_f[ii][:, 4:8], ip[:, 0:4], 2047.5, None, ALU.is_ge), reads=ipb, writes=[b_idx[ii]])
            S.op("dve", TT(idx_f[ii][:, 4:8], idx_f[ii][:, 4:8], slot_sb, ALU.mult), reads=[b_idx[ii], b_iota],
                 writes=[b_idx[ii]])
            S.op("dve", TT(idx_f[ii][:, 0:4], idx_f[ii][:, 4:8], ip[:, 0:4], ALU.add), reads=ipb + [b_idx[ii]],
                 writes=[b_idx[ii]])
            S.op("dve", CP(idx_i[ii], idx_f[ii][:, 0:4]), reads=[b_idx[ii]], writes=[b_idx[ii]])
            if debug:
                S.dma("sp", DMA(idx_dbg[e], idx_i[ii]), reads=[b_idx[ii]], writes=[b_dbg])

        def gather(e):
            wi_ = e % 2
            for c in range(4):
                S.dma("pool", lambda eng, o=xs[wi_][c], ix=idx_i[e % 3][:, c:c + 1]: eng.indirect_dma_start(
                    out=o[:, :], out_offset=None, in_=x1g[:, :],
                    in_offset=bass.IndirectOffsetOnAxis(ap=ix, axis=0)),
                    reads=[b_idx[e % 3], b_x1g], writes=[b_xs[wi_][c]], sem="g")

        make_idx(0)
        gather(0)
        yo_n = 0
        b_sc = [[Buf() for _ in range(4)] for _ in range(16)]
        for e in range(16):
            wi = e % 2
            if e + 1 < 16:
                load_w(e + 1)
                make_idx(e + 1)
                gather(e + 1)
            for c in range(4):
                tp, tpb = ps_single()
                tpv = tp.bitcast(BF16)
                for k in range(8):
                    S.op("pe", TR(tpv[:, k * 128:(k + 1) * 128], xs[wi][c][:, k * 128:(k + 1) * 128], ident),
                         reads=[b_xs[wi][c], b_const], writes=tpb)
                eng = "dve" if c % 2 == 0 else "act"
                if eng == "dve":
                    S.op("dve", CP(xsT[wi][:, :, c * 128:(c + 1) * 128], tpv.rearrange("p (k t) -> p k t", k=8)),
                         reads=tpb, writes=[b_xsT[wi]])
                else:
                    S.op("act", ACT(xsT[wi][:, :, c * 128:(c + 1) * 128], tpv.rearrange("p (k t) -> p k t", k=8),
                                    AF.Copy), reads=tpb, writes=[b_xsT[wi]])
            for fc in range(8):
                gp, gpb = ps_single()
                for k in range(8):
                    S.op("pe", MM(gp, Wgu_sb[wi][:, k, fc * 128:(fc + 1) * 128], xsT[wi][:, k, :], k == 0, k == 7),
                         reads=[b_Wgu[wi][fc // 4], b_xsT[wi]], writes=gpb)
                up, upb = ps_single()
                for k in range(8):
                    S.op("pe", MM(up, Wgu_sb[wi][:, k, 1024 + fc * 128:1024 + (fc + 1) * 128], xsT[wi][:, k, :],
                                  k == 0, k == 7), reads=[b_Wgu[wi][2 + fc // 4], b_xsT[wi]], writes=upb)
                si = fc % 2
                S.op("act", ACT(sg[si], gp, AF.Silu), reads=gpb, writes=[b_sg[si]])
                S.op("dve", TT(hT[wi][:, fc, :], up, sg[si], ALU.mult), reads=upb + [b_sg[si]], writes=[b_hT[wi]])
            for c in range(4):
                yi = yo_n % 3
                yo_n += 1
                gcol = xs[wi][c][:, 1024 + 2 * e:1024 + 2 * e + 2].bitcast(F32)
                for hf in range(2):
                    dp, dpb = ps_single()
                    for fc in range(8):
                        S.op("pe", MM(dp, hT[wi][:, fc, c * 128:(c + 1) * 128], Wd_sb[wi][:, fc, hf * 512:(hf + 1) * 512],
                                      fc == 0, fc == 7), reads=[b_hT[wi], b_Wd[wi][hf]], writes=dpb)
                    if hf == 0:
                        S.op("act", ACT(yo[yi][:, 0:512], dp, AF.Copy, scale=gcol), reads=dpb + [b_xs[wi][c]],
                             writes=[b_yo[yi]])
                    else:
                        S.op("dve", TS(yo[yi][:, 512:1024], dp, gcol, None, ALU.mult), reads=dpb + [b_xs[wi][c]],
                             writes=[b_yo[yi]])
                S.dma("pool", lambda eng, i_=yo[yi], ix=idx_i[e % 3][:, c:c + 1]: eng.indirect_dma_start(
                    out=ffn[:, :], out_offset=bass.IndirectOffsetOnAxis(ap=ix, axis=0), in_=i_[:, :], in_offset=None,
                    compute_op=ALU.add),
                    reads=[b_idx[e % 3], b_yo[yi], b_ffn] + (b_sc[e - 1] if e else []), writes=[b_sc[e][c]], sem="g")
        S.barrier()
        A.off = base_off0
        fbuf = [A.get([128, D], F32) for _ in range(4)]
        xbuf = [A.get([128, D], F32) for _ in range(4)]
        fst = [A.get([128, 8], F32) for _ in range(4)]
        fjunk = A.get([128, D], BF16)
        ln2b = A.get([128, 2 * D], F32)
        b_fb, b_xb, b_fst = [Buf() for _ in range(4)], [Buf() for _ in range(4)], [Buf() for _ in range(4)]
        b_fj, b_l2 = Buf(), Buf()
        S.dma("sp", DMA(ln2b, ln2), writes=[b_l2])
        def ln2_load(j):
            i = j % 4
            S.dma("sp", DMA(fbuf[i], ffn[j * 128:(j + 1) * 128, :]), reads=[b_ffn] + b_sc[15], writes=[b_fb[i]])
            S.dma("sp", DMA(xbuf[i], x1_f[j * 128:(j + 1) * 128, :]), reads=[b_x1f], writes=[b_xb[i]])

        for j in range(3):
            ln2_load(j)
        for j in range(16):
            i = j % 4
            if j + 3 < 16:
                ln2_load(j + 3)
            t = fbuf[i]
            st_ = fst[i]
            S.op("dve", STT(t, xbuf[i], ALPHA, t, ALU.mult, ALU.add), reads=[b_xb[i], b_fb[i]], writes=[b_fb[i]])
            S.op("act", ACT(fjunk, t, AF.Copy, accum_out=st_[:, 0:1]), reads=[b_fb[i]], writes=[b_fj, b_fst[i]])
            S.op("act", ACT(fjunk, t, AF.Square, accum_out=st_[:, 1:2]), reads=[b_fb[i]], writes=[b_fj, b_fst[i]])
            S.op("dve", TS(st_[:, 2:3], st_[:, 0:1], 1.0 / D, None, ALU.mult), reads=[b_fst[i]], writes=[b_fst[i]])
            S.op("dve", TT(st_[:, 3:4], st_[:, 2:3], st_[:, 2:3], ALU.mult), reads=[b_fst[i]], writes=[b_fst[i]])
            S.op("dve", STT(st_[:, 4:5], st_[:, 1:2], 1.0 / D, st_[:, 3:4], ALU.mult, ALU.subtract),
                 reads=[b_fst[i]], writes=[b_fst[i]])
            S.op("act", ACT(st_[:, 5:6], st_[:, 4:5], AF.Sqrt, bias=eps_t[:, 1:2]), reads=[b_fst[i], b_const],
                 writes=[b_fst[i]])
            S.op("dve", RECIP(st_[:, 5:6], st_[:, 5:6]), reads=[b_fst[i]], writes=[b_fst[i]])
            S.op("dve", TS(t, t, st_[:, 2:3], st_[:, 5:6], ALU.subtract, ALU.mult), reads=[b_fb[i], b_fst[i]],
                 writes=[b_fb[i]])
            S.op("dve", TT(t, t, ln2b[:, 0:D], ALU.mult), reads=[b_fb[i], b_l2], writes=[b_fb[i]])
            S.op("dve", TT(t, t, ln2b[:, D:2 * D], ALU.add), reads=[b_fb[i], b_l2], writes=[b_fb[i]])
            S.dma("sp", DMA(out_d[j * 128:(j + 1) * 128, :], t), reads=[b_fb[i]], writes=[b_out])
        S.barrier()
        S.emit(st)
    return nc


def _rope_tables():
    t = np.arange(SEQ)
    row = (t // 64).astype(np.float32)
    col = (t % 64).astype(np.float32)
    inv = (np.float32(10000.0) ** (-np.arange(0, 32, 2, dtype=np.float32) / np.float32(32))).astype(np.float32)
    ang = np.concatenate([row[:, None] * inv, col[:, None] * inv], axis=-1).astype(np.float32)
    return np.cos(ang).astype(np.float32), np.sin(ang).astype(np.float32)


def _bm_tile(rpb, jg, ktg):
    if ktg < 0 or ktg > 31:
        return np.full((8, 128, 128), NEG, np.float32)
    i = np.arange(128)
    kr, kc = (i // 64)[:, None], (i % 64)[:, None]
    qr, qc = (i // 64)[None, :], (i % 64)[None, :]
    key_row = 2 * ktg + kr
    r = 2 * jg + qr
    rs = np.clip(r - 4, 0, 56)
    cs = np.clip(qc - 8, 0, 48)
    valid = (key_row >= rs) & (key_row < rs + 8) & (kc >= cs) & (kc < cs + 16)
    dr = np.clip(key_row - r + 7, 0, 14)
    dc = np.clip(kc - qc + 15, 0, 30)
    dr, dc = np.broadcast_arrays(dr, dc)
    vals = rpb[:, dr, dc]
    return np.where(valid[None], vals, np.float32(NEG)).astype(np.float32)


def _core_inputs(b, h, inp, shared):
    x = inp["x"][b]
    own = [16 * h + i for i in range(16)]
    order = own + [16 * h - 2, 16 * h - 1, 16 * h + 16, 16 * h + 17] + [16 * (1 - h) + i for i in range(16)]
    xTc = np.zeros((D, NTOK), np.float32)
    cosT = np.zeros((128, NTOK), np.float32)
    sinT = np.zeros((128, NTOK), np.float32)
    cos, sin = shared["rope"]
    pidx = (np.arange(128) % 64) // 2
    for s, t in enumerate(order):
        if 0 <= t < 32:
            xTc[:, s * 128:(s + 1) * 128] = x[t * 128:(t + 1) * 128, :].T
            cosT[:, s * 128:(s + 1) * 128] = cos[t * 128:(t + 1) * 128, :][:, pidx].T
            sinT[:, s * 128:(s + 1) * 128] = sin[t * 128:(t + 1) * 128, :][:, pidx].T
    rpb = inp["na_rpb"][0]
    bmc = np.full((5, 128, 8, 6, 128), NEG, np.float32)
    for cls, jl in enumerate([2, 0, 1, 14, 15]):
        jg = 16 * h + jl
        for ki, w in enumerate(na_wlist(jl)):
            ktg = 16 * h - 2 + w
            bmc[cls, :, :, ki, :] = _bm_tile(rpb, jg, ktg).transpose(1, 0, 2)
    d = dict(shared["common"])
    d["xT"] = np.ascontiguousarray(xTc.reshape(8, 128, NTOK).transpose(1, 0, 2))
    d["x_own"] = np.ascontiguousarray(x[h * 2048:(h + 1) * 2048])
    d["memT"] = np.ascontiguousarray(inp["mem"][b].T.reshape(8, 128, 256).transpose(1, 0, 2))
    d["bm"] = np.ascontiguousarray(bmc.reshape(5, 128, 8 * 6 * 128))
    d["cosT"] = cosT
    d["sinT"] = sinT
    return d


def _pk(w):
    return np.ascontiguousarray(w.reshape(8, 128, -1).transpose(1, 0, 2))


def _shared(inp):
    w_in = inp["w_in"][0]
    q_na, k_na, v_na = w_in[:, 0:512], w_in[:, 512:1024], w_in[:, 1024:1536]
    q_g, k_g, v_g = w_in[:, 1536:2048], w_in[:, 2048:2176], w_in[:, 2176:2304]
    q_m, gl = w_in[:, 2304:2816], w_in[:, 2816:5888]
    wA = _pk(np.concatenate([k_na, v_na, k_g, v_g], axis=1))
    chunks = [q_na[:, c * 128:(c + 1) * 128] for c in range(4)]
    for i in range(4):
        chunks.append(np.concatenate([q_g[:, i * 64:(i + 1) * 64], q_g[:, (4 + i) * 64:(5 + i) * 64]], axis=1))
    chunks += [q_m[:, c * 128:(c + 1) * 128] for c in range(4)]
    wq = np.stack([_pk(c) for c in chunks])
    wg = np.stack([_pk(gl[:, c * 128:(c + 1) * 128]) for c in range(24)])
    wb = inp["w_branch"][0]
    wbr = np.zeros((8, 128, 12, 128), np.float32)
    for dc in range(8):
        for g in range(3):
            for c in range(4):
                wbr[dc, :, g * 4 + c, :] = wb[g, c * 128:(c + 1) * 128, dc * 128:(dc + 1) * 128]
    w_out = inp["w_out"][0]
    wo = np.stack([_pk(w_out[:, hf * 512:(hf + 1) * 512]) for hf in range(2)])
    wm = _pk(inp["w_mem_kv"][0])
    ident = np.eye(128, dtype=np.float32)
    bones = np.kron(np.eye(2, dtype=np.float32), np.ones((64, 64), np.float32))
    rot = np.zeros((128, 128), np.float32)
    for i in range(64):
        rot[2 * i + 1, 2 * i] = -1.0
        rot[2 * i, 2 * i + 1] = 1.0
    gains = np.stack([np.tile(inp["gqa_q_gain"][0], 2), np.tile(inp["gqa_k_gain"][0], 2)], axis=1)
    bgate = np.ascontiguousarray(inp["b_gate"][0].reshape(24, 128).T)
    ln1 = np.concatenate([np.broadcast_to(inp["ln1_g"][0], (128, D)), np.broadcast_to(inp["ln1_b"][0], (128, D))],
                         axis=1)
    ones = np.ones((128, 128), np.float32)
    tri = np.triu(np.ones((128, 128), np.float32))
    wgu = np.ascontiguousarray(inp["w_gate_up"][0].reshape(16, 8, 128, 2048).transpose(0, 2, 1, 3))
    wd = np.ascontiguousarray(inp["w_down"][0].reshape(16, 8, 128, 1024).transpose(0, 2, 1, 3))
    ln2 = np.concatenate([np.broadcast_to(inp["ln2_g"][0], (128, D)), np.broadcast_to(inp["ln2_b"][0], (128, D))],
                         axis=1)
    common = {
        "identf": ident, "iota": np.ascontiguousarray(np.broadcast_to(np.arange(512, dtype=np.float32), (128, 512))),
        "wr": _pk(inp["w_router"][0]),
        "slotn": np.ascontiguousarray((np.arange(128, dtype=np.float32)[:, None]
                                       + 128.0 * np.arange(4, dtype=np.float32)[None, :])), "ln2": np.ascontiguousarray(ln2.astype(np.float32)),
        "wgu": wgu, "wd": wd,
        "wA": wA, "wq": wq, "wg": wg, "wbr": wbr, "wo": wo, "wm": wm,
        "cmat": np.ascontiguousarray(np.concatenate([ident, bones, rot, ones, tri], axis=1)),
        "gains": np.ascontiguousarray(gains.astype(np.float32)),
        "bgate": bgate.astype(np.float32),
        "ln1": np.ascontiguousarray(ln1.astype(np.float32)),
    }
    return {"common": common, "rope": _rope_tables()}


def run(inp, debug=False):
    inp = {k: np.asarray(v) for k, v in inp.items()}
    shared = _shared(inp)
    in_maps = [_core_inputs(c // 2, c % 2, inp, shared) for c in range(8)]
    nc = build_program(debug=debug)
    res = run_bass_kernel_spmd(nc, in_maps, core_ids=list(range(8)))
    return res


def kernel(**inputs):
    res = run(inputs)
    out = np.zeros((4, SEQ, D), np.float32)
    for c in range(8):
        b, h = c // 2, c % 2
        out[b, h * 2048:(h + 1) * 2048] = res.results[c]["out"]
    return out
```
